# Optimizing a Trainium2 kernel written in Bass

```python
import math
import jax
import jax.numpy as jnp
from jax import lax
import numpy as np

D_MODEL = 2048
BATCH = 4
SEQ = 4096
DEPTH = 4

CHUNK = 64
N_META = 16
ROPE_THETA = 500000.0
NORM_EPS = 1e-5
N_A_LAYERS = DEPTH // 2
N_B_LAYERS = DEPTH - N_A_LAYERS

A_HEADS = 8
A_HEAD_DIM = D_MODEL // (2 * A_HEADS)
A_ROT = A_HEAD_DIM // 4
A_QBLOCK = 128

B_HEAD_DIM = 64
B_Q_HEADS = D_MODEL // B_HEAD_DIM
B_KV_HEADS = 4
B_GROUP = B_Q_HEADS // B_KV_HEADS
B_ROT = B_HEAD_DIM // 4
WINDOW = 128
WINDOW_CHUNKS = WINDOW // CHUNK

kernel_name = "hybrid_diffattn_yoco_swa_sinks"


def rms_norm(x, g):
    xf = x.astype(jnp.float32)
    y = xf * lax.rsqrt(jnp.mean(xf * xf, axis=-1, keepdims=True) + NORM_EPS)
    return (y * g.astype(jnp.float32)).astype(x.dtype)


def chunk_ids(pos):
    return jnp.where(pos < N_META, 0, (pos - N_META) // CHUNK + 1)


def rope_tables(pos, rot):
    inv = ROPE_THETA ** (-jnp.arange(0, rot, 2, dtype=jnp.float32) / rot)
    ang = pos.astype(jnp.float32)[:, None] * inv[None, :]
    return jnp.cos(ang), jnp.sin(ang)


def partial_rope(x, cos, sin):
    half = cos.shape[-1]
    c = cos[None, :, None, :].astype(x.dtype)
    s = sin[None, :, None, :].astype(x.dtype)
    x1 = x[..., :half]
    x2 = x[..., half:2 * half]
    return jnp.concatenate([x1 * c - x2 * s, x2 * c + x1 * s, x[..., 2 * half:]], axis=-1)


def diff_attention(q, k, v, lam, cid):
    L = q.shape[1]
    scale = A_HEAD_DIM ** -0.5
    outs = []
    for s in range(0, L, A_QBLOCK):
        e = min(L, s + A_QBLOCK)
        kend = min(L, e + CHUNK)
        sc = jnp.einsum('bqhd,bkhd->bhqk', q[:, s:e], k[:, :kend]).astype(jnp.float32) * scale
        mask = cid[None, :kend] <= cid[s:e, None]
        sc = jnp.where(mask[None, None], sc, -jnp.inf)
        p = jax.nn.softmax(sc, axis=-1)
        p = p.reshape(p.shape[0], A_HEADS, 2, e - s, kend)
        w = p[:, :, 0] - lam * p[:, :, 1]
        outs.append(jnp.einsum('bhqk,bkhd->bqhd', w.astype(v.dtype), v[:, :kend]))
    return jnp.concatenate(outs, axis=1)


def diff_attn_layer(x, norm_g, w_in, w_out, lq1, lk1, lq2, lk2, subln_g, lambda_init, cos, sin, cid):
    B, L, _ = x.shape
    h = rms_norm(x, norm_g)
    q, k, v, gate = jnp.split(h @ w_in, 4, axis=-1)
    q = partial_rope(q.reshape(B, L, 2 * A_HEADS, A_HEAD_DIM), cos, sin)
    k = partial_rope(k.reshape(B, L, 2 * A_HEADS, A_HEAD_DIM), cos, sin)
    v = v.reshape(B, L, A_HEADS, 2 * A_HEAD_DIM)
    f32 = jnp.float32
    lam = (jnp.exp(jnp.sum(lq1.astype(f32) * lk1.astype(f32)))
           - jnp.exp(jnp.sum(lq2.astype(f32) * lk2.astype(f32))) + lambda_init)
    o = diff_attention(q, k, v, lam, cid)
    o = rms_norm(o, subln_g) * (1.0 - lambda_init)
    o = o.reshape(B, L, D_MODEL) * jax.nn.silu(gate)
    return x + o @ w_out


def shared_kv(x, kv_norm, w_kv, cos, sin):
    B, L, _ = x.shape
    k, v = jnp.split(rms_norm(x, kv_norm) @ w_kv, 2, axis=-1)
    k = partial_rope(k.reshape(B, L, B_KV_HEADS, B_HEAD_DIM), cos, sin)
    v = v.reshape(B, L, B_KV_HEADS, B_HEAD_DIM)
    return k, v


def sink_softmax(sc, sink):
    m = jnp.maximum(jnp.max(sc, axis=-1, keepdims=True), sink)
    e = jnp.exp(sc - m)
    return e / (jnp.sum(e, axis=-1, keepdims=True) + jnp.exp(sink - m))


def swa_sink_attention(q, k, v, sinks):
    B, L = q.shape[:2]
    S = L - N_META
    NC = S // CHUNK
    scale = B_HEAD_DIM ** -0.5
    sink = sinks.astype(jnp.float32).reshape(B_KV_HEADS, B_GROUP)
    qm = q[:, :N_META].reshape(B, N_META, B_KV_HEADS, B_GROUP, B_HEAD_DIM)
    km, vm = k[:, :N_META], v[:, :N_META]
    sm = jnp.einsum('bqkgd,bskd->bkgqs', qm, km).astype(jnp.float32) * scale
    pm = sink_softmax(sm, sink[None, :, :, None, None])
    om = jnp.einsum('bkgqs,bskd->bqkgd', pm.astype(v.dtype), vm).reshape(B, N_META, D_MODEL)
    qr = q[:, N_META:].reshape(B, NC, CHUNK, B_KV_HEADS, B_GROUP, B_HEAD_DIM)
    kr = k[:, N_META:].reshape(B, NC, CHUNK, B_KV_HEADS, B_HEAD_DIM)
    vr = v[:, N_META:].reshape(B, NC, CHUNK, B_KV_HEADS, B_HEAD_DIM)
    pad = ((0, 0), (WINDOW_CHUNKS, 0), (0, 0), (0, 0), (0, 0))
    kp, vp = jnp.pad(kr, pad), jnp.pad(vr, pad)
    kb = jnp.concatenate([kp[:, j:j + NC] for j in range(WINDOW_CHUNKS + 1)], axis=2)
    vb = jnp.concatenate([vp[:, j:j + NC] for j in range(WINDOW_CHUNKS + 1)], axis=2)
    meta_shape = (B, NC, N_META, B_KV_HEADS, B_HEAD_DIM)
    kb = jnp.concatenate([jnp.broadcast_to(km[:, None], meta_shape), kb], axis=2)
    vb = jnp.concatenate([jnp.broadcast_to(vm[:, None], meta_shape), vb], axis=2)
    band_chunk = jnp.arange(NC)[:, None] - WINDOW_CHUNKS + jnp.arange(WINDOW_CHUNKS + 1)[None, :]
    valid = jnp.concatenate([jnp.ones((NC, N_META), dtype=bool),
                             jnp.repeat(band_chunk >= 0, CHUNK, axis=1)], axis=1)
    sr = jnp.einsum('bnqkgd,bnskd->bnkgqs', qr, kb).astype(jnp.float32) * scale
    sr = jnp.where(valid[None, :, None, None, None, :], sr, -jnp.inf)
    pr = sink_softmax(sr, sink[None, None, :, :, None, None])
    orr = jnp.einsum('bnkgqs,bnskd->bnqkgd', pr.astype(v.dtype), vb).reshape(B, S, D_MODEL)
    return jnp.concatenate([om, orr], axis=1)


def swa_layer(x, norm_g, w_in, w_out, sinks, k_sh, v_sh, cos, sin):
    B, L, _ = x.shape
    q, gate = jnp.split(rms_norm(x, norm_g) @ w_in, 2, axis=-1)
    q = partial_rope(q.reshape(B, L, B_Q_HEADS, B_HEAD_DIM), cos, sin)
    o = swa_sink_attention(q, k_sh, v_sh, sinks)
    return x + (o * jax.nn.silu(gate)) @ w_out


def setup_inputs(seed: int = 0) -> dict:
    key = jax.random.key(seed)
    ks = jax.random.split(key, 17)
    D = D_MODEL

    def nrm(k, shape, s):
        return jax.random.normal(k, shape, jnp.float32) * s

    return {
        'x': nrm(ks[0], (BATCH, SEQ, D), 1.0),
        'meta_tokens': nrm(ks[1], (N_META, D), 1.0),
        'a_norm': 1.0 + nrm(ks[2], (N_A_LAYERS, D), 0.02),
        'a_w_in': nrm(ks[3], (N_A_LAYERS, D, 4 * D), D ** -0.5),
        'a_w_out': nrm(ks[4], (N_A_LAYERS, D, D), D ** -0.5),
        'a_lambda_q1': nrm(ks[5], (N_A_LAYERS, A_HEAD_DIM), 0.1),
        'a_lambda_k1': nrm(ks[6], (N_A_LAYERS, A_HEAD_DIM), 0.1),
        'a_lambda_q2': nrm(ks[7], (N_A_LAYERS, A_HEAD_DIM), 0.1),
        'a_lambda_k2': nrm(ks[8], (N_A_LAYERS, A_HEAD_DIM), 0.1),
        'a_subln': 1.0 + nrm(ks[9], (N_A_LAYERS, 2 * A_HEAD_DIM), 0.02),
        'kv_norm': 1.0 + nrm(ks[10], (D,), 0.02),
        'w_kv': nrm(ks[11], (D, 2 * B_KV_HEADS * B_HEAD_DIM), D ** -0.5),
        'b_norm': 1.0 + nrm(ks[12], (N_B_LAYERS, D), 0.02),
        'b_w_in': nrm(ks[13], (N_B_LAYERS, D, 2 * D), D ** -0.5),
        'b_w_out': nrm(ks[14], (N_B_LAYERS, D, D), D ** -0.5),
        'b_sinks': nrm(ks[15], (N_B_LAYERS, B_Q_HEADS), 0.5),
        'final_norm': 1.0 + nrm(ks[16], (D,), 0.02),
    }


def reference(x, meta_tokens, a_norm, a_w_in, a_w_out, a_lambda_q1, a_lambda_k1, a_lambda_q2,
              a_lambda_k2, a_subln, kv_norm, w_kv, b_norm, b_w_in, b_w_out, b_sinks, final_norm):
    B = x.shape[0]
    meta = jnp.broadcast_to(meta_tokens.astype(x.dtype)[None], (B, N_META, D_MODEL))
    h = jnp.concatenate([meta, x], axis=1)
    L = h.shape[1]
    pos = jnp.arange(L, dtype=jnp.int32)
    cid = chunk_ids(pos)
    cos_a, sin_a = rope_tables(pos, A_ROT)
    cos_b, sin_b = rope_tables(pos, B_ROT)
    k_sh = v_sh = None
    for i in range(DEPTH):
        if i < N_A_LAYERS:
            lambda_init = 0.8 - 0.6 * math.exp(-0.3 * i)
            h = diff_attn_layer(h, a_norm[i], a_w_in[i], a_w_out[i], a_lambda_q1[i], a_lambda_k1[i],
                                a_lambda_q2[i], a_lambda_k2[i], a_subln[i], lambda_init,
                                cos_a, sin_a, cid)
        else:
            if i == N_A_LAYERS:
                k_sh, v_sh = shared_kv(h, kv_norm, w_kv, cos_b, sin_b)
            j = i - N_A_LAYERS
            h = swa_layer(h, b_norm[j], b_w_in[j], b_w_out[j], b_sinks[j], k_sh, v_sh, cos_b, sin_b)
    return rms_norm(h, final_norm)[:, N_META:]
```

```python
import os
import numpy as np
import ml_dtypes
from contextlib import ExitStack
import concourse.bass as bass
import concourse.mybir as mybir
from concourse.bass_utils import run_bass_kernel_spmd

F32 = mybir.dt.float32
BF16 = mybir.dt.bfloat16
AF = mybir.ActivationFunctionType
ALU = mybir.AluOpType
AX = mybir.AxisListType

DBG = int(os.environ.get('KDBG', '0'))
D = 2048
KC = 16
EPS = 1e-5
N_META = 16


class Rec:
    __slots__ = ("q", "idx", "needed", "val", "sem", "dma")

    def __init__(self, q, idx, dma=False):
        self.q = q
        self.idx = idx
        self.needed = False
        self.val = None
        self.sem = None
        self.dma = dma


class Dep:
    __slots__ = ("w", "rc", "rd")

    def __init__(self):
        self.w = None
        self.rc = {}
        self.rd = []


def deps(n):
    return [Dep() for _ in range(n)]


QUEUES = ("pe", "act", "dve", "pool", "sp")
ENG = {"pe": "tensor", "act": "scalar", "dve": "vector", "pool": "gpsimd", "sp": "sync"}
N_DMA_SEMS = {"sp": 40, "pool": 8, "act": 8}


class Prog:
    def __init__(self, nc):
        self.nc = nc
        self.stack = ExitStack()
        self.scopes = [self.stack]
        self.streams = {q: [] for q in QUEUES}
        self.count = {q: 0 for q in QUEUES}
        self.last = {q: None for q in QUEUES}
        self.seen = {q: {} for q in QUEUES}
        self.seen_dma = {q: set() for q in QUEUES}
        self.sems = {q: self.stack.enter_context(nc.semaphore(f"s_{q}")) for q in QUEUES}
        self.dma_sems = {}
        self.dma_slot = {}
        self.dma_rr = {}
        for q, n in N_DMA_SEMS.items():
            self.dma_sems[q] = [self.stack.enter_context(nc.semaphore(f"d_{q}{i}")) for i in range(n)]
            self.dma_slot[q] = [None] * n
            self.dma_rr[q] = 0
        self.customs = []
        self.cc_sem = None
        self.cc_dep = None
        self.n_alloc = 0

    def scope(self):
        prog = self

        class _S:
            def __enter__(s):
                s.st = ExitStack()
                prog.scopes.append(s.st)

            def __exit__(s, *a):
                prog.barrier()
                prog.scopes.pop()
                s.st.close()
                return False

        return _S()

    def sbuf(self, shape, dtype, name=None):
        self.n_alloc += 1
        return self.scopes[-1].enter_context(self.nc.sbuf_tensor(f"sb{self.n_alloc}_{name or ''}", list(shape), dtype))

    def psum(self, shape, dtype, name=None):
        self.n_alloc += 1
        return self.scopes[-1].enter_context(self.nc.psum_tensor(f"ps{self.n_alloc}_{name or ''}", list(shape), dtype))

    def _need(self, q, rec):
        if rec is None:
            return
        if rec.dma:
            if id(rec) in self.seen_dma[q]:
                return
            self.seen_dma[q].add(id(rec))
            self.streams[q].append(("wait", rec))
            return
        if rec.q == q and q == "pe":
            return
        if self.seen[q].get(rec.q, -1) >= rec.idx:
            return
        self.seen[q][rec.q] = rec.idx
        rec.needed = True
        self.streams[q].append(("wait", rec))

    def _deps(self, q, reads, writes):
        best = {}
        dmas = []

        def add(rec):
            if rec is None:
                return
            if rec.dma:
                dmas.append(rec)
            else:
                b = best.get(rec.q)
                if b is None or b.idx < rec.idx:
                    best[rec.q] = rec

        for d in reads:
            add(d.w)
        for d in writes:
            add(d.w)
            for r in d.rc.values():
                add(r)
            for r in d.rd:
                add(r)
        for r in dmas:
            self._need(q, r)
        for r in best.values():
            self._need(q, r)

    def _commit(self, rec, reads, writes):
        for d in reads:
            if rec.dma:
                d.rd.append(rec)
            else:
                d.rc[rec.q] = rec
        for d in writes:
            d.w = rec
            d.rc = {}
            d.rd = []

    def op(self, q, fn, reads=(), writes=()):
        self._deps(q, reads, writes)
        rec = Rec(q, self.count[q])
        self.count[q] += 1
        self.last[q] = rec
        self.streams[q].append(("op", fn, rec))
        self._commit(rec, reads, writes)
        return rec

    def dma(self, q, out, in_, reads=(), writes=(), **kw):
        self._deps(q, reads, writes)
        k = self.dma_rr[q]
        self.dma_rr[q] = (k + 1) % len(self.dma_sems[q])
        prev = self.dma_slot[q][k]
        if prev is not None:
            self._need(q, prev)
        rec = Rec(q, -1, dma=True)
        rec.sem = self.dma_sems[q][k]
        rec.val = (prev.val if prev is not None else 0) + 16
        self.dma_slot[q][k] = rec
        self.streams[q].append(("dma", (out, in_, kw), rec))
        self._commit(rec, reads, writes)
        return rec

    def custom(self, q, fn, inc, reads=(), writes=()):
        self._deps(q, reads, writes)
        if self.customs:
            self._need(q, self.customs[-1])
        rec = Rec(q, -1, dma=True)
        if self.cc_sem is None:
            self.cc_sem = self.stack.enter_context(self.nc.semaphore("cc_sem"))
            self.cc_dep = Dep()
        rec.sem = self.cc_sem
        rec.val = (self.customs[-1].val if self.customs else 0) + inc
        rec.idx = inc
        self.customs.append(rec)
        self.streams[q].append(("custom", fn, rec))
        self._commit(rec, reads, writes)
        return rec

    def wait(self, q, rec):
        self._need(q, rec)

    def barrier(self):
        recs = [self.last[q] for q in QUEUES if self.last[q] is not None]
        for q in self.dma_slot:
            recs += [r for r in self.dma_slot[q] if r is not None]
        recs += self.customs[-1:]
        for q in QUEUES:
            for r in recs:
                self._need(q, r)

    def finish(self):
        nc = self.nc
        for q in QUEUES:
            c = 0
            for ent in self.streams[q]:
                if ent[0] == "op" and ent[2].needed:
                    c += 1
                    ent[2].val = c
                    ent[2].sem = self.sems[q]

        def run(q, e):
            for ent in self.streams[q]:
                kind = ent[0]
                if kind == "wait":
                    e.wait_ge(ent[1].sem, ent[1].val)
                elif kind == "op":
                    ins = ent[1](e)
                    if ent[2].needed:
                        ins.then_inc(ent[2].sem, 1)
                elif kind == "dma":
                    out, in_, kw = ent[1]
                    e.dma_start(out=out, in_=in_, **kw).then_inc(ent[2].sem, 16)
                elif kind == "custom":
                    ent[1](e).then_inc(ent[2].sem, ent[2].idx)

        with nc.Block() as block:
            for q in QUEUES:
                if self.streams[q]:
                    getattr(block, ENG[q])(lambda e, q=q: run(q, e))
        self.stack.close()


A_HEADS = 8
A_HD = 128
B_HD = 64
B_QH = 32
B_KVH = 4
ROPE_THETA = 500000.0
PAIRS = [[0, 1], [2, 3], [4, 5], [6, 7]]


def slot_cols(s):
    return (0, N_META) if s == 0 else (N_META + 128 * (s - 1), 128)


def own_global_tile(i, p):
    return 4 * (i // 2) + 2 * p + (i % 2)


def global_to_ridx(g):
    return (g // 2) % 2, 2 * (g // 4) + (g % 2)


def rope_tab(pos, rot):
    inv = (np.float32(ROPE_THETA) ** (-np.arange(0, rot, 2, dtype=np.float32) / np.float32(rot))).astype(np.float32)
    ang = pos.astype(np.float32)[:, None] * inv[None, :]
    return np.cos(ang).astype(np.float32), np.sin(ang).astype(np.float32)


def host_tables(NI, p):
    NSL = NI + 1
    ropeA = np.zeros((128, NSL, 2, 64), np.float32)
    ropeB = np.zeros((128, NSL, 2, 64), np.float32)
    for s in range(NSL):
        if s == 0:
            pos = np.arange(N_META)
        else:
            pos = N_META + 128 * own_global_tile(s - 1, p) + np.arange(128)
        n = len(pos)
        c, sn = rope_tab(pos, 32)
        ropeA[:n, s, 0] = np.tile(c, (1, 4))
        ropeA[:n, s, 1] = np.tile(sn, (1, 4))
        c, sn = rope_tab(pos, 16)
        ropeB[:n, s, 0] = np.tile(c, (1, 8))
        ropeB[:n, s, 1] = np.tile(sn, (1, 8))
    mA = np.zeros((128, 4, 2, 2, 128), np.float32)
    diag = np.ones((128, 128), np.float32)
    diag[64:, :64] = 0.0
    for j in range(4):
        for t in range(2):
            gq = 2 * p + t
            if j < gq:
                m = np.ones((128, 128), np.float32)
            elif j == gq:
                m = diag
            else:
                m = np.zeros((128, 128), np.float32)
            mA[:, j, :, t, :] = m[:, None, :]
    mA = mA.reshape(128, 4, 512)
    prev = np.ones((128, 128), np.float32)
    prev[:64, 64:] = 0.0
    mB = np.zeros((128, 4, 4, 128), np.float32)
    mB[:, 0] = (prev * (1.0 if p == 0 else 0.0))[:, None, :]
    mB[:, 1] = (prev * (1.0 if p == 1 else 0.0))[:, None, :]
    mB[:, 2] = prev[:, None, :]
    mB[:, 3] = diag[:, None, :]
    mB = mB.reshape(128, 4, 512)
    return ropeA, ropeB, mA.astype(ml_dtypes.bfloat16), mB.astype(ml_dtypes.bfloat16)


def build_program(NI=16, n_a=2, n_b=2, debug_x=False, stop_after=None, proj_cbs=None):
    NSL = NI + 1
    NG = NI // 2
    TOK = N_META + 128 * NI
    nc = bass.Bass("TRN2", target_bir_lowering=False)

    def din(name, shape, dt=F32):
        return nc.dram_tensor(name, list(shape), dt, kind="ExternalInput").ap()

    x_in = din("x", [NI * 128, D])
    meta_in = din("meta", [N_META, D])
    acbs = list(proj_cbs) if proj_cbs is not None else list(range(16))
    a_w_in = din("a_w_in", [max(n_a, 1), D, 512 * len(acbs)])
    a_w_out = din("a_w_out", [max(n_a, 1), D, D])
    w_kv = din("w_kv", [D, 512])
    b_w_in = din("b_w_in", [max(n_b, 1), D, 2 * D if n_b else 512])
    b_w_out = din("b_w_out", [max(n_b, 1), D, D if n_b else 512])
    gn_in = din("gn", [128, 5 * KC])
    fnorm_in = din("fnorm", [D])
    lamv_in = din("lamv", [2 * 4 * 128])
    subln_in = din("subln", [2 * 256])
    sinks_in = din("sinks", [2 * 32])
    ropeA_in = din("ropeA", [128, NSL * 2 * 64])
    ropeB_in = din("ropeB", [128, NSL * 2 * 64])
    maskA_in = din("maskA", [128, 4 * 512], BF16)
    maskB_in = din("maskB", [128, 4 * 512], BF16)
    out = nc.dram_tensor("out", [NI * 128, D], F32, kind="ExternalOutput").ap()
    if debug_x:
        dbg = nc.dram_tensor("dbg", [NSL * 128, D], F32, kind="ExternalOutput").ap()

    xres = nc.dram_tensor("xres", [NSL * 128, D], F32).ap()
    qT_loc = nc.dram_tensor("qT_loc", [8 * 128, 2 * TOK], BF16).ap()
    kT_loc = nc.dram_tensor("kT_loc", [8 * NSL * 128, 256], BF16)
    v_loc = nc.dram_tensor("v_loc", [8 * NSL * 128, 256], BF16)
    kT_all = nc.dram_tensor("kT_all", [2 * 8 * NSL * 128, 256], BF16)
    v_all = nc.dram_tensor("v_all", [2 * 8 * NSL * 128, 256], BF16)
    gate_s = nc.dram_tensor("gate_s", [NSL * 128, D], BF16).ap()
    kb_loc = nc.dram_tensor("kb_loc", [4 * NSL * 128, 128], BF16)
    vb_loc = nc.dram_tensor("vb_loc", [NSL * 128, 4 * 64], BF16)
    kb_all = nc.dram_tensor("kb_all", [2 * 4 * NSL * 128, 128], BF16)
    vb_all = nc.dram_tensor("vb_all", [2 * NSL * 128, 4 * 64], BF16)
    qb_loc = nc.dram_tensor("qb_loc", [16 * 128, TOK], BF16).ap()

    P = Prog(nc)
    NOCC = int(os.environ.get("KNOCC", "0"))
    CCCH = int(os.environ.get("KCCCH", "1"))

    def gather(loc, allt, reads, d_outs, only=None):
        la = loc.ap().bitcast(F32)
        aa = allt.ap().bitcast(F32)
        nch = len(d_outs)
        rows = la.shape[0] // nch
        for ch in (range(nch) if only is None else [only]):
            src = la[ch * rows:(ch + 1) * rows, :]
            dst = aa[2 * ch * rows:2 * (ch + 1) * rows, :]
            rd = reads[ch]
            if NOCC:
                P.dma("sp", dst[0:rows, :], src, reads=rd, writes=[d_outs[ch]])
                P.dma("sp", dst[rows:2 * rows, :], src, reads=rd, writes=[d_outs[ch]])
            else:
                P.custom("pool", lambda e, src=src, dst=dst: e.collective_compute(
                    "AllGather", ALU.bypass, replica_groups=PAIRS, ins=[src.opt()], outs=[dst.opt()]), 1,
                         reads=rd, writes=[d_outs[ch]])
    ident = P.sbuf([128, 128], BF16, "ident")
    gn = P.sbuf([128, 5 * KC], F32, "gn")
    ropeA = P.sbuf([128, NSL, 2, 64], F32, "ropeA")
    ropeB = P.sbuf([128, NSL, 2, 64], F32, "ropeB")
    maskA = P.sbuf([128, 4, 512], BF16, "maskA")
    maskB = P.sbuf([128, 4, 512], BF16, "maskB")
    lamv = P.sbuf([128, 2, 4, 128], F32, "lamv")
    subln = P.sbuf([128, 2, 256], F32, "subln")
    sinks = P.sbuf([128, 2, 32], F32, "sinks")
    nlam = P.sbuf([128, 2], F32, "nlam")
    epsT = P.sbuf([128, 1], F32, "epsT")
    hT = P.sbuf([128, KC, TOK], BF16, "hT")
    d_const = Dep()
    d_x = deps(NSL)
    d_hT = deps(NSL)

    with P.scope():
        idf = P.sbuf([128, 128], F32)
        lt = P.sbuf([128, 2, 2, 128], F32)
        ls = P.sbuf([128, 2, 2], F32)
        d_i = Dep()
        P.op("pool", lambda e: e.memset(idf[:], 1.0), writes=[d_i])
        P.op("pool", lambda e: e.affine_select(out=idf[:], in_=idf[:], pattern=[[-1, 128]], compare_op=ALU.is_equal,
                                               fill=0.0, base=0, channel_multiplier=1), reads=[d_i], writes=[d_i])
        P.op("dve", lambda e: e.tensor_copy(out=ident[:], in_=idf[:]), reads=[d_i], writes=[d_const])
        P.dma("sp", gn[:], gn_in, writes=[d_const])
        P.dma("sp", ropeA[:].rearrange("p a b c -> p (a b c)"), ropeA_in, writes=[d_const])
        P.dma("sp", ropeB[:].rearrange("p a b c -> p (a b c)"), ropeB_in, writes=[d_const])
        P.dma("sp", maskA[:].rearrange("p a b -> p (a b)"), maskA_in, writes=[d_const])
        P.dma("sp", maskB[:].rearrange("p a b -> p (a b)"), maskB_in, writes=[d_const])
        P.dma("sp", lamv[:].rearrange("p a b c -> p (a b c)"), lamv_in.partition_broadcast(128), writes=[d_const])
        P.dma("sp", subln[:].rearrange("p a b -> p (a b)"), subln_in.partition_broadcast(128), writes=[d_const])
        P.dma("sp", sinks[:].rearrange("p a b -> p (a b)"), sinks_in.partition_broadcast(128), writes=[d_const])
        zt = P.sbuf([128, 256], BF16)
        d_z = Dep()
        P.op("pool", lambda e: e.memset(zt[:], 0.0), writes=[d_z])
        kz = kT_loc.ap().rearrange("(h s d) c -> h s d c", h=8, s=NSL)
        vz = v_loc.ap().rearrange("(h s t) c -> h s t c", h=8, s=NSL)
        for hp in range(8):
            P.dma("sp", kz[hp, 0], zt[:], reads=[d_z])
            P.dma("sp", vz[hp, 0], zt[:], reads=[d_z])
        kbz = kb_loc.ap().rearrange("(c s d) t -> c s d t", c=4, s=NSL)
        for c in range(4):
            P.dma("sp", kbz[c, 0], zt[:, 0:128], reads=[d_z])
        P.dma("sp", vb_loc.ap()[0:128, :], zt[:], reads=[d_z])
        P.dma("sp", xres[0:N_META, :], meta_in, writes=[d_x[0]])
        for s in range(1, NSL):
            P.dma("sp", xres[s * 128:(s + 1) * 128, :], x_in[(s - 1) * 128:s * 128, :], writes=[d_x[s]])
        for l in range(2):
            for j in range(2):
                P.op("dve", lambda e, l=l, j=j: e.tensor_tensor(out=lt[:, l, j, :], in0=lamv[:, l, 2 * j, :],
                                                                 in1=lamv[:, l, 2 * j + 1, :], op=ALU.mult),
                     reads=[d_const], writes=[d_i])
                P.op("dve", lambda e, l=l, j=j: e.reduce_sum(out=ls[:, l, j:j + 1], in_=lt[:, l, j, :], axis=AX.X),
                     reads=[d_i], writes=[d_i])
        P.op("act", lambda e: e.activation(out=ls[:].rearrange("p a b -> p (a b)"), in_=ls[:].rearrange("p a b -> p (a b)"),
                                           func=AF.Exp), reads=[d_i], writes=[d_i])
        for l in range(2):
            lam_init = 0.8 - 0.6 * float(np.exp(-0.3 * l))
            P.op("dve", lambda e, l=l: e.tensor_tensor(out=nlam[:, l:l + 1], in0=ls[:, l, 1:2], in1=ls[:, l, 0:1],
                                                       op=ALU.subtract), reads=[d_i], writes=[d_const])
            P.op("dve", lambda e, l=l, li=lam_init: e.tensor_scalar(out=nlam[:, l:l + 1], in0=nlam[:, l:l + 1],
                                                                    scalar1=-li, scalar2=None, op0=ALU.add),
                 reads=[d_const], writes=[d_const])
        for l in range(2):
            lam_init = 0.8 - 0.6 * float(np.exp(-0.3 * l))
            P.op("dve", lambda e, l=l, li=lam_init: e.tensor_scalar(out=subln[:, l, :], in0=subln[:, l, :], scalar1=1.0 - li,
                                                                    scalar2=None, op0=ALU.mult),
                 reads=[d_const], writes=[d_const])
        P.op("dve", lambda e: e.memset(epsT[:], EPS), writes=[d_const])
        P.op("act", lambda e: e.activation(out=sinks[:].rearrange("p a b -> p (a b)"),
                                           in_=sinks[:].rearrange("p a b -> p (a b)"), func=AF.Exp),
             reads=[d_const], writes=[d_const])

    def phase_norm():
        with P.scope():
            xt = [P.sbuf([128, D], F32) for _ in range(2)]
            xn = [P.sbuf([128, D], BF16) for _ in range(2)]
            junk = P.sbuf([128, D], BF16)
            ss = [P.sbuf([128, 1], F32) for _ in range(2)]
            pt = [P.psum([128, 4, 128], BF16) for _ in range(3)]
            d_xt, d_xn, d_ss = deps(2), deps(2), deps(2)
            d_junk = Dep()
            d_pt = deps(3)
            n = 0
            for s in range(NSL):
                st, rows = slot_cols(s)
                b = s % 2
                P.dma("sp", xt[b][:rows], xres[s * 128:s * 128 + rows, :], reads=[d_x[s]], writes=[d_xt[b]])
                P.op("dve", lambda e, b=b: e.memset(ss[b][:], 0.0), writes=[d_ss[b]])
                P.op("act", lambda e, b=b, rows=rows: e.activation(out=junk[:rows], in_=xt[b][:rows], func=AF.Square,
                                                                  accum_out=ss[b][:rows]),
                     reads=[d_xt[b]], writes=[d_junk, d_ss[b]])
                P.op("act", lambda e, b=b, rows=rows: e.activation(out=ss[b][:rows], in_=ss[b][:rows], func=AF.Ln,
                                                                  bias=epsT[:rows], scale=1.0 / D),
                     reads=[d_ss[b], d_const], writes=[d_ss[b]])
                P.op("act", lambda e, b=b, rows=rows: e.activation(out=ss[b][:rows], in_=ss[b][:rows], func=AF.Exp,
                                                                  scale=-0.5),
                     reads=[d_ss[b]], writes=[d_ss[b]])
                P.op("dve", lambda e, b=b, rows=rows: e.tensor_scalar(out=xn[b][:rows], in0=xt[b][:rows],
                                                                     scalar1=ss[b][:rows, 0:1], scalar2=None, op0=ALU.mult),
                     reads=[d_xt[b], d_ss[b]], writes=[d_xn[b]])
                for k4 in range(4):
                    j = n % 3
                    n += 1
                    for k in range(4):
                        kc = k4 * 4 + k
                        P.op("pe", lambda e, j=j, k=k, kc=kc, b=b, rows=rows: e.transpose(
                            out=pt[j][:, k, :rows], in_=xn[b][:rows, kc * 128:(kc + 1) * 128], identity=ident[:rows, :rows]),
                             reads=[d_xn[b], d_const], writes=[d_pt[j]])
                    q = "act" if k4 % 2 else "dve"
                    if q == "act":
                        P.op(q, lambda e, j=j, k4=k4, st=st, rows=rows: e.activation(
                            out=hT[:, k4 * 4:k4 * 4 + 4, st:st + rows], in_=pt[j][:, :, :rows], func=AF.Copy),
                             reads=[d_pt[j]], writes=[d_hT[s]])
                    else:
                        P.op(q, lambda e, j=j, k4=k4, st=st, rows=rows: e.tensor_copy(
                            out=hT[:, k4 * 4:k4 * 4 + 4, st:st + rows], in_=pt[j][:, :, :rows]),
                             reads=[d_pt[j]], writes=[d_hT[s]])

    def project(w_ap, ncols, gcol, consume, flush, cbmap=None):
        ncb = ncols // 512
        with P.scope():
            wf = [P.sbuf([128, 8, 512], F32) for _ in range(2)]
            wb = [P.sbuf([128, KC, 512], BF16) for _ in range(2)]
            acc = [P.psum([128, 512], F32) for _ in range(4)]
            d_wf = deps(2)
            d_wb = [deps(KC) for _ in range(2)]
            d_acc = deps(4)
            cast_jobs = []

            def issue_load(cb):
                for half in range(2):
                    f = (2 * cb + half) % 2
                    P.dma("sp", wf[f][:], w_ap[half * 1024:(half + 1) * 1024, cb * 512:(cb + 1) * 512]
                          .rearrange("(kc p) n -> p kc n", p=128), writes=[d_wf[f]])
                    for k in range(8):
                        cast_jobs.append((cb, half, k))

            def do_casts(nmax):
                for _ in range(min(nmax, len(cast_jobs))):
                    cb, half, k = cast_jobs.pop(0)
                    f = (2 * cb + half) % 2
                    kc = half * 8 + k
                    q = "pool" if k % 2 else "dve"
                    wbuf = wb[cb % 2]
                    if gcol is not None:
                        P.op(q, lambda e, wbuf=wbuf, f=f, k=k, kc=kc: e.tensor_scalar(
                            out=wbuf[:, kc, :], in0=wf[f][:, k, :], scalar1=gn[:, gcol * KC + kc:gcol * KC + kc + 1],
                            scalar2=None, op0=ALU.mult), reads=[d_wf[f], d_const], writes=[d_wb[cb % 2][kc]])
                    else:
                        P.op(q, lambda e, wbuf=wbuf, f=f, k=k, kc=kc: e.tensor_copy(out=wbuf[:, kc, :], in_=wf[f][:, k, :]),
                             reads=[d_wf[f]], writes=[d_wb[cb % 2][kc]])

            issue_load(0)
            do_casts(16)
            n = 0
            for cb in range(ncb):
                if cb + 1 < ncb:
                    issue_load(cb + 1)
                for s in range(NSL):
                    st, rows = slot_cols(s)
                    a = n % 4
                    n += 1
                    for kc in range(KC):
                        P.op("pe", lambda e, a=a, kc=kc, st=st, rows=rows, cb=cb: e.matmul(
                            acc[a][:rows, :], lhsT=hT[:, kc, st:st + rows], rhs=wb[cb % 2][:, kc, :],
                            start=(kc == 0), stop=(kc == KC - 1)),
                             reads=[d_hT[s], d_wb[cb % 2][kc]], writes=[d_acc[a]])
                    consume(cbmap[cb] if cbmap else cb, s, acc[a], rows, d_acc[a])
                    if s >= 2:
                        do_casts(2)
                do_casts(16)
            flush()

    def out_proj(w_ap, d_og):
        for s in range(NSL):
            d_hT[s] = d_og[s]
        with P.scope():
            xr = [P.sbuf([128, 512], F32) for _ in range(3)]
            xo = [P.sbuf([128, 512], F32) for _ in range(3)]
            d_xr, d_xo = deps(3), deps(3)
            d_xs = [[Dep() for _ in range(4)] for _ in range(NSL)]
            cnt = {"n": 0}

            def consume(cb, s, a, rows, d_a):
                j = cnt["n"] % 3
                cnt["n"] += 1
                P.dma("sp", xr[j][:rows], xres[s * 128:s * 128 + rows, cb * 512:(cb + 1) * 512], reads=[d_x[s]],
                      writes=[d_xr[j]])
                P.op("dve", lambda e: e.tensor_tensor(out=xo[j][:rows], in0=a[:rows, :], in1=xr[j][:rows], op=ALU.add),
                     reads=[d_a, d_xr[j]], writes=[d_xo[j]])
                P.dma("sp", xres[s * 128:s * 128 + rows, cb * 512:(cb + 1) * 512], xo[j][:rows], reads=[d_xo[j]],
                      writes=[d_xs[s][cb]])

            project(w_ap, D, None, consume, lambda: None)

    def a_layer(l):
        phase_norm()
        if stop_after == "norm":
            return
        d_q = [[Dep() for _ in range(NSL)] for _ in range(8)]
        d_k = [[Dep() for _ in range(NSL)] for _ in range(8)]
        d_v = [[Dep() for _ in range(NSL)] for _ in range(8)]
        d_g = [[Dep() for _ in range(4)] for _ in range(NSL)]
        qT4 = qT_loc.rearrange("(h d) (e t) -> h d e t", d=128, e=2)
        kT4 = kT_loc.ap().rearrange("(h s d) (e t) -> h s d e t", h=8, s=NSL, e=2)
        v4 = v_loc.ap().rearrange("(h s t) c -> h s t c", h=8, s=NSL)

        with P.scope():
            tmp = [P.sbuf([128, 4, 128], BF16) for _ in range(3)]
            T = [P.sbuf([128, 4, 4, 16], F32) for _ in range(2)]
            stg = [P.sbuf([128, 4, 128], BF16) for _ in range(3)]
            vt = [P.sbuf([128, 512], BF16) for _ in range(3)]
            tp = [P.psum([128, 4, 128], BF16) for _ in range(2)]
            d_tmp = [deps(3) for _ in range(3)]
            d_T = deps(2)
            d_stg = deps(3)
            d_vt = deps(3)
            d_tp = deps(2)
            cnt = {"r": 0, "t": 0, "v": 0}
            pending = []

            def emit_transposes(item):
                cb, s, j, rows = item
                st, _ = slot_cols(s)
                k = cnt["t"] % 2
                g = cnt["t"] % 3
                cnt["t"] += 1
                for h in range(4):
                    P.op("pe", lambda e, k=k, h=h, j=j, rows=rows: e.transpose(
                        out=tp[k][:, h, :rows], in_=tmp[j][:rows, h, :], identity=ident[:rows, :rows]),
                         reads=d_tmp[j] + [d_const], writes=[d_tp[k]])
                P.op("act", lambda e, k=k, g=g, rows=rows: e.activation(out=stg[g][:, :, :rows], in_=tp[k][:, :, :rows],
                                                                       func=AF.Copy),
                     reads=[d_tp[k]], writes=[d_stg[g]])
                isq = cb < 4
                c4 = cb % 4
                if DBG == 3:
                    return
                for hh in range(2):
                    hp = 2 * c4 + hh
                    if isq:
                        P.dma("sp", qT4[hp, :, :, st:st + rows], stg[g][:, 2 * hh:2 * hh + 2, :rows],
                              reads=[d_stg[g]], writes=[d_q[hp][s]])
                    else:
                        P.dma("sp", kT4[hp, s, :, :, 0:rows], stg[g][:, 2 * hh:2 * hh + 2, :rows],
                              reads=[d_stg[g]], writes=[d_k[hp][s]])

            def consume(cb, s, a, rows, d_a):
                kind = cb // 4
                if DBG == 1:
                    return
                if kind < 2:
                    j = cnt["r"] % 3
                    tt = cnt["r"] % 2
                    cnt["r"] += 1
                    a3 = a[:rows, :].rearrange("p (h d) -> p h d", h=4)
                    cs = ropeA[:rows, s, 0, :].rearrange("p (h i) -> p h i", h=4)
                    sn = ropeA[:rows, s, 1, :].rearrange("p (h i) -> p h i", h=4)
                    Tt = T[tt]
                    t3 = tmp[j]
                    P.op("dve", lambda e: e.tensor_tensor(out=Tt[:rows, 0], in0=a3[:, :, 0:16], in1=cs, op=ALU.mult),
                         reads=[d_a, d_const], writes=[d_T[tt]])
                    P.op("dve", lambda e: e.tensor_tensor(out=Tt[:rows, 1], in0=a3[:, :, 16:32], in1=sn, op=ALU.mult),
                         reads=[d_a], writes=[d_T[tt]])
                    P.op("dve", lambda e: e.tensor_tensor(out=Tt[:rows, 2], in0=a3[:, :, 16:32], in1=cs, op=ALU.mult),
                         reads=[d_a], writes=[d_T[tt]])
                    P.op("dve", lambda e: e.tensor_tensor(out=Tt[:rows, 3], in0=a3[:, :, 0:16], in1=sn, op=ALU.mult),
                         reads=[d_a], writes=[d_T[tt]])
                    if DBG == 4:
                        return
                    P.op("dve", lambda e: e.tensor_tensor(out=t3[:rows, :, 0:16], in0=Tt[:rows, 0], in1=Tt[:rows, 1],
                                                          op=ALU.subtract), reads=[d_T[tt]], writes=[d_tmp[j][0]])
                    P.op("dve", lambda e: e.tensor_tensor(out=t3[:rows, :, 16:32], in0=Tt[:rows, 2], in1=Tt[:rows, 3],
                                                          op=ALU.add), reads=[d_T[tt]], writes=[d_tmp[j][1]])
                    if DBG == 5:
                        return
                    P.op("dve", lambda e: e.tensor_copy(out=t3[:rows, :, 32:128], in_=a3[:, :, 32:128]),
                         reads=[d_a], writes=[d_tmp[j][2]])
                    if DBG == 2:
                        return
                    pending.append((cb, s, j, rows))
                    if len(pending) > 1:
                        emit_transposes(pending.pop(0))
                elif kind == 2:
                    j = cnt["v"] % 3
                    cnt["v"] += 1
                    c4 = cb % 4
                    P.op("act", lambda e: e.activation(out=vt[j][:rows], in_=a[:rows, :], func=AF.Copy),
                         reads=[d_a], writes=[d_vt[j]])
                    for hh in range(2):
                        hp = 2 * c4 + hh
                        P.dma("sp", v4[hp, s, 0:rows, :], vt[j][:rows, hh * 256:(hh + 1) * 256],
                              reads=[d_vt[j]], writes=[d_v[hp][s]])
                else:
                    j = cnt["v"] % 3
                    cnt["v"] += 1
                    c4 = cb % 4
                    P.op("act", lambda e: e.activation(out=vt[j][:rows], in_=a[:rows, :], func=AF.Silu),
                         reads=[d_a], writes=[d_vt[j]])
                    P.dma("sp", gate_s[s * 128:s * 128 + rows, c4 * 512:(c4 + 1) * 512], vt[j][:rows],
                          reads=[d_vt[j]], writes=[d_g[s][c4]])

            def flush():
                while pending:
                    emit_transposes(pending.pop(0))

            project(a_w_in[l], 512 * len(acbs), l, consume, flush, cbmap=acbs)

        if stop_after == "proj":
            return
        d_kall, d_vall = deps(8), deps(8)
        for hp in range(8):
            gather(kT_loc, kT_all, [d_k[h] for h in range(8)], d_kall, only=hp)
            gather(v_loc, v_all, [d_v[h] for h in range(8)], d_vall, only=hp)
        if stop_after == "gather":
            return

        kA = kT_all.ap().rearrange("(h r s d) c -> r h d s c", r=2, h=8, s=NSL)
        vA = v_all.ap().rearrange("(h r s t) c -> r h t s c", r=2, h=8, s=NSL)
        qT3 = qT_loc.rearrange("(h d) c -> h d c", d=128)
        SCALE = float(A_HD) ** -0.5
        lam_init = 0.8 - 0.6 * float(np.exp(-0.3 * l))
        d_og = deps(NSL)
        with P.scope():
            kbuf = [P.sbuf([128, 2 * NSL, 256], BF16) for _ in range(2)]
            vbuf = [P.sbuf([128, 2 * NSL, 264], BF16) for _ in range(2)]
            qbuf = [P.sbuf([128, 2 * TOK], BF16) for _ in range(2)]
            PT = [P.sbuf([128, 512], BF16) for _ in range(3)]
            gt = [P.sbuf([128, 256], BF16) for _ in range(2)]
            oa = [P.sbuf([128, 256], F32) for _ in range(2)]
            ob = [P.sbuf([128, 256], F32) for _ in range(2)]
            og = [P.sbuf([128, 256], BF16) for _ in range(2)]
            junk = P.sbuf([128, 256], BF16)
            sm = [P.sbuf([128, 8], F32) for _ in range(2)]
            S = [P.psum([128, 512], F32) for _ in range(2)]
            O = [[P.psum([128, 512], F32) for _ in range(2)] for _ in range(2)]
            tps = P.psum([128, 2, 128], BF16)
            d_kb, d_vb, d_qb = deps(2), deps(2), deps(2)
            d_PT, d_S = deps(3), deps(2)
            d_O = [deps(2) for _ in range(2)]
            d_gt, d_oa, d_ob, d_ogs, d_sm = deps(2), deps(2), deps(2), deps(2), deps(2)
            d_junk, d_tps = Dep(), Dep()
            for b in range(2):
                P.op("pool", lambda e, b=b: e.memset(vbuf[b][:, :, 256:257], 1.0), writes=[d_vb[b]])

            def load_hp(hp):
                b = hp % 2
                for r in range(2):
                    P.dma("sp", kbuf[b][:, r * NSL:(r + 1) * NSL, :], kA[r, hp], reads=[d_kall[hp]], writes=[d_kb[b]])
                    P.dma("sp", vbuf[b][:, r * NSL:(r + 1) * NSL, 0:256], vA[r, hp], reads=[d_vall[hp]], writes=[d_vb[b]])
                P.dma("sp", qbuf[b][:], qT3[hp], reads=d_q[hp], writes=[d_qb[b]])

            load_hp(0)
            ns = 0
            ngr = 0
            deferred = []

            def flush_deferred():
                while deferred:
                    deferred.pop(0)()
            for hp in range(8):
                b = hp % 2
                if hp + 1 < 8:
                    load_hp(hp + 1)
                qb3 = qbuf[b][:].rearrange("p (e t) -> p e t", e=2)
                groups = [("meta", None)] + [("real", m) for m in range(NG)]
                for kind, m in groups:
                    if kind == "meta":
                        qc, nq = 0, N_META
                        qtiles = [(0, N_META, 0)]
                        kslots = [(0, N_META, None)]
                    else:
                        qc, nq = N_META + 256 * m, 256
                        qtiles = [(0, 128, 1 + 2 * m), (128, 128, 2 + 2 * m)]
                        kslots = [(0, N_META, None)]
                        for g in range(4 * m + 4):
                            r, i = global_to_ridx(g)
                            kslots.append((r * NSL + 1 + i, 128, (g - 4 * m) if g >= 4 * m else None))
                    nks = len(kslots)
                    slot_bufs = []

                    def emit_qk(si):
                        nonlocal ns
                        kidx, nk, mi = kslots[si]
                        sb = ns % 2
                        pb = ns % 3
                        ns += 1
                        slot_bufs.append(pb)
                        for e_ in range(2):
                            P.op("pe", lambda e, sb=sb, e_=e_, kidx=kidx, nk=nk, qc=qc, nq=nq, b=b, qb3=qb3: e.matmul(
                                S[sb][:nk, e_ * 256:e_ * 256 + nq], lhsT=kbuf[b][:, kidx, e_ * 128:e_ * 128 + nk],
                                rhs=qb3[:, e_, qc:qc + nq], start=True, stop=True),
                                 reads=[d_kb[b], d_qb[b]], writes=[d_S[sb]])
                        if nq == 256:
                            P.op("act", lambda e, sb=sb, pb=pb, nk=nk: e.activation(out=PT[pb][:nk, :], in_=S[sb][:nk, :],
                                                                                 func=AF.Exp, scale=SCALE),
                                 reads=[d_S[sb]], writes=[d_PT[pb]])
                        else:
                            P.op("act", lambda e, sb=sb, pb=pb, nk=nk, nq=nq: e.activation(
                                out=PT[pb][:nk, :].rearrange("p (e q) -> p e q", e=2)[:, :, 0:nq],
                                in_=S[sb][:nk, :].rearrange("p (e q) -> p e q", e=2)[:, :, 0:nq], func=AF.Exp, scale=SCALE),
                                 reads=[d_S[sb]], writes=[d_PT[pb]])
                        if mi is not None:
                            mq = "pool" if (mi % 2) else "dve"
                            P.op(mq, lambda e, pb=pb, mi=mi: e.tensor_tensor(out=PT[pb][:, :], in0=PT[pb][:, :],
                                                                             in1=maskA[:, mi, :], op=ALU.mult),
                                 reads=[d_PT[pb], d_const], writes=[d_PT[pb]])

                    def emit_pv(si):
                        kidx, nk, mi = kslots[si]
                        pb = slot_bufs[si]
                        for e_ in range(2):
                            for ti, (qo, qr, _) in enumerate(qtiles):
                                P.op("pe", lambda e, e_=e_, ti=ti, qo=qo, qr=qr, pb=pb, nk=nk, kidx=kidx, b=b, si=si, nks=nks:
                                     e.matmul(O[e_][ti][:qr, 0:257], lhsT=PT[pb][:nk, e_ * 256 + qo:e_ * 256 + qo + qr],
                                              rhs=vbuf[b][:nk, kidx, 0:257], start=(si == 0), stop=(si == nks - 1)),
                                     reads=[d_PT[pb], d_vb[b]], writes=[d_O[e_][ti]])

                    for si in range(nks):
                        emit_qk(si)
                        if si >= 1:
                            emit_pv(si - 1)
                        if si == 1:
                            flush_deferred()
                    emit_pv(nks - 1)
                    if nks == 1:
                        flush_deferred()
                    for ti, (qo, qr, slot) in enumerate(qtiles):
                        st, rows = slot_cols(slot)
                        w = ngr % 2
                        ngr += 1
                        smw = sm[w]
                        P.dma("sp", gt[w][:rows], gate_s[slot * 128:slot * 128 + rows, hp * 256:(hp + 1) * 256],
                              reads=d_g[slot], writes=[d_gt[w]])
                        P.op("dve", lambda e, smw=smw, ti=ti, rows=rows: e.reciprocal(out=smw[:rows, 0:1],
                                                                                    in_=O[0][ti][:rows, 256:257]),
                             reads=[d_O[0][ti]], writes=[d_sm[w]])
                        P.op("dve", lambda e, smw=smw, ti=ti, rows=rows: e.reciprocal(out=smw[:rows, 1:2],
                                                                                    in_=O[1][ti][:rows, 256:257]),
                             reads=[d_O[1][ti]], writes=[d_sm[w]])
                        P.op("dve", lambda e, smw=smw, rows=rows: e.tensor_tensor(out=smw[:rows, 2:3], in0=smw[:rows, 1:2],
                                                                                 in1=nlam[:rows, l:l + 1], op=ALU.mult),
                             reads=[d_sm[w], d_const], writes=[d_sm[w]])
                        P.op("dve", lambda e, smw=smw, ti=ti, rows=rows, w=w: e.tensor_scalar(
                            out=oa[w][:rows], in0=O[0][ti][:rows, 0:256], scalar1=smw[:rows, 0:1], scalar2=None, op0=ALU.mult),
                             reads=[d_O[0][ti], d_sm[w]], writes=[d_oa[w]])
                        P.op("dve", lambda e, smw=smw, ti=ti, rows=rows, w=w: e.scalar_tensor_tensor(
                            out=ob[w][:rows], in0=O[1][ti][:rows, 0:256], scalar=smw[:rows, 2:3], in1=oa[w][:rows],
                            op0=ALU.mult, op1=ALU.add),
                             reads=[d_O[1][ti], d_sm[w], d_oa[w]], writes=[d_ob[w]])
                        P.op("dve", lambda e, smw=smw: e.memset(smw[:, 3:4], 0.0), writes=[d_sm[w]])
                        P.op("act", lambda e, smw=smw, rows=rows, w=w: e.activation(out=junk[:rows], in_=ob[w][:rows],
                                                                                 func=AF.Square, accum_out=smw[:rows, 3:4]),
                             reads=[d_ob[w], d_sm[w]], writes=[d_junk, d_sm[w]])
                        P.op("act", lambda e, smw=smw, rows=rows: e.activation(
                            out=smw[:rows, 4:5], in_=smw[:rows, 3:4], func=AF.Ln, bias=epsT[:rows], scale=1.0 / 256.0),
                             reads=[d_sm[w], d_const], writes=[d_sm[w]])
                        P.op("act", lambda e, smw=smw, rows=rows: e.activation(
                            out=smw[:rows, 5:6], in_=smw[:rows, 4:5], func=AF.Exp, scale=-0.5),
                             reads=[d_sm[w]], writes=[d_sm[w]])
                        P.op("dve", lambda e, smw=smw, rows=rows, w=w: e.scalar_tensor_tensor(
                            out=oa[w][:rows], in0=ob[w][:rows], scalar=smw[:rows, 5:6], in1=subln[:rows, l, :],
                            op0=ALU.mult, op1=ALU.mult),
                             reads=[d_ob[w], d_sm[w], d_const], writes=[d_oa[w]])
                        P.op("pool", lambda e, rows=rows, w=w: e.tensor_tensor(out=og[w][:rows], in0=oa[w][:rows],
                                                                             in1=gt[w][:rows], op=ALU.mult),
                             reads=[d_oa[w], d_gt[w]], writes=[d_ogs[w]])
                        def _tr(w=w, rows=rows, hp=hp, st=st, slot=slot):
                            for c in range(2):
                                P.op("pe", lambda e, c=c: e.transpose(out=tps[:, c, :rows],
                                                                      in_=og[w][:rows, c * 128:(c + 1) * 128],
                                                                      identity=ident[:rows, :rows]),
                                     reads=[d_ogs[w], d_const], writes=[d_tps])
                            P.op("act", lambda e: e.activation(
                                out=hT[:, 2 * hp:2 * hp + 2, st:st + rows], in_=tps[:, :, :rows], func=AF.Copy),
                                 reads=[d_tps], writes=[d_og[slot]])
                        deferred.append(_tr)
            flush_deferred()

        if stop_after == "attn":
            return
        out_proj(a_w_out[l], d_og)

    kb4 = kb_loc.ap().rearrange("(c s d) t -> c s d t", c=4, s=NSL)
    kbA = kb_all.ap().rearrange("(c r s d) t -> r c s d t", r=2, c=4, s=NSL)
    vbA = vb_all.ap().rearrange("(r s t) c -> r s t c", r=2, s=NSL)
    qb3 = qb_loc.rearrange("(a d) t -> a d t", d=128)
    d_kball, d_vball = deps(4), deps(1)
    SCALE_B = float(B_HD) ** -0.5

    def rope_b(a3, s, rows, nh, T_t, d_T_t, out3, d_out, d_a):
        cs = ropeB[:rows, s, 0, 0:nh * 8].rearrange("p (h i) -> p h i", h=nh)
        sn = ropeB[:rows, s, 1, 0:nh * 8].rearrange("p (h i) -> p h i", h=nh)
        P.op("dve", lambda e: e.tensor_tensor(out=T_t[:rows, 0, 0:nh], in0=a3[:, :, 0:8], in1=cs, op=ALU.mult),
             reads=[d_a, d_const], writes=[d_T_t])
        P.op("dve", lambda e: e.tensor_tensor(out=T_t[:rows, 1, 0:nh], in0=a3[:, :, 8:16], in1=sn, op=ALU.mult),
             reads=[d_a], writes=[d_T_t])
        P.op("dve", lambda e: e.tensor_tensor(out=T_t[:rows, 2, 0:nh], in0=a3[:, :, 8:16], in1=cs, op=ALU.mult),
             reads=[d_a], writes=[d_T_t])
        P.op("dve", lambda e: e.tensor_tensor(out=T_t[:rows, 3, 0:nh], in0=a3[:, :, 0:8], in1=sn, op=ALU.mult),
             reads=[d_a], writes=[d_T_t])
        P.op("dve", lambda e: e.tensor_tensor(out=out3[:, :, 0:8], in0=T_t[:rows, 0, 0:nh], in1=T_t[:rows, 1, 0:nh],
                                              op=ALU.subtract), reads=[d_T_t], writes=[d_out[0]])
        P.op("dve", lambda e: e.tensor_tensor(out=out3[:, :, 8:16], in0=T_t[:rows, 2, 0:nh], in1=T_t[:rows, 3, 0:nh],
                                              op=ALU.add), reads=[d_T_t], writes=[d_out[1]])
        P.op("dve", lambda e: e.tensor_copy(out=out3[:, :, 16:64], in_=a3[:, :, 16:64]), reads=[d_a], writes=[d_out[2]])

    def shared_kv():
        d_kb = deps(NSL)
        d_vb = deps(NSL)
        with P.scope():
            tmpk = [P.sbuf([128, 4, 2, 64], BF16) for _ in range(2)]
            T = [P.sbuf([128, 4, 8, 8], F32) for _ in range(2)]
            stg = [P.sbuf([128, 4, 128], BF16) for _ in range(2)]
            vt = [P.sbuf([128, 256], BF16) for _ in range(2)]
            tp = [P.psum([128, 4, 128], BF16) for _ in range(2)]
            d_tmp = [deps(4) for _ in range(2)]
            d_T, d_stg, d_vt, d_tp = deps(2), deps(2), deps(2), deps(2)
            cnt = {"n": 0}

            def consume(cb, s, a, rows, d_a):
                j = cnt["n"] % 2
                cnt["n"] += 1
                st, _ = slot_cols(s)
                a3 = a[:rows, 0:256].rearrange("p (h d) -> p h d", h=4)
                rope_b(a3, s, rows, 4, T[j], d_T[j], tmpk[j][:rows, :, 0, :], d_tmp[j], d_a)
                P.op("dve", lambda e: e.tensor_copy(out=tmpk[j][:rows, :, 1, :], in_=tmpk[j][:rows, :, 0, :]),
                     reads=d_tmp[j][0:3], writes=[d_tmp[j][3]])
                P.op("dve", lambda e: e.tensor_copy(out=vt[j][:rows], in_=a[:rows, 256:512]),
                     reads=[d_a], writes=[d_vt[j]])
                P.dma("sp", vb_loc.ap()[s * 128:s * 128 + rows, :], vt[j][:rows], reads=[d_vt[j]], writes=[d_vb[s]])
                for c in range(4):
                    P.op("pe", lambda e, c=c: e.transpose(out=tp[j][:, c, :rows],
                                                          in_=tmpk[j][:rows, c].rearrange("p a b -> p (a b)"),
                                                          identity=ident[:rows, :rows]),
                         reads=d_tmp[j] + [d_const], writes=[d_tp[j]])
                P.op("act", lambda e: e.activation(out=stg[j][:, :, :rows], in_=tp[j][:, :, :rows], func=AF.Copy),
                     reads=[d_tp[j]], writes=[d_stg[j]])
                P.dma("sp", kb4[:, s, :, 0:rows].rearrange("c d t -> d c t"), stg[j][:, :, :rows], reads=[d_stg[j]],
                      writes=[d_kb[s]])

            project(w_kv, 512, 2, consume, lambda: None)
        if stop_after == "bkv":
            return d_kb, d_vb
        gather(kb_loc, kb_all, [d_kb] * 4, d_kball)
        gather(vb_loc, vb_all, [d_vb], d_vball)
        return d_kb, d_vb

    def b_layer(j, d_kb, d_vb):
        if stop_after in ("bkv", "bgather"):
            return
        d_q = [[Dep() for _ in range(NSL)] for _ in range(4)]
        d_g = [[Dep() for _ in range(4)] for _ in range(NSL)]
        with P.scope():
            tmp = [P.sbuf([128, 8, 64], BF16) for _ in range(3)]
            T = [P.sbuf([128, 4, 8, 8], F32) for _ in range(2)]
            stg = [P.sbuf([128, 4, 128], BF16) for _ in range(3)]
            vt = [P.sbuf([128, 512], BF16) for _ in range(3)]
            tp = [P.psum([128, 4, 128], BF16) for _ in range(2)]
            d_tmp = [deps(3) for _ in range(3)]
            d_T, d_stg, d_vt, d_tp = deps(2), deps(3), deps(3), deps(2)
            cnt = {"r": 0, "t": 0, "v": 0}
            pending = []

            def emit_transposes(item):
                cb, s, jj, rows = item
                st, _ = slot_cols(s)
                k = cnt["t"] % 2
                g = cnt["t"] % 3
                cnt["t"] += 1
                for h in range(4):
                    P.op("pe", lambda e, h=h: e.transpose(out=tp[k][:, h, :rows],
                                                          in_=tmp[jj][:rows, 2 * h:2 * h + 2, :].rearrange("p a b -> p (a b)"),
                                                          identity=ident[:rows, :rows]),
                         reads=d_tmp[jj] + [d_const], writes=[d_tp[k]])
                P.op("act", lambda e: e.activation(out=stg[g][:, :, :rows], in_=tp[k][:, :, :rows], func=AF.Copy),
                     reads=[d_tp[k]], writes=[d_stg[g]])
                P.dma("sp", qb3[4 * cb:4 * cb + 4, :, st:st + rows].rearrange("a d t -> d a t"), stg[g][:, :, :rows],
                      reads=[d_stg[g]], writes=[d_q[cb][s]])

            def consume(cb, s, a, rows, d_a):
                if cb < 4:
                    jj = cnt["r"] % 3
                    tt = cnt["r"] % 2
                    cnt["r"] += 1
                    a3 = a[:rows, :].rearrange("p (h d) -> p h d", h=8)
                    rope_b(a3, s, rows, 8, T[tt], d_T[tt], tmp[jj][:rows], d_tmp[jj], d_a)
                    pending.append((cb, s, jj, rows))
                    if len(pending) > 1:
                        emit_transposes(pending.pop(0))
                else:
                    jj = cnt["v"] % 3
                    cnt["v"] += 1
                    c4 = cb - 4
                    P.op("act", lambda e: e.activation(out=vt[jj][:rows], in_=a[:rows, :], func=AF.Silu),
                         reads=[d_a], writes=[d_vt[jj]])
                    P.dma("sp", gate_s[s * 128:s * 128 + rows, c4 * 512:(c4 + 1) * 512], vt[jj][:rows],
                          reads=[d_vt[jj]], writes=[d_g[s][c4]])

            def flush():
                while pending:
                    emit_transposes(pending.pop(0))

            project(b_w_in[j], 2 * D, 3 + j, consume, flush)
        if stop_after == "bproj":
            return

        d_og = deps(NSL)
        with P.scope():
            ko = P.sbuf([128, 4, NSL, 128], BF16)
            vo = P.sbuf([128, NSL, 4, 66], BF16)
            kc_ = P.sbuf([128, 4, 2, NG, 128], BF16)
            vc_ = P.sbuf([128, 2, NG, 4, 66], BF16)
            qbuf = [P.sbuf([128, 4, TOK], BF16) for _ in range(2)]
            PT = [P.sbuf([128, 512], BF16) for _ in range(4)]
            gt = [P.sbuf([128, 512], BF16) for _ in range(2)]
            ob = [P.sbuf([128, 512], F32) for _ in range(2)]
            og = [P.sbuf([128, 512], BF16) for _ in range(2)]
            sm = [P.sbuf([128, 2, 8], F32) for _ in range(2)]
            S = [P.psum([128, 512], F32) for _ in range(4)]
            O = [P.psum([128, 4, 128], F32) for _ in range(2)]
            tps = P.psum([128, 4, 128], BF16)
            d_ko, d_vo, d_kc, d_vc = Dep(), Dep(), Dep(), Dep()
            d_qb = deps(2)
            d_PT, d_S, d_O = deps(4), deps(4), deps(2)
            d_gt, d_ob, d_ogs, d_sm = deps(2), deps(2), deps(2), deps(2)
            d_tps = Dep()
            P.op("pool", lambda e: e.memset(vo[:, :, :, 64:65], 1.0), writes=[d_vo])
            P.op("pool", lambda e: e.memset(vc_[:, :, :, :, 64:65], 1.0), writes=[d_vc])
            for c in range(4):
                P.dma("sp", ko[:, c], kb4[c].rearrange("s d t -> d s t"), reads=d_kb, writes=[d_ko])
            for s_ in range(NSL):
                P.dma("sp", vo[:, s_, :, 0:64], vb_loc.ap()[s_ * 128:(s_ + 1) * 128, :].rearrange("t (c d) -> t c d", c=4),
                      reads=[d_vb[s_]], writes=[d_vo])
            for m in range(NG):
                for r in range(2):
                    idx = 2 * m - 1 if r == 1 else 2 * m + 1
                    if idx < 0:
                        continue
                    for c in range(4):
                        P.dma("sp", kc_[:, c, r, m, :], kbA[r, c, 1 + idx], reads=[d_kball[c]], writes=[d_kc])
                    P.dma("sp", vc_[:, r, m, :, 0:64], vbA[r, 1 + idx].rearrange("t (c d) -> t c d", c=4),
                          reads=d_vball, writes=[d_vc])

            def load_q(c):
                P.dma("sp", qbuf[c % 2][:], qb3[4 * c:4 * c + 4].rearrange("a d t -> d a t"), reads=d_q[c],
                      writes=[d_qb[c % 2]])

            load_q(0)
            ns = 0
            ne = 0
            deferred_b = []

            def flush_deferred_b():
                while deferred_b:
                    deferred_b.pop(0)()
            for c in range(4):
                qb_ = qbuf[c % 2]
                if c + 1 < 4:
                    load_q(c + 1)
                for s in range(NSL):
                    st, rows = slot_cols(s)
                    ks = [(ko[:, c, 0, 0:N_META], vo[0:N_META, 0, c, 0:65], N_META, None)]
                    if s >= 1:
                        i = s - 1
                        if i % 2 == 1:
                            ks.append((ko[:, c, s - 1, :], vo[:, s - 1, c, 0:65], 128, 2))
                        else:
                            m = i // 2
                            if m >= 1:
                                ks.append((kc_[:, c, 1, m, :], vc_[:, 1, m, c, 0:65], 128, 0))
                            ks.append((kc_[:, c, 0, m, :], vc_[:, 0, m, c, 0:65], 128, 1))
                        ks.append((ko[:, c, s, :], vo[:, s, c, 0:65], 128, 3))
                    nks = len(ks)
                    steps = [(si, half) for si in range(nks) for half in range(2)]
                    step_buf = []

                    def emit_qk_b(n):
                        nonlocal ns
                        si, half = steps[n]
                        kap, vap, nk, mi = ks[si]
                        sb = ns % 4
                        ns += 1
                        step_buf.append(sb)
                        lo, hi = half * 64, half * 64 + 64
                        P.op("pe", lambda e, sb=sb, kap=kap, nk=nk, lo=lo, hi=hi, qb_=qb_, st=st, rows=rows: e.matmul(
                            S[sb][:nk, 0:4 * rows], lhsT=kap[lo:hi, 0:nk],
                            rhs=qb_[lo:hi, :, st:st + rows], start=True, stop=True),
                             reads=[d_ko, d_kc, d_qb[c % 2]], writes=[d_S[sb]])
                        P.op("act", lambda e, sb=sb, nk=nk, rows=rows: e.activation(
                            out=PT[sb][:nk, 0:4 * rows], in_=S[sb][:nk, 0:4 * rows], func=AF.Exp, scale=SCALE_B),
                             reads=[d_S[sb]], writes=[d_PT[sb]])
                        if mi is not None:
                            mq = "pool" if half else "dve"
                            P.op(mq, lambda e, sb=sb, mi=mi: e.tensor_tensor(out=PT[sb][:, :], in0=PT[sb][:, :],
                                                                             in1=maskB[:, mi, :], op=ALU.mult),
                                 reads=[d_PT[sb], d_const], writes=[d_PT[sb]])

                    def emit_pv_b(n):
                        si, half = steps[n]
                        kap, vap, nk, mi = ks[si]
                        sb = step_buf[n]
                        for a in range(4):
                            P.op("pe", lambda e, a=a, sb=sb, nk=nk, rows=rows, vap=vap, half=half, si=si, nks=nks: e.matmul(
                                O[half][:rows, a, 0:65], lhsT=PT[sb][:nk, a * rows:(a + 1) * rows], rhs=vap[0:nk, :],
                                start=(si == 0 and a == 0), stop=(si == nks - 1 and a == 3)),
                                 reads=[d_PT[sb], d_vo, d_vc], writes=[d_O[half]])

                    for n in range(len(steps)):
                        emit_qk_b(n)
                        if n >= 1:
                            emit_pv_b(n - 1)
                        if n == 1:
                            flush_deferred_b()
                    emit_pv_b(len(steps) - 1)
                    w = ne % 2
                    ne += 1
                    smw = sm[w]
                    P.dma("sp", gt[w][:rows], gate_s[s * 128:s * 128 + rows, c * 512:(c + 1) * 512], reads=d_g[s],
                          writes=[d_gt[w]])
                    for half in range(2):
                        P.op("dve", lambda e, half=half, smw=smw, rows=rows: e.tensor_tensor(
                            out=smw[:rows, half, 0:4], in0=O[half][:rows, :, 64],
                            in1=sinks[:rows, j, half * 16 + 4 * c:half * 16 + 4 * c + 4], op=ALU.add),
                             reads=[d_O[half], d_const], writes=[d_sm[w]])
                        P.op("dve", lambda e, half=half, smw=smw, rows=rows: e.reciprocal(out=smw[:rows, half, 4:8],
                                                                                       in_=smw[:rows, half, 0:4]),
                             reads=[d_sm[w]], writes=[d_sm[w]])
                        for a in range(4):
                            P.op("dve", lambda e, half=half, a=a, smw=smw, rows=rows, w=w: e.tensor_scalar(
                                out=ob[w][:rows, (2 * a + half) * 64:(2 * a + half) * 64 + 64], in0=O[half][:rows, a, 0:64],
                                scalar1=smw[:rows, half, 4 + a:5 + a], scalar2=None, op0=ALU.mult),
                                 reads=[d_O[half], d_sm[w]], writes=[d_ob[w]])
                    P.op("pool", lambda e, rows=rows, w=w: e.tensor_tensor(out=og[w][:rows], in0=ob[w][:rows], in1=gt[w][:rows],
                                                                         op=ALU.mult),
                         reads=[d_ob[w], d_gt[w]], writes=[d_ogs[w]])
                    def _trb(w=w, rows=rows, c=c, st=st, s=s):
                        for k in range(4):
                            P.op("pe", lambda e, k=k: e.transpose(out=tps[:, k, :rows],
                                                                  in_=og[w][:rows, k * 128:(k + 1) * 128],
                                                                  identity=ident[:rows, :rows]),
                                 reads=[d_ogs[w], d_const], writes=[d_tps])
                        P.op("act", lambda e: e.activation(
                            out=hT[:, 4 * c:4 * c + 4, st:st + rows], in_=tps[:, :, :rows], func=AF.Copy),
                             reads=[d_tps], writes=[d_og[s]])
                    deferred_b.append(_trb)
            flush_deferred_b()
        if stop_after == "battn":
            return
        out_proj(b_w_out[j], d_og)

    def final_phase():
        with P.scope():
            fn = P.sbuf([128, D], F32)
            xt = [P.sbuf([128, D], F32) for _ in range(2)]
            xo = [P.sbuf([128, D], F32) for _ in range(2)]
            junk = P.sbuf([128, D], BF16)
            ss = [P.sbuf([128, 1], F32) for _ in range(2)]
            d_fn, d_junk = Dep(), Dep()
            d_xt, d_xo, d_ss = deps(2), deps(2), deps(2)
            P.dma("sp", fn[:], fnorm_in.partition_broadcast(128), writes=[d_fn])
            recs = []
            for s in range(1, NSL):
                b = s % 2
                P.dma("sp", xt[b][:], xres[s * 128:(s + 1) * 128, :], reads=[d_x[s]], writes=[d_xt[b]])
                P.op("dve", lambda e, b=b: e.memset(ss[b][:], 0.0), writes=[d_ss[b]])
                P.op("act", lambda e, b=b: e.activation(out=junk[:], in_=xt[b][:], func=AF.Square, accum_out=ss[b][:]),
                     reads=[d_xt[b]], writes=[d_junk, d_ss[b]])
                P.op("act", lambda e, b=b: e.activation(out=ss[b][:], in_=ss[b][:], func=AF.Ln, bias=epsT[:], scale=1.0 / D),
                     reads=[d_ss[b], d_const], writes=[d_ss[b]])
                P.op("act", lambda e, b=b: e.activation(out=ss[b][:], in_=ss[b][:], func=AF.Exp, scale=-0.5),
                     reads=[d_ss[b]], writes=[d_ss[b]])
                P.op("dve", lambda e, b=b: e.scalar_tensor_tensor(out=xo[b][:], in0=xt[b][:], scalar=ss[b][:, 0:1], in1=fn[:],
                                                                  op0=ALU.mult, op1=ALU.mult),
                     reads=[d_xt[b], d_ss[b], d_fn], writes=[d_xo[b]])
                recs.append(P.dma("sp", out[(s - 1) * 128:s * 128, :], xo[b][:], reads=[d_xo[b]]))
            for r in recs:
                P.wait("sp", r)

    if stop_after != "init":
        for l in range(n_a):
            a_layer(l)
        if n_b > 0:
            phase_norm()
            d_kb, d_vb = shared_kv()
            for j in range(n_b):
                if j > 0:
                    phase_norm()
                b_layer(j, d_kb, d_vb)

    with P.scope():
        if debug_x:
            for s in range(NSL):
                st, rows = slot_cols(s)
                r = P.dma("sp", dbg[s * 128:s * 128 + rows, :], xres[s * 128:s * 128 + rows, :], reads=[d_x[s]])
                P.wait("sp", r)
        pass
    if n_a == 2 and n_b == 2 and stop_after is None:
        final_phase()
    else:
        rs_ = [P.dma("sp", out[(s - 1) * 128:s * 128, :], xres[s * 128:(s + 1) * 128, :], reads=[d_x[s]])
               for s in range(1, NSL)]
        for r in rs_:
            P.wait("sp", r)
    P.finish()
    return nc


def make_in_maps(inputs, NI, n_a=2, n_b=2, proj_cbs=None):
    NSL = NI + 1
    f32 = np.float32
    x = np.asarray(inputs["x"], f32)
    B = x.shape[0]
    gn = np.zeros((128, 5 * KC), f32)
    gains = [inputs["a_norm"][0], inputs["a_norm"][1], inputs["kv_norm"], inputs["b_norm"][0], inputs["b_norm"][1]]
    for w, g in enumerate(gains):
        gn[:, w * KC:(w + 1) * KC] = np.asarray(g, f32).reshape(KC, 128).T
    lamv = np.stack([np.stack([np.asarray(inputs[k], f32)[l] for k in
                               ("a_lambda_q1", "a_lambda_k1", "a_lambda_q2", "a_lambda_k2")]) for l in range(2)]).reshape(-1)
    shared = {
        "meta": np.ascontiguousarray(inputs["meta_tokens"], f32),
        "a_w_in": np.ascontiguousarray(np.asarray(inputs["a_w_in"], f32)[:max(n_a, 1)] if proj_cbs is None else
                                       np.concatenate([np.asarray(inputs["a_w_in"], f32)[:max(n_a, 1), :, cb * 512:(cb + 1) * 512]
                                                       for cb in proj_cbs], axis=2)),
        "a_w_out": np.ascontiguousarray(np.asarray(inputs["a_w_out"], f32)[:max(n_a, 1)]),
        "w_kv": np.ascontiguousarray(inputs["w_kv"], f32),
        "b_w_in": np.ascontiguousarray(np.asarray(inputs["b_w_in"], f32)[:max(n_b, 1), :, :(2 * D if n_b else 512)]),
        "b_w_out": np.ascontiguousarray(np.asarray(inputs["b_w_out"], f32)[:max(n_b, 1), :, :(D if n_b else 512)]),
        "gn": gn,
        "fnorm": np.ascontiguousarray(inputs["final_norm"], f32),
        "lamv": np.ascontiguousarray(lamv, f32),
        "subln": np.ascontiguousarray(inputs["a_subln"], f32).reshape(-1),
        "sinks": np.ascontiguousarray(np.asarray(inputs["b_sinks"], f32).reshape(2, 4, 4, 2).transpose(0, 3, 1, 2)).reshape(-1),
    }
    tabs = [host_tables(NI, p) for p in range(2)]
    in_maps = []
    for c in range(8):
        b, p = c // 2, c % 2
        xt = x[b].reshape(2 * NI, 128, D)
        own = [own_global_tile(i, p) for i in range(NI)]
        m = dict(shared)
        m["x"] = np.ascontiguousarray(xt[own].reshape(NI * 128, D))
        ra, rb, ma, mb = tabs[p]
        m["ropeA"] = ra.reshape(128, -1)
        m["ropeB"] = rb.reshape(128, -1)
        m["maskA"] = ma.reshape(128, -1)
        m["maskB"] = mb.reshape(128, -1)
        in_maps.append(m)
    return in_maps


def assemble(results, NI, key="out"):
    B = 4
    outp = np.zeros((B, 2 * NI, 128, D), np.float32)
    for c in range(8):
        b, p = c // 2, c % 2
        y = np.asarray(results[c][key]).reshape(NI, 128, D)
        for i in range(NI):
            outp[b, own_global_tile(i, p)] = y[i]
    return outp.reshape(B, 2 * NI * 128, D)


_CACHE = {}


def kernel(**inputs):
    NI = 16
    if "nc" not in _CACHE:
        _CACHE["nc"] = build_program(NI)
    nc = _CACHE["nc"]
    in_maps = make_in_maps(inputs, NI)
    res = run_bass_kernel_spmd(nc, in_maps, core_ids=list(range(8)))
    return assemble(res.results, NI)
```

```python
import os
import numpy as np
import ml_dtypes
from contextlib import ExitStack
import concourse.bass as bass
import concourse.mybir as mybir
from concourse.bass_utils import run_bass_kernel_spmd

F32 = mybir.dt.float32
BF16 = mybir.dt.bfloat16
AF = mybir.ActivationFunctionType
ALU = mybir.AluOpType
AX = mybir.AxisListType

DBG = int(os.environ.get('KDBG', '0'))
D = 2048
KC = 16
EPS = 1e-5
N_META = 16


class Rec:
    __slots__ = ("q", "idx", "needed", "val", "sem", "dma")

    def __init__(self, q, idx, dma=False):
        self.q = q
        self.idx = idx
        self.needed = False
        self.val = None
        self.sem = None
        self.dma = dma


class Dep:
    __slots__ = ("w", "rc", "rd")

    def __init__(self):
        self.w = None
        self.rc = {}
        self.rd = []


def deps(n):
    return [Dep() for _ in range(n)]


QUEUES = ("pe", "act", "dve", "pool", "sp")
ENG = {"pe": "tensor", "act": "scalar", "dve": "vector", "pool": "gpsimd", "sp": "sync"}
N_DMA_SEMS = {"sp": 40, "pool": 8, "act": 8}


class Prog:
    def __init__(self, nc):
        self.nc = nc
        self.stack = ExitStack()
        self.scopes = [self.stack]
        self.streams = {q: [] for q in QUEUES}
        self.count = {q: 0 for q in QUEUES}
        self.last = {q: None for q in QUEUES}
        self.seen = {q: {} for q in QUEUES}
        self.seen_dma = {q: set() for q in QUEUES}
        self.sems = {q: self.stack.enter_context(nc.semaphore(f"s_{q}")) for q in QUEUES}
        self.dma_sems = {}
        self.dma_slot = {}
        self.dma_rr = {}
        for q, n in N_DMA_SEMS.items():
            self.dma_sems[q] = [self.stack.enter_context(nc.semaphore(f"d_{q}{i}")) for i in range(n)]
            self.dma_slot[q] = [None] * n
            self.dma_rr[q] = 0
        self.customs = []
        self.cc_sem = None
        self.cc_dep = None
        self.n_alloc = 0

    def scope(self):
        prog = self

        class _S:
            def __enter__(s):
                s.st = ExitStack()
                prog.scopes.append(s.st)

            def __exit__(s, *a):
                prog.barrier()
                prog.scopes.pop()
                s.st.close()
                return False

        return _S()

    def sbuf(self, shape, dtype, name=None):
        self.n_alloc += 1
        return self.scopes[-1].enter_context(self.nc.sbuf_tensor(f"sb{self.n_alloc}_{name or ''}", list(shape), dtype))

    def psum(self, shape, dtype, name=None):
        self.n_alloc += 1
        return self.scopes[-1].enter_context(self.nc.psum_tensor(f"ps{self.n_alloc}_{name or ''}", list(shape), dtype))

    def _need(self, q, rec):
        if rec is None:
            return
        if rec.dma:
            if id(rec) in self.seen_dma[q]:
                return
            self.seen_dma[q].add(id(rec))
            self.streams[q].append(("wait", rec))
            return
        if rec.q == q and q == "pe":
            return
        if self.seen[q].get(rec.q, -1) >= rec.idx:
            return
        self.seen[q][rec.q] = rec.idx
        rec.needed = True
        self.streams[q].append(("wait", rec))

    def _deps(self, q, reads, writes):
        best = {}
        dmas = []

        def add(rec):
            if rec is None:
                return
            if rec.dma:
                dmas.append(rec)
            else:
                b = best.get(rec.q)
                if b is None or b.idx < rec.idx:
                    best[rec.q] = rec

        for d in reads:
            add(d.w)
        for d in writes:
            add(d.w)
            for r in d.rc.values():
                add(r)
            for r in d.rd:
                add(r)
        for r in dmas:
            self._need(q, r)
        for r in best.values():
            self._need(q, r)

    def _commit(self, rec, reads, writes):
        for d in reads:
            if rec.dma:
                d.rd.append(rec)
            else:
                d.rc[rec.q] = rec
        for d in writes:
            d.w = rec
            d.rc = {}
            d.rd = []

    def op(self, q, fn, reads=(), writes=()):
        self._deps(q, reads, writes)
        rec = Rec(q, self.count[q])
        self.count[q] += 1
        self.last[q] = rec
        self.streams[q].append(("op", fn, rec))
        self._commit(rec, reads, writes)
        return rec

    def dma(self, q, out, in_, reads=(), writes=(), **kw):
        self._deps(q, reads, writes)
        k = self.dma_rr[q]
        self.dma_rr[q] = (k + 1) % len(self.dma_sems[q])
        prev = self.dma_slot[q][k]
        if prev is not None:
            self._need(q, prev)
        rec = Rec(q, -1, dma=True)
        rec.sem = self.dma_sems[q][k]
        rec.val = (prev.val if prev is not None else 0) + 16
        self.dma_slot[q][k] = rec
        self.streams[q].append(("dma", (out, in_, kw), rec))
        self._commit(rec, reads, writes)
        return rec

    def custom(self, q, fn, inc, reads=(), writes=()):
        self._deps(q, reads, writes)
        if self.customs:
            self._need(q, self.customs[-1])
        rec = Rec(q, -1, dma=True)
        if self.cc_sem is None:
            self.cc_sem = self.stack.enter_context(self.nc.semaphore("cc_sem"))
            self.cc_dep = Dep()
        rec.sem = self.cc_sem
        rec.val = (self.customs[-1].val if self.customs else 0) + inc
        rec.idx = inc
        self.customs.append(rec)
        self.streams[q].append(("custom", fn, rec))
        self._commit(rec, reads, writes)
        return rec

    def wait(self, q, rec):
        self._need(q, rec)

    def barrier(self):
        recs = [self.last[q] for q in QUEUES if self.last[q] is not None]
        for q in self.dma_slot:
            recs += [r for r in self.dma_slot[q] if r is not None]
        recs += self.customs[-1:]
        for q in QUEUES:
            for r in recs:
                self._need(q, r)

    def finish(self):
        nc = self.nc
        for q in QUEUES:
            c = 0
            for ent in self.streams[q]:
                if ent[0] == "op" and ent[2].needed:
                    c += 1
                    ent[2].val = c
                    ent[2].sem = self.sems[q]

        def run(q, e):
            for ent in self.streams[q]:
                kind = ent[0]
                if kind == "wait":
                    e.wait_ge(ent[1].sem, ent[1].val)
                elif kind == "op":
                    ins = ent[1](e)
                    if ent[2].needed:
                        ins.then_inc(ent[2].sem, 1)
                elif kind == "dma":
                    out, in_, kw = ent[1]
                    e.dma_start(out=out, in_=in_, **kw).then_inc(ent[2].sem, 16)
                elif kind == "custom":
                    ent[1](e).then_inc(ent[2].sem, ent[2].idx)

        with nc.Block() as block:
            for q in QUEUES:
                if self.streams[q]:
                    getattr(block, ENG[q])(lambda e, q=q: run(q, e))
        self.stack.close()


A_HEADS = 8
A_HD = 128
B_HD = 64
B_QH = 32
B_KVH = 4
ROPE_THETA = 500000.0
PAIRS = [[0, 1], [2, 3], [4, 5], [6, 7]]


def slot_cols(s):
    return (0, N_META) if s == 0 else (N_META + 128 * (s - 1), 128)


def own_global_tile(i, p):
    return 4 * (i // 2) + 2 * p + (i % 2)


def global_to_ridx(g):
    return (g // 2) % 2, 2 * (g // 4) + (g % 2)


def rope_tab(pos, rot):
    inv = (np.float32(ROPE_THETA) ** (-np.arange(0, rot, 2, dtype=np.float32) / np.float32(rot))).astype(np.float32)
    ang = pos.astype(np.float32)[:, None] * inv[None, :]
    return np.cos(ang).astype(np.float32), np.sin(ang).astype(np.float32)


def host_tables(NI, p):
    NSL = NI + 1
    ropeA = np.zeros((128, NSL, 2, 64), np.float32)
    ropeB = np.zeros((128, NSL, 2, 64), np.float32)
    for s in range(NSL):
        if s == 0:
            pos = np.arange(N_META)
        else:
            pos = N_META + 128 * own_global_tile(s - 1, p) + np.arange(128)
        n = len(pos)
        c, sn = rope_tab(pos, 32)
        ropeA[:n, s, 0] = np.tile(c, (1, 4))
        ropeA[:n, s, 1] = np.tile(sn, (1, 4))
        c, sn = rope_tab(pos, 16)
        ropeB[:n, s, 0] = np.tile(c, (1, 8))
        ropeB[:n, s, 1] = np.tile(sn, (1, 8))
    mA = np.zeros((128, 4, 2, 2, 128), np.float32)
    diag = np.ones((128, 128), np.float32)
    diag[64:, :64] = 0.0
    for j in range(4):
        for t in range(2):
            gq = 2 * p + t
            if j < gq:
                m = np.ones((128, 128), np.float32)
            elif j == gq:
                m = diag
            else:
                m = np.zeros((128, 128), np.float32)
            mA[:, j, :, t, :] = m[:, None, :]
    mA = mA.reshape(128, 4, 512)
    prev = np.ones((128, 128), np.float32)
    prev[:64, 64:] = 0.0
    mB = np.zeros((128, 4, 4, 128), np.float32)
    mB[:, 0] = (prev * (1.0 if p == 0 else 0.0))[:, None, :]
    mB[:, 1] = (prev * (1.0 if p == 1 else 0.0))[:, None, :]
    mB[:, 2] = prev[:, None, :]
    mB[:, 3] = diag[:, None, :]
    mB = mB.reshape(128, 4, 512)
    return ropeA, ropeB, mA.astype(ml_dtypes.bfloat16), mB.astype(ml_dtypes.bfloat16)


def build_program(NI=16, n_a=2, n_b=2, debug_x=False, stop_after=None, proj_cbs=None):
    NSL = NI + 1
    NG = NI // 2
    TOK = N_META + 128 * NI
    nc = bass.Bass("TRN2", target_bir_lowering=False)

    def din(name, shape, dt=F32):
        return nc.dram_tensor(name, list(shape), dt, kind="ExternalInput").ap()

    x_in = din("x", [NI * 128, D])
    meta_in = din("meta", [N_META, D])
    acbs = list(proj_cbs) if proj_cbs is not None else list(range(16))
    a_w_in = din("a_w_in", [max(n_a, 1), D, 512 * len(acbs)])
    a_w_out = din("a_w_out", [max(n_a, 1), D, D])
    w_kv = din("w_kv", [D, 512])
    b_w_in = din("b_w_in", [max(n_b, 1), D, 2 * D if n_b else 512])
    b_w_out = din("b_w_out", [max(n_b, 1), D, D if n_b else 512])
    gn_in = din("gn", [128, 5 * KC])
    fnorm_in = din("fnorm", [D])
    lamv_in = din("lamv", [2 * 4 * 128])
    subln_in = din("subln", [2 * 256])
    sinks_in = din("sinks", [2 * 32])
    ropeA_in = din("ropeA", [128, NSL * 2 * 64])
    ropeB_in = din("ropeB", [128, NSL * 2 * 64])
    maskA_in = din("maskA", [128, 4 * 512], BF16)
    maskB_in = din("maskB", [128, 4 * 512], BF16)
    out = nc.dram_tensor("out", [NI * 128, D], F32, kind="ExternalOutput").ap()
    if debug_x:
        dbg = nc.dram_tensor("dbg", [NSL * 128, D], F32, kind="ExternalOutput").ap()

    xres = nc.dram_tensor("xres", [NSL * 128, D], F32).ap()
    qT_loc = nc.dram_tensor("qT_loc", [8 * 128, 2 * TOK], BF16).ap()
    kT_loc = nc.dram_tensor("kT_loc", [8 * NSL * 128, 256], BF16)
    v_loc = nc.dram_tensor("v_loc", [8 * NSL * 128, 256], BF16)
    kT_all = nc.dram_tensor("kT_all", [2 * 8 * NSL * 128, 256], BF16)
    v_all = nc.dram_tensor("v_all", [2 * 8 * NSL * 128, 256], BF16)
    gate_s = nc.dram_tensor("gate_s", [NSL * 128, D], BF16).ap()
    kb_loc = nc.dram_tensor("kb_loc", [4 * NSL * 128, 128], BF16)
    vb_loc = nc.dram_tensor("vb_loc", [NSL * 128, 4 * 64], BF16)
    kb_all = nc.dram_tensor("kb_all", [2 * 4 * NSL * 128, 128], BF16)
    vb_all = nc.dram_tensor("vb_all", [2 * NSL * 128, 4 * 64], BF16)
    qb_loc = nc.dram_tensor("qb_loc", [16 * 128, TOK], BF16).ap()

    P = Prog(nc)
    NOCC = int(os.environ.get("KNOCC", "0"))
    CCCH = int(os.environ.get("KCCCH", "1"))

    def gather(loc, allt, reads, d_outs, only=None):
        la = loc.ap().bitcast(F32)
        aa = allt.ap().bitcast(F32)
        nch = len(d_outs)
        rows = la.shape[0] // nch
        for ch in (range(nch) if only is None else [only]):
            src = la[ch * rows:(ch + 1) * rows, :]
            dst = aa[2 * ch * rows:2 * (ch + 1) * rows, :]
            rd = reads[ch]
            if NOCC:
                P.dma("sp", dst[0:rows, :], src, reads=rd, writes=[d_outs[ch]])
                P.dma("sp", dst[rows:2 * rows, :], src, reads=rd, writes=[d_outs[ch]])
            else:
                P.custom("pool", lambda e, src=src, dst=dst: e.collective_compute(
                    "AllGather", ALU.bypass, replica_groups=PAIRS, ins=[src.opt()], outs=[dst.opt()]), 1,
                         reads=rd, writes=[d_outs[ch]])
    ident = P.sbuf([128, 128], BF16, "ident")
    gn = P.sbuf([128, 5 * KC], F32, "gn")
    ropeA = P.sbuf([128, NSL, 2, 64], F32, "ropeA")
    ropeB = P.sbuf([128, NSL, 2, 64], F32, "ropeB")
    maskA = P.sbuf([128, 4, 512], BF16, "maskA")
    maskB = P.sbuf([128, 4, 512], BF16, "maskB")
    lamv = P.sbuf([128, 2, 4, 128], F32, "lamv")
    subln = P.sbuf([128, 2, 256], F32, "subln")
    sinks = P.sbuf([128, 2, 32], F32, "sinks")
    nlam = P.sbuf([128, 2], F32, "nlam")
    epsT = P.sbuf([128, 1], F32, "epsT")
    hT = P.sbuf([128, KC, TOK], BF16, "hT")
    d_const = Dep()
    d_x = deps(NSL)
    d_hT = deps(NSL)

    with P.scope():
        idf = P.sbuf([128, 128], F32)
        lt = P.sbuf([128, 2, 2, 128], F32)
        ls = P.sbuf([128, 2, 2], F32)
        d_i = Dep()
        P.op("pool", lambda e: e.memset(idf[:], 1.0), writes=[d_i])
        P.op("pool", lambda e: e.affine_select(out=idf[:], in_=idf[:], pattern=[[-1, 128]], compare_op=ALU.is_equal,
                                               fill=0.0, base=0, channel_multiplier=1), reads=[d_i], writes=[d_i])
        P.op("dve", lambda e: e.tensor_copy(out=ident[:], in_=idf[:]), reads=[d_i], writes=[d_const])
        P.dma("sp", gn[:], gn_in, writes=[d_const])
        P.dma("sp", ropeA[:].rearrange("p a b c -> p (a b c)"), ropeA_in, writes=[d_const])
        P.dma("sp", ropeB[:].rearrange("p a b c -> p (a b c)"), ropeB_in, writes=[d_const])
        P.dma("sp", maskA[:].rearrange("p a b -> p (a b)"), maskA_in, writes=[d_const])
        P.dma("sp", maskB[:].rearrange("p a b -> p (a b)"), maskB_in, writes=[d_const])
        P.dma("sp", lamv[:].rearrange("p a b c -> p (a b c)"), lamv_in.partition_broadcast(128), writes=[d_const])
        P.dma("sp", subln[:].rearrange("p a b -> p (a b)"), subln_in.partition_broadcast(128), writes=[d_const])
        P.dma("sp", sinks[:].rearrange("p a b -> p (a b)"), sinks_in.partition_broadcast(128), writes=[d_const])
        zt = P.sbuf([128, 256], BF16)
        d_z = Dep()
        P.op("pool", lambda e: e.memset(zt[:], 0.0), writes=[d_z])
        kz = kT_loc.ap().rearrange("(h s d) c -> h s d c", h=8, s=NSL)
        vz = v_loc.ap().rearrange("(h s t) c -> h s t c", h=8, s=NSL)
        for hp in range(8):
            P.dma("sp", kz[hp, 0], zt[:], reads=[d_z])
            P.dma("sp", vz[hp, 0], zt[:], reads=[d_z])
        kbz = kb_loc.ap().rearrange("(c s d) t -> c s d t", c=4, s=NSL)
        for c in range(4):
            P.dma("sp", kbz[c, 0], zt[:, 0:128], reads=[d_z])
        P.dma("sp", vb_loc.ap()[0:128, :], zt[:], reads=[d_z])
        P.dma("sp", xres[0:N_META, :], meta_in, writes=[d_x[0]])
        for s in range(1, NSL):
            P.dma("sp", xres[s * 128:(s + 1) * 128, :], x_in[(s - 1) * 128:s * 128, :], writes=[d_x[s]])
        for l in range(2):
            for j in range(2):
                P.op("dve", lambda e, l=l, j=j: e.tensor_tensor(out=lt[:, l, j, :], in0=lamv[:, l, 2 * j, :],
                                                                 in1=lamv[:, l, 2 * j + 1, :], op=ALU.mult),
                     reads=[d_const], writes=[d_i])
                P.op("dve", lambda e, l=l, j=j: e.reduce_sum(out=ls[:, l, j:j + 1], in_=lt[:, l, j, :], axis=AX.X),
                     reads=[d_i], writes=[d_i])
        P.op("act", lambda e: e.activation(out=ls[:].rearrange("p a b -> p (a b)"), in_=ls[:].rearrange("p a b -> p (a b)"),
                                           func=AF.Exp), reads=[d_i], writes=[d_i])
        for l in range(2):
            lam_init = 0.8 - 0.6 * float(np.exp(-0.3 * l))
            P.op("dve", lambda e, l=l: e.tensor_tensor(out=nlam[:, l:l + 1], in0=ls[:, l, 1:2], in1=ls[:, l, 0:1],
                                                       op=ALU.subtract), reads=[d_i], writes=[d_const])
            P.op("dve", lambda e, l=l, li=lam_init: e.tensor_scalar(out=nlam[:, l:l + 1], in0=nlam[:, l:l + 1],
                                                                    scalar1=-li, scalar2=None, op0=ALU.add),
                 reads=[d_const], writes=[d_const])
        for l in range(2):
            lam_init = 0.8 - 0.6 * float(np.exp(-0.3 * l))
            P.op("dve", lambda e, l=l, li=lam_init: e.tensor_scalar(out=subln[:, l, :], in0=subln[:, l, :], scalar1=1.0 - li,
                                                                    scalar2=None, op0=ALU.mult),
                 reads=[d_const], writes=[d_const])
        P.op("dve", lambda e: e.memset(epsT[:], EPS), writes=[d_const])
        P.op("act", lambda e: e.activation(out=sinks[:].rearrange("p a b -> p (a b)"),
                                           in_=sinks[:].rearrange("p a b -> p (a b)"), func=AF.Exp),
             reads=[d_const], writes=[d_const])

    def phase_norm():
        with P.scope():
            xt = [P.sbuf([128, D], F32) for _ in range(2)]
            xn = [P.sbuf([128, D], BF16) for _ in range(2)]
            junk = P.sbuf([128, D], BF16)
            ss = [P.sbuf([128, 1], F32) for _ in range(2)]
            pt = [P.psum([128, 4, 128], BF16) for _ in range(3)]
            d_xt, d_xn, d_ss = deps(2), deps(2), deps(2)
            d_junk = Dep()
            d_pt = deps(3)
            n = 0
            for s in range(NSL):
                st, rows = slot_cols(s)
                b = s % 2
                P.dma("sp", xt[b][:rows], xres[s * 128:s * 128 + rows, :], reads=[d_x[s]], writes=[d_xt[b]])
                P.op("dve", lambda e, b=b: e.memset(ss[b][:], 0.0), writes=[d_ss[b]])
                P.op("act", lambda e, b=b, rows=rows: e.activation(out=junk[:rows], in_=xt[b][:rows], func=AF.Square,
                                                                  accum_out=ss[b][:rows]),
                     reads=[d_xt[b]], writes=[d_junk, d_ss[b]])
                P.op("act", lambda e, b=b, rows=rows: e.activation(out=ss[b][:rows], in_=ss[b][:rows], func=AF.Ln,
                                                                  bias=epsT[:rows], scale=1.0 / D),
                     reads=[d_ss[b], d_const], writes=[d_ss[b]])
                P.op("act", lambda e, b=b, rows=rows: e.activation(out=ss[b][:rows], in_=ss[b][:rows], func=AF.Exp,
                                                                  scale=-0.5),
                     reads=[d_ss[b]], writes=[d_ss[b]])
                P.op("dve", lambda e, b=b, rows=rows: e.tensor_scalar(out=xn[b][:rows], in0=xt[b][:rows],
                                                                     scalar1=ss[b][:rows, 0:1], scalar2=None, op0=ALU.mult),
                     reads=[d_xt[b], d_ss[b]], writes=[d_xn[b]])
                for k4 in range(4):
                    j = n % 3
                    n += 1
                    for k in range(4):
                        kc = k4 * 4 + k
                        P.op("pe", lambda e, j=j, k=k, kc=kc, b=b, rows=rows: e.transpose(
                            out=pt[j][:, k, :rows], in_=xn[b][:rows, kc * 128:(kc + 1) * 128], identity=ident[:rows, :rows]),
                             reads=[d_xn[b], d_const], writes=[d_pt[j]])
                    q = "act" if k4 % 2 else "dve"
                    if q == "act":
                        P.op(q, lambda e, j=j, k4=k4, st=st, rows=rows: e.activation(
                            out=hT[:, k4 * 4:k4 * 4 + 4, st:st + rows], in_=pt[j][:, :, :rows], func=AF.Copy),
                             reads=[d_pt[j]], writes=[d_hT[s]])
                    else:
                        P.op(q, lambda e, j=j, k4=k4, st=st, rows=rows: e.tensor_copy(
                            out=hT[:, k4 * 4:k4 * 4 + 4, st:st + rows], in_=pt[j][:, :, :rows]),
                             reads=[d_pt[j]], writes=[d_hT[s]])

    def project(w_ap, ncols, gcol, consume, flush, cbmap=None):
        ncb = ncols // 512
        with P.scope():
            wf = [P.sbuf([128, 8, 512], F32) for _ in range(2)]
            wb = [P.sbuf([128, KC, 512], BF16) for _ in range(2)]
            acc = [P.psum([128, 512], F32) for _ in range(4)]
            d_wf = deps(2)
            d_wb = [deps(KC) for _ in range(2)]
            d_acc = deps(4)
            cast_jobs = []

            def issue_load(cb):
                for half in range(2):
                    f = (2 * cb + half) % 2
                    P.dma("sp", wf[f][:], w_ap[half * 1024:(half + 1) * 1024, cb * 512:(cb + 1) * 512]
                          .rearrange("(kc p) n -> p kc n", p=128), writes=[d_wf[f]])
                    for k in range(8):
                        cast_jobs.append((cb, half, k))

            def do_casts(nmax):
                for _ in range(min(nmax, len(cast_jobs))):
                    cb, half, k = cast_jobs.pop(0)
                    f = (2 * cb + half) % 2
                    kc = half * 8 + k
                    q = "pool" if k % 2 else "dve"
                    wbuf = wb[cb % 2]
                    if gcol is not None:
                        P.op(q, lambda e, wbuf=wbuf, f=f, k=k, kc=kc: e.tensor_scalar(
                            out=wbuf[:, kc, :], in0=wf[f][:, k, :], scalar1=gn[:, gcol * KC + kc:gcol * KC + kc + 1],
                            scalar2=None, op0=ALU.mult), reads=[d_wf[f], d_const], writes=[d_wb[cb % 2][kc]])
                    else:
                        P.op(q, lambda e, wbuf=wbuf, f=f, k=k, kc=kc: e.tensor_copy(out=wbuf[:, kc, :], in_=wf[f][:, k, :]),
                             reads=[d_wf[f]], writes=[d_wb[cb % 2][kc]])

            issue_load(0)
            do_casts(16)
            n = 0
            for cb in range(ncb):
                if cb + 1 < ncb:
                    issue_load(cb + 1)
                for s in range(NSL):
                    st, rows = slot_cols(s)
                    a = n % 4
                    n += 1
                    for kc in range(KC):
                        P.op("pe", lambda e, a=a, kc=kc, st=st, rows=rows, cb=cb: e.matmul(
                            acc[a][:rows, :], lhsT=hT[:, kc, st:st + rows], rhs=wb[cb % 2][:, kc, :],
                            start=(kc == 0), stop=(kc == KC - 1)),
                             reads=[d_hT[s], d_wb[cb % 2][kc]], writes=[d_acc[a]])
                    consume(cbmap[cb] if cbmap else cb, s, acc[a], rows, d_acc[a])
                    if s >= 2:
                        do_casts(2)
                do_casts(16)
            flush()

    def out_proj(w_ap, d_og):
        for s in range(NSL):
            d_hT[s] = d_og[s]
        with P.scope():
            xr = [P.sbuf([128, 512], F32) for _ in range(3)]
            xo = [P.sbuf([128, 512], F32) for _ in range(3)]
            d_xr, d_xo = deps(3), deps(3)
            d_xs = [[Dep() for _ in range(4)] for _ in range(NSL)]
            cnt = {"n": 0}

            def consume(cb, s, a, rows, d_a):
                j = cnt["n"] % 3
                cnt["n"] += 1
                P.dma("sp", xr[j][:rows], xres[s * 128:s * 128 + rows, cb * 512:(cb + 1) * 512], reads=[d_x[s]],
                      writes=[d_xr[j]])
                P.op("dve", lambda e: e.tensor_tensor(out=xo[j][:rows], in0=a[:rows, :], in1=xr[j][:rows], op=ALU.add),
                     reads=[d_a, d_xr[j]], writes=[d_xo[j]])
                P.dma("sp", xres[s * 128:s * 128 + rows, cb * 512:(cb + 1) * 512], xo[j][:rows], reads=[d_xo[j]],
                      writes=[d_xs[s][cb]])

            project(w_ap, D, None, consume, lambda: None)

    def a_layer(l):
        phase_norm()
        if stop_after == "norm":
            return
        d_q = [[Dep() for _ in range(NSL)] for _ in range(8)]
        d_k = [[Dep() for _ in range(NSL)] for _ in range(8)]
        d_v = [[Dep() for _ in range(NSL)] for _ in range(8)]
        d_g = [[Dep() for _ in range(4)] for _ in range(NSL)]
        qT4 = qT_loc.rearrange("(h d) (e t) -> h d e t", d=128, e=2)
        kT4 = kT_loc.ap().rearrange("(h s d) (e t) -> h s d e t", h=8, s=NSL, e=2)
        v4 = v_loc.ap().rearrange("(h s t) c -> h s t c", h=8, s=NSL)

        with P.scope():
            tmp = [P.sbuf([128, 4, 128], BF16) for _ in range(3)]
            T = [P.sbuf([128, 4, 4, 16], F32) for _ in range(2)]
            stg = [P.sbuf([128, 4, 128], BF16) for _ in range(3)]
            vt = [P.sbuf([128, 512], BF16) for _ in range(3)]
            tp = [P.psum([128, 4, 128], BF16) for _ in range(2)]
            d_tmp = [deps(3) for _ in range(3)]
            d_T = deps(2)
            d_stg = deps(3)
            d_vt = deps(3)
            d_tp = deps(2)
            cnt = {"r": 0, "t": 0, "v": 0}
            pending = []

            def emit_transposes(item):
                cb, s, j, rows = item
                st, _ = slot_cols(s)
                k = cnt["t"] % 2
                g = cnt["t"] % 3
                cnt["t"] += 1
                for h in range(4):
                    P.op("pe", lambda e, k=k, h=h, j=j, rows=rows: e.transpose(
                        out=tp[k][:, h, :rows], in_=tmp[j][:rows, h, :], identity=ident[:rows, :rows]),
                         reads=d_tmp[j] + [d_const], writes=[d_tp[k]])
                P.op("act", lambda e, k=k, g=g, rows=rows: e.activation(out=stg[g][:, :, :rows], in_=tp[k][:, :, :rows],
                                                                       func=AF.Copy),
                     reads=[d_tp[k]], writes=[d_stg[g]])
                isq = cb < 4
                c4 = cb % 4
                if DBG == 3:
                    return
                for hh in range(2):
                    hp = 2 * c4 + hh
                    if isq:
                        P.dma("sp", qT4[hp, :, :, st:st + rows], stg[g][:, 2 * hh:2 * hh + 2, :rows],
                              reads=[d_stg[g]], writes=[d_q[hp][s]])
                    else:
                        P.dma("sp", kT4[hp, s, :, :, 0:rows], stg[g][:, 2 * hh:2 * hh + 2, :rows],
                              reads=[d_stg[g]], writes=[d_k[hp][s]])

            def consume(cb, s, a, rows, d_a):
                kind = cb // 4
                if DBG == 1:
                    return
                if kind < 2:
                    j = cnt["r"] % 3
                    tt = cnt["r"] % 2
                    cnt["r"] += 1
                    a3 = a[:rows, :].rearrange("p (h d) -> p h d", h=4)
                    cs = ropeA[:rows, s, 0, :].rearrange("p (h i) -> p h i", h=4)
                    sn = ropeA[:rows, s, 1, :].rearrange("p (h i) -> p h i", h=4)
                    Tt = T[tt]
                    t3 = tmp[j]
                    P.op("dve", lambda e: e.tensor_tensor(out=Tt[:rows, 0], in0=a3[:, :, 0:16], in1=cs, op=ALU.mult),
                         reads=[d_a, d_const], writes=[d_T[tt]])
                    P.op("dve", lambda e: e.tensor_tensor(out=Tt[:rows, 1], in0=a3[:, :, 16:32], in1=sn, op=ALU.mult),
                         reads=[d_a], writes=[d_T[tt]])
                    P.op("dve", lambda e: e.tensor_tensor(out=Tt[:rows, 2], in0=a3[:, :, 16:32], in1=cs, op=ALU.mult),
                         reads=[d_a], writes=[d_T[tt]])
                    P.op("dve", lambda e: e.tensor_tensor(out=Tt[:rows, 3], in0=a3[:, :, 0:16], in1=sn, op=ALU.mult),
                         reads=[d_a], writes=[d_T[tt]])
                    if DBG == 4:
                        return
                    P.op("dve", lambda e: e.tensor_tensor(out=t3[:rows, :, 0:16], in0=Tt[:rows, 0], in1=Tt[:rows, 1],
                                                          op=ALU.subtract), reads=[d_T[tt]], writes=[d_tmp[j][0]])
                    P.op("dve", lambda e: e.tensor_tensor(out=t3[:rows, :, 16:32], in0=Tt[:rows, 2], in1=Tt[:rows, 3],
                                                          op=ALU.add), reads=[d_T[tt]], writes=[d_tmp[j][1]])
                    if DBG == 5:
                        return
                    P.op("dve", lambda e: e.tensor_copy(out=t3[:rows, :, 32:128], in_=a3[:, :, 32:128]),
                         reads=[d_a], writes=[d_tmp[j][2]])
                    if DBG == 2:
                        return
                    pending.append((cb, s, j, rows))
                    if len(pending) > 1:
                        emit_transposes(pending.pop(0))
                elif kind == 2:
                    j = cnt["v"] % 3
                    cnt["v"] += 1
                    c4 = cb % 4
                    P.op("act", lambda e: e.activation(out=vt[j][:rows], in_=a[:rows, :], func=AF.Copy),
                         reads=[d_a], writes=[d_vt[j]])
                    for hh in range(2):
                        hp = 2 * c4 + hh
                        P.dma("sp", v4[hp, s, 0:rows, :], vt[j][:rows, hh * 256:(hh + 1) * 256],
                              reads=[d_vt[j]], writes=[d_v[hp][s]])
                else:
                    j = cnt["v"] % 3
                    cnt["v"] += 1
                    c4 = cb % 4
                    P.op("act", lambda e: e.activation(out=vt[j][:rows], in_=a[:rows, :], func=AF.Silu),
                         reads=[d_a], writes=[d_vt[j]])
                    P.dma("sp", gate_s[s * 128:s * 128 + rows, c4 * 512:(c4 + 1) * 512], vt[j][:rows],
                          reads=[d_vt[j]], writes=[d_g[s][c4]])

            def flush():
                while pending:
                    emit_transposes(pending.pop(0))

            project(a_w_in[l], 512 * len(acbs), l, consume, flush, cbmap=acbs)

        if stop_after == "proj":
            return
        d_kall, d_vall = deps(8), deps(8)
        for hp in range(8):
            gather(kT_loc, kT_all, [d_k[h] for h in range(8)], d_kall, only=hp)
            gather(v_loc, v_all, [d_v[h] for h in range(8)], d_vall, only=hp)
        if stop_after == "gather":
            return

        kA = kT_all.ap().rearrange("(h r s d) c -> r h d s c", r=2, h=8, s=NSL)
        vA = v_all.ap().rearrange("(h r s t) c -> r h t s c", r=2, h=8, s=NSL)
        qT3 = qT_loc.rearrange("(h d) c -> h d c", d=128)
        SCALE = float(A_HD) ** -0.5
        lam_init = 0.8 - 0.6 * float(np.exp(-0.3 * l))
        d_og = deps(NSL)
        with P.scope():
            kbuf = [P.sbuf([128, 2 * NSL, 256], BF16) for _ in range(2)]
            vbuf = [P.sbuf([128, 2 * NSL, 264], BF16) for _ in range(2)]
            qbuf = [P.sbuf([128, 2 * TOK], BF16) for _ in range(2)]
            PT = [P.sbuf([128, 512], BF16) for _ in range(3)]
            gt = [P.sbuf([128, 256], BF16) for _ in range(2)]
            oa = [P.sbuf([128, 256], F32) for _ in range(2)]
            ob = [P.sbuf([128, 256], F32) for _ in range(2)]
            og = [P.sbuf([128, 256], BF16) for _ in range(2)]
            j32 = P.sbuf([128, 256], F32)
            sm = [P.sbuf([128, 8], F32) for _ in range(2)]
            S = [P.psum([128, 512], F32) for _ in range(2)]
            O = [[P.psum([128, 512], F32) for _ in range(2)] for _ in range(2)]
            tps = P.psum([128, 2, 128], BF16)
            d_kb, d_vb, d_qb = deps(2), deps(2), deps(2)
            d_PT, d_S = deps(3), deps(2)
            d_O = [deps(2) for _ in range(2)]
            d_gt, d_oa, d_ob, d_ogs, d_sm = deps(2), deps(2), deps(2), deps(2), deps(2)
            d_junk, d_tps = Dep(), Dep()
            for b in range(2):
                P.op("dve", lambda e, b=b: e.memset(vbuf[b][:, :, 256:257], 1.0), writes=[d_vb[b]])

            def load_hp(hp):
                b = hp % 2
                for r in range(2):
                    P.dma("sp", kbuf[b][:, r * NSL:(r + 1) * NSL, :], kA[r, hp], reads=[d_kall[hp]], writes=[d_kb[b]])
                    P.dma("sp", vbuf[b][:, r * NSL:(r + 1) * NSL, 0:256], vA[r, hp], reads=[d_vall[hp]], writes=[d_vb[b]])
                P.dma("sp", qbuf[b][:], qT3[hp], reads=d_q[hp], writes=[d_qb[b]])

            load_hp(0)
            ns = 0
            ngr = 0
            deferred = []

            def flush_deferred():
                while deferred:
                    deferred.pop(0)()
            for hp in range(8):
                b = hp % 2
                if hp + 1 < 8:
                    load_hp(hp + 1)
                qb3 = qbuf[b][:].rearrange("p (e t) -> p e t", e=2)
                groups = [("meta", None)] + [("real", m) for m in range(NG)]
                for kind, m in groups:
                    if kind == "meta":
                        qc, nq = 0, N_META
                        qtiles = [(0, N_META, 0)]
                        kslots = [(0, N_META, None)]
                    else:
                        qc, nq = N_META + 256 * m, 256
                        qtiles = [(0, 128, 1 + 2 * m), (128, 128, 2 + 2 * m)]
                        kslots = [(0, N_META, None)]
                        for g in range(4 * m + 4):
                            r, i = global_to_ridx(g)
                            kslots.append((r * NSL + 1 + i, 128, (g - 4 * m) if g >= 4 * m else None))
                    nks = len(kslots)
                    slot_bufs = []

                    def emit_qk(si):
                        nonlocal ns
                        kidx, nk, mi = kslots[si]
                        sb = ns % 2
                        pb = ns % 3
                        ns += 1
                        slot_bufs.append(pb)
                        for e_ in range(2):
                            P.op("pe", lambda e, sb=sb, e_=e_, kidx=kidx, nk=nk, qc=qc, nq=nq, b=b, qb3=qb3: e.matmul(
                                S[sb][:nk, e_ * 256:e_ * 256 + nq], lhsT=kbuf[b][:, kidx, e_ * 128:e_ * 128 + nk],
                                rhs=qb3[:, e_, qc:qc + nq], start=True, stop=True),
                                 reads=[d_kb[b], d_qb[b]], writes=[d_S[sb]])
                        if nq == 256:
                            P.op("act", lambda e, sb=sb, pb=pb, nk=nk: e.activation(out=PT[pb][:nk, :], in_=S[sb][:nk, :],
                                                                                 func=AF.Exp, scale=SCALE),
                                 reads=[d_S[sb]], writes=[d_PT[pb]])
                        else:
                            P.op("act", lambda e, sb=sb, pb=pb, nk=nk, nq=nq: e.activation(
                                out=PT[pb][:nk, :].rearrange("p (e q) -> p e q", e=2)[:, :, 0:nq],
                                in_=S[sb][:nk, :].rearrange("p (e q) -> p e q", e=2)[:, :, 0:nq], func=AF.Exp, scale=SCALE),
                                 reads=[d_S[sb]], writes=[d_PT[pb]])
                        if mi is not None:
                            P.op("dve", lambda e, pb=pb, mi=mi: e.tensor_tensor(out=PT[pb][:, :], in0=PT[pb][:, :],
                                                                                in1=maskA[:, mi, :], op=ALU.mult),
                                 reads=[d_PT[pb], d_const], writes=[d_PT[pb]])

                    def emit_pv(si):
                        kidx, nk, mi = kslots[si]
                        pb = slot_bufs[si]
                        for e_ in range(2):
                            for ti, (qo, qr, _) in enumerate(qtiles):
                                P.op("pe", lambda e, e_=e_, ti=ti, qo=qo, qr=qr, pb=pb, nk=nk, kidx=kidx, b=b, si=si, nks=nks:
                                     e.matmul(O[e_][ti][:qr, 0:257], lhsT=PT[pb][:nk, e_ * 256 + qo:e_ * 256 + qo + qr],
                                              rhs=vbuf[b][:nk, kidx, 0:257], start=(si == 0), stop=(si == nks - 1)),
                                     reads=[d_PT[pb], d_vb[b]], writes=[d_O[e_][ti]])

                    for si in range(nks):
                        emit_qk(si)
                        if si >= 1:
                            emit_pv(si - 1)
                        if si == 1:
                            flush_deferred()
                    emit_pv(nks - 1)
                    if nks == 1:
                        flush_deferred()
                    for ti, (qo, qr, slot) in enumerate(qtiles):
                        st, rows = slot_cols(slot)
                        w = ngr % 2
                        ngr += 1
                        smw = sm[w]
                        P.dma("sp", gt[w][:rows], gate_s[slot * 128:slot * 128 + rows, hp * 256:(hp + 1) * 256],
                              reads=d_g[slot], writes=[d_gt[w]])
                        P.op("dve", lambda e, smw=smw, ti=ti, rows=rows: e.reciprocal(out=smw[:rows, 0:1],
                                                                                    in_=O[0][ti][:rows, 256:257]),
                             reads=[d_O[0][ti]], writes=[d_sm[w]])
                        P.op("dve", lambda e, smw=smw, ti=ti, rows=rows: e.reciprocal(out=smw[:rows, 1:2],
                                                                                    in_=O[1][ti][:rows, 256:257]),
                             reads=[d_O[1][ti]], writes=[d_sm[w]])
                        P.op("dve", lambda e, smw=smw, rows=rows: e.tensor_tensor(out=smw[:rows, 2:3], in0=smw[:rows, 1:2],
                                                                                 in1=nlam[:rows, l:l + 1], op=ALU.mult),
                             reads=[d_sm[w], d_const], writes=[d_sm[w]])
                        P.op("dve", lambda e, smw=smw, ti=ti, rows=rows, w=w: e.tensor_scalar(
                            out=oa[w][:rows], in0=O[0][ti][:rows, 0:256], scalar1=smw[:rows, 0:1], scalar2=None, op0=ALU.mult),
                             reads=[d_O[0][ti], d_sm[w]], writes=[d_oa[w]])
                        P.op("dve", lambda e, smw=smw, ti=ti, rows=rows, w=w: e.scalar_tensor_tensor(
                            out=ob[w][:rows], in0=O[1][ti][:rows, 0:256], scalar=smw[:rows, 2:3], in1=oa[w][:rows],
                            op0=ALU.mult, op1=ALU.add),
                             reads=[d_O[1][ti], d_sm[w], d_oa[w]], writes=[d_ob[w]])
                        P.op("dve", lambda e, rows=rows, w=w: e.tensor_tensor(out=j32[:rows], in0=ob[w][:rows], in1=ob[w][:rows],
                                                                            op=ALU.mult),
                             reads=[d_ob[w]], writes=[d_junk])
                        P.op("dve", lambda e, smw=smw, rows=rows: e.reduce_sum(out=smw[:rows, 3:4], in_=j32[:rows], axis=AX.X),
                             reads=[d_junk, d_sm[w]], writes=[d_sm[w]])
                        P.op("act", lambda e, smw=smw, rows=rows: e.activation(
                            out=smw[:rows, 4:5], in_=smw[:rows, 3:4], func=AF.Ln, bias=epsT[:rows], scale=1.0 / 256.0),
                             reads=[d_sm[w], d_const], writes=[d_sm[w]])
                        P.op("act", lambda e, smw=smw, rows=rows: e.activation(
                            out=smw[:rows, 5:6], in_=smw[:rows, 4:5], func=AF.Exp, scale=-0.5),
                             reads=[d_sm[w]], writes=[d_sm[w]])
                        P.op("dve", lambda e, smw=smw, rows=rows, w=w: e.scalar_tensor_tensor(
                            out=oa[w][:rows], in0=ob[w][:rows], scalar=smw[:rows, 5:6], in1=subln[:rows, l, :],
                            op0=ALU.mult, op1=ALU.mult),
                             reads=[d_ob[w], d_sm[w], d_const], writes=[d_oa[w]])
                        P.op("dve", lambda e, rows=rows, w=w: e.tensor_tensor(out=og[w][:rows], in0=oa[w][:rows],
                                                                            in1=gt[w][:rows], op=ALU.mult),
                             reads=[d_oa[w], d_gt[w]], writes=[d_ogs[w]])
                        def _tr(w=w, rows=rows, hp=hp, st=st, slot=slot):
                            for c in range(2):
                                P.op("pe", lambda e, c=c: e.transpose(out=tps[:, c, :rows],
                                                                      in_=og[w][:rows, c * 128:(c + 1) * 128],
                                                                      identity=ident[:rows, :rows]),
                                     reads=[d_ogs[w], d_const], writes=[d_tps])
                            P.op("dve", lambda e: e.tensor_copy(
                                out=hT[:, 2 * hp:2 * hp + 2, st:st + rows], in_=tps[:, :, :rows]),
                                 reads=[d_tps], writes=[d_og[slot]])
                        deferred.append(_tr)
            flush_deferred()

        if stop_after == "attn":
            return
        out_proj(a_w_out[l], d_og)

    kb4 = kb_loc.ap().rearrange("(c s d) t -> c s d t", c=4, s=NSL)
    kbA = kb_all.ap().rearrange("(c r s d) t -> r c s d t", r=2, c=4, s=NSL)
    vbA = vb_all.ap().rearrange("(r s t) c -> r s t c", r=2, s=NSL)
    qb3 = qb_loc.rearrange("(a d) t -> a d t", d=128)
    d_kball, d_vball = deps(4), deps(1)
    SCALE_B = float(B_HD) ** -0.5

    def rope_b(a3, s, rows, nh, T_t, d_T_t, out3, d_out, d_a):
        cs = ropeB[:rows, s, 0, 0:nh * 8].rearrange("p (h i) -> p h i", h=nh)
        sn = ropeB[:rows, s, 1, 0:nh * 8].rearrange("p (h i) -> p h i", h=nh)
        P.op("dve", lambda e: e.tensor_tensor(out=T_t[:rows, 0, 0:nh], in0=a3[:, :, 0:8], in1=cs, op=ALU.mult),
             reads=[d_a, d_const], writes=[d_T_t])
        P.op("dve", lambda e: e.tensor_tensor(out=T_t[:rows, 1, 0:nh], in0=a3[:, :, 8:16], in1=sn, op=ALU.mult),
             reads=[d_a], writes=[d_T_t])
        P.op("dve", lambda e: e.tensor_tensor(out=T_t[:rows, 2, 0:nh], in0=a3[:, :, 8:16], in1=cs, op=ALU.mult),
             reads=[d_a], writes=[d_T_t])
        P.op("dve", lambda e: e.tensor_tensor(out=T_t[:rows, 3, 0:nh], in0=a3[:, :, 0:8], in1=sn, op=ALU.mult),
             reads=[d_a], writes=[d_T_t])
        P.op("dve", lambda e: e.tensor_tensor(out=out3[:, :, 0:8], in0=T_t[:rows, 0, 0:nh], in1=T_t[:rows, 1, 0:nh],
                                              op=ALU.subtract), reads=[d_T_t], writes=[d_out[0]])
        P.op("dve", lambda e: e.tensor_tensor(out=out3[:, :, 8:16], in0=T_t[:rows, 2, 0:nh], in1=T_t[:rows, 3, 0:nh],
                                              op=ALU.add), reads=[d_T_t], writes=[d_out[1]])
        P.op("dve", lambda e: e.tensor_copy(out=out3[:, :, 16:64], in_=a3[:, :, 16:64]), reads=[d_a], writes=[d_out[2]])

    def shared_kv():
        d_kb = deps(NSL)
        d_vb = deps(NSL)
        with P.scope():
            tmpk = [P.sbuf([128, 4, 2, 64], BF16) for _ in range(2)]
            T = [P.sbuf([128, 4, 8, 8], F32) for _ in range(2)]
            stg = [P.sbuf([128, 4, 128], BF16) for _ in range(2)]
            vt = [P.sbuf([128, 256], BF16) for _ in range(2)]
            tp = [P.psum([128, 4, 128], BF16) for _ in range(2)]
            d_tmp = [deps(4) for _ in range(2)]
            d_T, d_stg, d_vt, d_tp = deps(2), deps(2), deps(2), deps(2)
            cnt = {"n": 0}

            def consume(cb, s, a, rows, d_a):
                j = cnt["n"] % 2
                cnt["n"] += 1
                st, _ = slot_cols(s)
                a3 = a[:rows, 0:256].rearrange("p (h d) -> p h d", h=4)
                rope_b(a3, s, rows, 4, T[j], d_T[j], tmpk[j][:rows, :, 0, :], d_tmp[j], d_a)
                P.op("dve", lambda e: e.tensor_copy(out=tmpk[j][:rows, :, 1, :], in_=tmpk[j][:rows, :, 0, :]),
                     reads=d_tmp[j][0:3], writes=[d_tmp[j][3]])
                P.op("dve", lambda e: e.tensor_copy(out=vt[j][:rows], in_=a[:rows, 256:512]),
                     reads=[d_a], writes=[d_vt[j]])
                P.dma("sp", vb_loc.ap()[s * 128:s * 128 + rows, :], vt[j][:rows], reads=[d_vt[j]], writes=[d_vb[s]])
                for c in range(4):
                    P.op("pe", lambda e, c=c: e.transpose(out=tp[j][:, c, :rows],
                                                          in_=tmpk[j][:rows, c].rearrange("p a b -> p (a b)"),
                                                          identity=ident[:rows, :rows]),
                         reads=d_tmp[j] + [d_const], writes=[d_tp[j]])
                P.op("act", lambda e: e.activation(out=stg[j][:, :, :rows], in_=tp[j][:, :, :rows], func=AF.Copy),
                     reads=[d_tp[j]], writes=[d_stg[j]])
                P.dma("sp", kb4[:, s, :, 0:rows].rearrange("c d t -> d c t"), stg[j][:, :, :rows], reads=[d_stg[j]],
                      writes=[d_kb[s]])

            project(w_kv, 512, 2, consume, lambda: None)
        if stop_after == "bkv":
            return d_kb, d_vb
        gather(kb_loc, kb_all, [d_kb] * 4, d_kball)
        gather(vb_loc, vb_all, [d_vb], d_vball)
        return d_kb, d_vb

    def b_layer(j, d_kb, d_vb):
        if stop_after in ("bkv", "bgather"):
            return
        d_q = [[Dep() for _ in range(NSL)] for _ in range(4)]
        d_g = [[Dep() for _ in range(4)] for _ in range(NSL)]
        with P.scope():
            tmp = [P.sbuf([128, 8, 64], BF16) for _ in range(3)]
            T = [P.sbuf([128, 4, 8, 8], F32) for _ in range(2)]
            stg = [P.sbuf([128, 4, 128], BF16) for _ in range(3)]
            vt = [P.sbuf([128, 512], BF16) for _ in range(3)]
            tp = [P.psum([128, 4, 128], BF16) for _ in range(2)]
            d_tmp = [deps(3) for _ in range(3)]
            d_T, d_stg, d_vt, d_tp = deps(2), deps(3), deps(3), deps(2)
            cnt = {"r": 0, "t": 0, "v": 0}
            pending = []

            def emit_transposes(item):
                cb, s, jj, rows = item
                st, _ = slot_cols(s)
                k = cnt["t"] % 2
                g = cnt["t"] % 3
                cnt["t"] += 1
                for h in range(4):
                    P.op("pe", lambda e, h=h: e.transpose(out=tp[k][:, h, :rows],
                                                          in_=tmp[jj][:rows, 2 * h:2 * h + 2, :].rearrange("p a b -> p (a b)"),
                                                          identity=ident[:rows, :rows]),
                         reads=d_tmp[jj] + [d_const], writes=[d_tp[k]])
                P.op("act", lambda e: e.activation(out=stg[g][:, :, :rows], in_=tp[k][:, :, :rows], func=AF.Copy),
                     reads=[d_tp[k]], writes=[d_stg[g]])
                P.dma("sp", qb3[4 * cb:4 * cb + 4, :, st:st + rows].rearrange("a d t -> d a t"), stg[g][:, :, :rows],
                      reads=[d_stg[g]], writes=[d_q[cb][s]])

            def consume(cb, s, a, rows, d_a):
                if cb < 4:
                    jj = cnt["r"] % 3
                    tt = cnt["r"] % 2
                    cnt["r"] += 1
                    a3 = a[:rows, :].rearrange("p (h d) -> p h d", h=8)
                    rope_b(a3, s, rows, 8, T[tt], d_T[tt], tmp[jj][:rows], d_tmp[jj], d_a)
                    pending.append((cb, s, jj, rows))
                    if len(pending) > 1:
                        emit_transposes(pending.pop(0))
                else:
                    jj = cnt["v"] % 3
                    cnt["v"] += 1
                    c4 = cb - 4
                    P.op("act", lambda e: e.activation(out=vt[jj][:rows], in_=a[:rows, :], func=AF.Silu),
                         reads=[d_a], writes=[d_vt[jj]])
                    P.dma("sp", gate_s[s * 128:s * 128 + rows, c4 * 512:(c4 + 1) * 512], vt[jj][:rows],
                          reads=[d_vt[jj]], writes=[d_g[s][c4]])

            def flush():
                while pending:
                    emit_transposes(pending.pop(0))

            project(b_w_in[j], 2 * D, 3 + j, consume, flush)
        if stop_after == "bproj":
            return

        d_og = deps(NSL)
        with P.scope():
            ko = P.sbuf([128, 4, NSL, 128], BF16)
            vo = P.sbuf([128, NSL, 4, 66], BF16)
            kc_ = P.sbuf([128, 4, 2, NG, 128], BF16)
            vc_ = P.sbuf([128, 2, NG, 4, 66], BF16)
            qbuf = [P.sbuf([128, 4, TOK], BF16) for _ in range(2)]
            PT = [P.sbuf([128, 512], BF16) for _ in range(4)]
            gt = [P.sbuf([128, 512], BF16) for _ in range(2)]
            ob = [P.sbuf([128, 512], F32) for _ in range(2)]
            og = [P.sbuf([128, 512], BF16) for _ in range(2)]
            sm = [P.sbuf([128, 2, 8], F32) for _ in range(2)]
            S = [P.psum([128, 512], F32) for _ in range(4)]
            O = [P.psum([128, 4, 128], F32) for _ in range(2)]
            tps = P.psum([128, 4, 128], BF16)
            d_ko, d_vo, d_kc, d_vc = Dep(), Dep(), Dep(), Dep()
            d_qb = deps(2)
            d_PT, d_S, d_O = deps(4), deps(4), deps(2)
            d_gt, d_ob, d_ogs, d_sm = deps(2), deps(2), deps(2), deps(2)
            d_tps = Dep()
            P.op("pool", lambda e: e.memset(vo[:, :, :, 64:65], 1.0), writes=[d_vo])
            P.op("pool", lambda e: e.memset(vc_[:, :, :, :, 64:65], 1.0), writes=[d_vc])
            for c in range(4):
                P.dma("sp", ko[:, c], kb4[c].rearrange("s d t -> d s t"), reads=d_kb, writes=[d_ko])
            for s_ in range(NSL):
                P.dma("sp", vo[:, s_, :, 0:64], vb_loc.ap()[s_ * 128:(s_ + 1) * 128, :].rearrange("t (c d) -> t c d", c=4),
                      reads=[d_vb[s_]], writes=[d_vo])
            for m in range(NG):
                for r in range(2):
                    idx = 2 * m - 1 if r == 1 else 2 * m + 1
                    if idx < 0:
                        continue
                    for c in range(4):
                        P.dma("sp", kc_[:, c, r, m, :], kbA[r, c, 1 + idx], reads=[d_kball[c]], writes=[d_kc])
                    P.dma("sp", vc_[:, r, m, :, 0:64], vbA[r, 1 + idx].rearrange("t (c d) -> t c d", c=4),
                          reads=d_vball, writes=[d_vc])

            def load_q(c):
                P.dma("sp", qbuf[c % 2][:], qb3[4 * c:4 * c + 4].rearrange("a d t -> d a t"), reads=d_q[c],
                      writes=[d_qb[c % 2]])

            load_q(0)
            ns = 0
            ne = 0
            deferred_b = []

            def flush_deferred_b():
                while deferred_b:
                    deferred_b.pop(0)()
            for c in range(4):
                qb_ = qbuf[c % 2]
                if c + 1 < 4:
                    load_q(c + 1)
                for s in range(NSL):
                    st, rows = slot_cols(s)
                    ks = [(ko[:, c, 0, 0:N_META], vo[0:N_META, 0, c, 0:65], N_META, None)]
                    if s >= 1:
                        i = s - 1
                        if i % 2 == 1:
                            ks.append((ko[:, c, s - 1, :], vo[:, s - 1, c, 0:65], 128, 2))
                        else:
                            m = i // 2
                            if m >= 1:
                                ks.append((kc_[:, c, 1, m, :], vc_[:, 1, m, c, 0:65], 128, 0))
                            ks.append((kc_[:, c, 0, m, :], vc_[:, 0, m, c, 0:65], 128, 1))
                        ks.append((ko[:, c, s, :], vo[:, s, c, 0:65], 128, 3))
                    nks = len(ks)
                    steps = [(si, half) for si in range(nks) for half in range(2)]
                    step_buf = []

                    def emit_qk_b(n):
                        nonlocal ns
                        si, half = steps[n]
                        kap, vap, nk, mi = ks[si]
                        sb = ns % 4
                        ns += 1
                        step_buf.append(sb)
                        lo, hi = half * 64, half * 64 + 64
                        P.op("pe", lambda e, sb=sb, kap=kap, nk=nk, lo=lo, hi=hi, qb_=qb_, st=st, rows=rows: e.matmul(
                            S[sb][:nk, 0:4 * rows], lhsT=kap[lo:hi, 0:nk],
                            rhs=qb_[lo:hi, :, st:st + rows], start=True, stop=True),
                             reads=[d_ko, d_kc, d_qb[c % 2]], writes=[d_S[sb]])
                        P.op("act", lambda e, sb=sb, nk=nk, rows=rows: e.activation(
                            out=PT[sb][:nk, 0:4 * rows], in_=S[sb][:nk, 0:4 * rows], func=AF.Exp, scale=SCALE_B),
                             reads=[d_S[sb]], writes=[d_PT[sb]])
                        if mi is not None:
                            mq = "pool" if half else "dve"
                            P.op(mq, lambda e, sb=sb, mi=mi: e.tensor_tensor(out=PT[sb][:, :], in0=PT[sb][:, :],
                                                                             in1=maskB[:, mi, :], op=ALU.mult),
                                 reads=[d_PT[sb], d_const], writes=[d_PT[sb]])

                    def emit_pv_b(n):
                        si, half = steps[n]
                        kap, vap, nk, mi = ks[si]
                        sb = step_buf[n]
                        for a in range(4):
                            P.op("pe", lambda e, a=a, sb=sb, nk=nk, rows=rows, vap=vap, half=half, si=si, nks=nks: e.matmul(
                                O[half][:rows, a, 0:65], lhsT=PT[sb][:nk, a * rows:(a + 1) * rows], rhs=vap[0:nk, :],
                                start=(si == 0 and a == 0), stop=(si == nks - 1 and a == 3)),
                                 reads=[d_PT[sb], d_vo, d_vc], writes=[d_O[half]])

                    for n in range(len(steps)):
                        emit_qk_b(n)
                        if n >= 1:
                            emit_pv_b(n - 1)
                        if n == 1:
                            flush_deferred_b()
                    emit_pv_b(len(steps) - 1)
                    w = ne % 2
                    ne += 1
                    smw = sm[w]
                    P.dma("sp", gt[w][:rows], gate_s[s * 128:s * 128 + rows, c * 512:(c + 1) * 512], reads=d_g[s],
                          writes=[d_gt[w]])
                    for half in range(2):
                        P.op("dve", lambda e, half=half, smw=smw, rows=rows: e.tensor_tensor(
                            out=smw[:rows, half, 0:4], in0=O[half][:rows, :, 64],
                            in1=sinks[:rows, j, half * 16 + 4 * c:half * 16 + 4 * c + 4], op=ALU.add),
                             reads=[d_O[half], d_const], writes=[d_sm[w]])
                        P.op("dve", lambda e, half=half, smw=smw, rows=rows: e.reciprocal(out=smw[:rows, half, 4:8],
                                                                                       in_=smw[:rows, half, 0:4]),
                             reads=[d_sm[w]], writes=[d_sm[w]])
                        for a in range(4):
                            P.op("dve", lambda e, half=half, a=a, smw=smw, rows=rows, w=w: e.tensor_scalar(
                                out=ob[w][:rows, (2 * a + half) * 64:(2 * a + half) * 64 + 64], in0=O[half][:rows, a, 0:64],
                                scalar1=smw[:rows, half, 4 + a:5 + a], scalar2=None, op0=ALU.mult),
                                 reads=[d_O[half], d_sm[w]], writes=[d_ob[w]])
                    P.op("pool", lambda e, rows=rows, w=w: e.tensor_tensor(out=og[w][:rows], in0=ob[w][:rows], in1=gt[w][:rows],
                                                                         op=ALU.mult),
                         reads=[d_ob[w], d_gt[w]], writes=[d_ogs[w]])
                    def _trb(w=w, rows=rows, c=c, st=st, s=s):
                        for k in range(4):
                            P.op("pe", lambda e, k=k: e.transpose(out=tps[:, k, :rows],
                                                                  in_=og[w][:rows, k * 128:(k + 1) * 128],
                                                                  identity=ident[:rows, :rows]),
                                 reads=[d_ogs[w], d_const], writes=[d_tps])
                        P.op("act", lambda e: e.activation(
                            out=hT[:, 4 * c:4 * c + 4, st:st + rows], in_=tps[:, :, :rows], func=AF.Copy),
                             reads=[d_tps], writes=[d_og[s]])
                    deferred_b.append(_trb)
            flush_deferred_b()
        if stop_after == "battn":
            return
        out_proj(b_w_out[j], d_og)

    def final_phase():
        with P.scope():
            fn = P.sbuf([128, D], F32)
            xt = [P.sbuf([128, D], F32) for _ in range(2)]
            xo = [P.sbuf([128, D], F32) for _ in range(2)]
            junk = P.sbuf([128, D], BF16)
            ss = [P.sbuf([128, 1], F32) for _ in range(2)]
            d_fn, d_junk = Dep(), Dep()
            d_xt, d_xo, d_ss = deps(2), deps(2), deps(2)
            P.dma("sp", fn[:], fnorm_in.partition_broadcast(128), writes=[d_fn])
            recs = []
            for s in range(1, NSL):
                b = s % 2
                P.dma("sp", xt[b][:], xres[s * 128:(s + 1) * 128, :], reads=[d_x[s]], writes=[d_xt[b]])
                P.op("dve", lambda e, b=b: e.memset(ss[b][:], 0.0), writes=[d_ss[b]])
                P.op("act", lambda e, b=b: e.activation(out=junk[:], in_=xt[b][:], func=AF.Square, accum_out=ss[b][:]),
                     reads=[d_xt[b]], writes=[d_junk, d_ss[b]])
                P.op("act", lambda e, b=b: e.activation(out=ss[b][:], in_=ss[b][:], func=AF.Ln, bias=epsT[:], scale=1.0 / D),
                     reads=[d_ss[b], d_const], writes=[d_ss[b]])
                P.op("act", lambda e, b=b: e.activation(out=ss[b][:], in_=ss[b][:], func=AF.Exp, scale=-0.5),
                     reads=[d_ss[b]], writes=[d_ss[b]])
                P.op("dve", lambda e, b=b: e.scalar_tensor_tensor(out=xo[b][:], in0=xt[b][:], scalar=ss[b][:, 0:1], in1=fn[:],
                                                                  op0=ALU.mult, op1=ALU.mult),
                     reads=[d_xt[b], d_ss[b], d_fn], writes=[d_xo[b]])
                recs.append(P.dma("sp", out[(s - 1) * 128:s * 128, :], xo[b][:], reads=[d_xo[b]]))
            for r in recs:
                P.wait("sp", r)

    if stop_after != "init":
        for l in range(n_a):
            a_layer(l)
        if n_b > 0:
            phase_norm()
            d_kb, d_vb = shared_kv()
            for j in range(n_b):
                if j > 0:
                    phase_norm()
                b_layer(j, d_kb, d_vb)

    with P.scope():
        if debug_x:
            for s in range(NSL):
                st, rows = slot_cols(s)
                r = P.dma("sp", dbg[s * 128:s * 128 + rows, :], xres[s * 128:s * 128 + rows, :], reads=[d_x[s]])
                P.wait("sp", r)
        pass
    if n_a == 2 and n_b == 2 and stop_after is None:
        final_phase()
    else:
        rs_ = [P.dma("sp", out[(s - 1) * 128:s * 128, :], xres[s * 128:(s + 1) * 128, :], reads=[d_x[s]])
               for s in range(1, NSL)]
        for r in rs_:
            P.wait("sp", r)
    P.finish()
    return nc


def make_in_maps(inputs, NI, n_a=2, n_b=2, proj_cbs=None):
    NSL = NI + 1
    f32 = np.float32
    x = np.asarray(inputs["x"], f32)
    B = x.shape[0]
    gn = np.zeros((128, 5 * KC), f32)
    gains = [inputs["a_norm"][0], inputs["a_norm"][1], inputs["kv_norm"], inputs["b_norm"][0], inputs["b_norm"][1]]
    for w, g in enumerate(gains):
        gn[:, w * KC:(w + 1) * KC] = np.asarray(g, f32).reshape(KC, 128).T
    lamv = np.stack([np.stack([np.asarray(inputs[k], f32)[l] for k in
                               ("a_lambda_q1", "a_lambda_k1", "a_lambda_q2", "a_lambda_k2")]) for l in range(2)]).reshape(-1)
    shared = {
        "meta": np.ascontiguousarray(inputs["meta_tokens"], f32),
        "a_w_in": np.ascontiguousarray(np.asarray(inputs["a_w_in"], f32)[:max(n_a, 1)] if proj_cbs is None else
                                       np.concatenate([np.asarray(inputs["a_w_in"], f32)[:max(n_a, 1), :, cb * 512:(cb + 1) * 512]
                                                       for cb in proj_cbs], axis=2)),
        "a_w_out": np.ascontiguousarray(np.asarray(inputs["a_w_out"], f32)[:max(n_a, 1)]),
        "w_kv": np.ascontiguousarray(inputs["w_kv"], f32),
        "b_w_in": np.ascontiguousarray(np.asarray(inputs["b_w_in"], f32)[:max(n_b, 1), :, :(2 * D if n_b else 512)]),
        "b_w_out": np.ascontiguousarray(np.asarray(inputs["b_w_out"], f32)[:max(n_b, 1), :, :(D if n_b else 512)]),
        "gn": gn,
        "fnorm": np.ascontiguousarray(inputs["final_norm"], f32),
        "lamv": np.ascontiguousarray(lamv, f32),
        "subln": np.ascontiguousarray(inputs["a_subln"], f32).reshape(-1),
        "sinks": np.ascontiguousarray(np.asarray(inputs["b_sinks"], f32).reshape(2, 4, 4, 2).transpose(0, 3, 1, 2)).reshape(-1),
    }
    tabs = [host_tables(NI, p) for p in range(2)]
    in_maps = []
    for c in range(8):
        b, p = c // 2, c % 2
        xt = x[b].reshape(2 * NI, 128, D)
        own = [own_global_tile(i, p) for i in range(NI)]
        m = dict(shared)
        m["x"] = np.ascontiguousarray(xt[own].reshape(NI * 128, D))
        ra, rb, ma, mb = tabs[p]
        m["ropeA"] = ra.reshape(128, -1)
        m["ropeB"] = rb.reshape(128, -1)
        m["maskA"] = ma.reshape(128, -1)
        m["maskB"] = mb.reshape(128, -1)
        in_maps.append(m)
    return in_maps


def assemble(results, NI, key="out"):
    B = 4
    outp = np.zeros((B, 2 * NI, 128, D), np.float32)
    for c in range(8):
        b, p = c // 2, c % 2
        y = np.asarray(results[c][key]).reshape(NI, 128, D)
        for i in range(NI):
            outp[b, own_global_tile(i, p)] = y[i]
    return outp.reshape(B, 2 * NI * 128, D)


_CACHE = {}


def kernel(**inputs):
    NI = 16
    if "nc" not in _CACHE:
        _CACHE["nc"] = build_program(NI)
    nc = _CACHE["nc"]
    in_maps = make_in_maps(inputs, NI)
    res = run_bass_kernel_spmd(nc, in_maps, core_ids=list(range(8)))
    return assemble(res.results, NI)
```

```python
import os
import numpy as np
import ml_dtypes
from contextlib import ExitStack
import concourse.bass as bass
import concourse.mybir as mybir
from concourse.bass_utils import run_bass_kernel_spmd

F32 = mybir.dt.float32
BF16 = mybir.dt.bfloat16
AF = mybir.ActivationFunctionType
ALU = mybir.AluOpType
AX = mybir.AxisListType

DBG = int(os.environ.get('KDBG', '0'))
D = 2048
KC = 16
EPS = 1e-5
N_META = 16


class Rec:
    __slots__ = ("q", "idx", "needed", "val", "sem", "dma")

    def __init__(self, q, idx, dma=False):
        self.q = q
        self.idx = idx
        self.needed = False
        self.val = None
        self.sem = None
        self.dma = dma


class Dep:
    __slots__ = ("w", "rc", "rd")

    def __init__(self):
        self.w = None
        self.rc = {}
        self.rd = []


def deps(n):
    return [Dep() for _ in range(n)]


QUEUES = ("pe", "act", "dve", "pool", "sp")
ENG = {"pe": "tensor", "act": "scalar", "dve": "vector", "pool": "gpsimd", "sp": "sync"}
N_DMA_SEMS = {"sp": 40, "pool": 8, "act": 8}


class Prog:
    def __init__(self, nc):
        self.nc = nc
        self.stack = ExitStack()
        self.scopes = [self.stack]
        self.streams = {q: [] for q in QUEUES}
        self.count = {q: 0 for q in QUEUES}
        self.last = {q: None for q in QUEUES}
        self.seen = {q: {} for q in QUEUES}
        self.seen_dma = {q: set() for q in QUEUES}
        self.sems = {q: self.stack.enter_context(nc.semaphore(f"s_{q}")) for q in QUEUES}
        self.dma_sems = {}
        self.dma_slot = {}
        self.dma_rr = {}
        for q, n in N_DMA_SEMS.items():
            self.dma_sems[q] = [self.stack.enter_context(nc.semaphore(f"d_{q}{i}")) for i in range(n)]
            self.dma_slot[q] = [None] * n
            self.dma_rr[q] = 0
        self.customs = []
        self.cc_sem = None
        self.cc_dep = None
        self.n_alloc = 0

    def scope(self):
        prog = self

        class _S:
            def __enter__(s):
                s.st = ExitStack()
                prog.scopes.append(s.st)

            def __exit__(s, *a):
                prog.barrier()
                prog.scopes.pop()
                s.st.close()
                return False

        return _S()

    def sbuf(self, shape, dtype, name=None):
        self.n_alloc += 1
        return self.scopes[-1].enter_context(self.nc.sbuf_tensor(f"sb{self.n_alloc}_{name or ''}", list(shape), dtype))

    def psum(self, shape, dtype, name=None):
        self.n_alloc += 1
        return self.scopes[-1].enter_context(self.nc.psum_tensor(f"ps{self.n_alloc}_{name or ''}", list(shape), dtype))

    def _need(self, q, rec):
        if rec is None:
            return
        if rec.dma:
            if id(rec) in self.seen_dma[q]:
                return
            self.seen_dma[q].add(id(rec))
            self.streams[q].append(("wait", rec))
            return
        if rec.q == q and q == "pe":
            return
        if self.seen[q].get(rec.q, -1) >= rec.idx:
            return
        self.seen[q][rec.q] = rec.idx
        rec.needed = True
        self.streams[q].append(("wait", rec))

    def _deps(self, q, reads, writes):
        best = {}
        dmas = []

        def add(rec):
            if rec is None:
                return
            if rec.dma:
                dmas.append(rec)
            else:
                b = best.get(rec.q)
                if b is None or b.idx < rec.idx:
                    best[rec.q] = rec

        for d in reads:
            add(d.w)
        for d in writes:
            add(d.w)
            for r in d.rc.values():
                add(r)
            for r in d.rd:
                add(r)
        for r in dmas:
            self._need(q, r)
        for r in best.values():
            self._need(q, r)

    def _commit(self, rec, reads, writes):
        for d in reads:
            if rec.dma:
                d.rd.append(rec)
            else:
                d.rc[rec.q] = rec
        for d in writes:
            d.w = rec
            d.rc = {}
            d.rd = []

    def op(self, q, fn, reads=(), writes=()):
        self._deps(q, reads, writes)
        rec = Rec(q, self.count[q])
        self.count[q] += 1
        self.last[q] = rec
        self.streams[q].append(("op", fn, rec))
        self._commit(rec, reads, writes)
        return rec

    def dma(self, q, out, in_, reads=(), writes=(), **kw):
        self._deps(q, reads, writes)
        k = self.dma_rr[q]
        self.dma_rr[q] = (k + 1) % len(self.dma_sems[q])
        prev = self.dma_slot[q][k]
        if prev is not None:
            self._need(q, prev)
        rec = Rec(q, -1, dma=True)
        rec.sem = self.dma_sems[q][k]
        rec.val = (prev.val if prev is not None else 0) + 16
        self.dma_slot[q][k] = rec
        self.streams[q].append(("dma", (out, in_, kw), rec))
        self._commit(rec, reads, writes)
        return rec

    def custom(self, q, fn, inc, reads=(), writes=()):
        self._deps(q, reads, writes)
        if self.customs:
            self._need(q, self.customs[-1])
        rec = Rec(q, -1, dma=True)
        if self.cc_sem is None:
            self.cc_sem = self.stack.enter_context(self.nc.semaphore("cc_sem"))
            self.cc_dep = Dep()
        rec.sem = self.cc_sem
        rec.val = (self.customs[-1].val if self.customs else 0) + inc
        rec.idx = inc
        self.customs.append(rec)
        self.streams[q].append(("custom", fn, rec))
        self._commit(rec, reads, writes)
        return rec

    def wait(self, q, rec):
        self._need(q, rec)

    def barrier(self):
        recs = [self.last[q] for q in QUEUES if self.last[q] is not None]
        for q in self.dma_slot:
            recs += [r for r in self.dma_slot[q] if r is not None]
        recs += self.customs[-1:]
        for q in QUEUES:
            for r in recs:
                self._need(q, r)

    def finish(self):
        nc = self.nc
        for q in QUEUES:
            c = 0
            for ent in self.streams[q]:
                if ent[0] == "op" and ent[2].needed:
                    c += 1
                    ent[2].val = c
                    ent[2].sem = self.sems[q]

        def run(q, e):
            for ent in self.streams[q]:
                kind = ent[0]
                if kind == "wait":
                    e.wait_ge(ent[1].sem, ent[1].val)
                elif kind == "op":
                    ins = ent[1](e)
                    if ent[2].needed:
                        ins.then_inc(ent[2].sem, 1)
                elif kind == "dma":
                    out, in_, kw = ent[1]
                    e.dma_start(out=out, in_=in_, **kw).then_inc(ent[2].sem, 16)
                elif kind == "custom":
                    ent[1](e).then_inc(ent[2].sem, ent[2].idx)

        with nc.Block() as block:
            for q in QUEUES:
                if self.streams[q]:
                    getattr(block, ENG[q])(lambda e, q=q: run(q, e))
        self.stack.close()


A_HEADS = 8
A_HD = 128
B_HD = 64
B_QH = 32
B_KVH = 4
ROPE_THETA = 500000.0
PAIRS = [[0, 1], [2, 3], [4, 5], [6, 7]]


def slot_cols(s):
    return (0, N_META) if s == 0 else (N_META + 128 * (s - 1), 128)


def own_global_tile(i, p):
    return 4 * (i // 2) + 2 * p + (i % 2)


def global_to_ridx(g):
    return (g // 2) % 2, 2 * (g // 4) + (g % 2)


def rope_tab(pos, rot):
    inv = (np.float32(ROPE_THETA) ** (-np.arange(0, rot, 2, dtype=np.float32) / np.float32(rot))).astype(np.float32)
    ang = pos.astype(np.float32)[:, None] * inv[None, :]
    return np.cos(ang).astype(np.float32), np.sin(ang).astype(np.float32)


def host_tables(NI, p):
    NSL = NI + 1
    ropeA = np.zeros((128, NSL, 2, 64), np.float32)
    ropeB = np.zeros((128, NSL, 2, 64), np.float32)
    for s in range(NSL):
        if s == 0:
            pos = np.arange(N_META)
        else:
            pos = N_META + 128 * own_global_tile(s - 1, p) + np.arange(128)
        n = len(pos)
        c, sn = rope_tab(pos, 32)
        ropeA[:n, s, 0] = np.tile(c, (1, 4))
        ropeA[:n, s, 1] = np.tile(sn, (1, 4))
        c, sn = rope_tab(pos, 16)
        ropeB[:n, s, 0] = np.tile(c, (1, 8))
        ropeB[:n, s, 1] = np.tile(sn, (1, 8))
    mA = np.zeros((128, 4, 2, 2, 128), np.float32)
    diag = np.ones((128, 128), np.float32)
    diag[64:, :64] = 0.0
    for j in range(4):
        for t in range(2):
            gq = 2 * p + t
            if j < gq:
                m = np.ones((128, 128), np.float32)
            elif j == gq:
                m = diag
            else:
                m = np.zeros((128, 128), np.float32)
            mA[:, j, :, t, :] = m[:, None, :]
    mA = mA.reshape(128, 4, 512)
    prev = np.ones((128, 128), np.float32)
    prev[:64, 64:] = 0.0
    mB = np.zeros((128, 4, 4, 128), np.float32)
    mB[:, 0] = (prev * (1.0 if p == 0 else 0.0))[:, None, :]
    mB[:, 1] = (prev * (1.0 if p == 1 else 0.0))[:, None, :]
    mB[:, 2] = prev[:, None, :]
    mB[:, 3] = diag[:, None, :]
    mB = mB.reshape(128, 4, 512)
    return ropeA, ropeB, mA.astype(ml_dtypes.bfloat16), mB.astype(ml_dtypes.bfloat16)


def build_program(NI=16, n_a=2, n_b=2, debug_x=False, stop_after=None, proj_cbs=None):
    NSL = NI + 1
    NG = NI // 2
    TOK = N_META + 128 * NI
    nc = bass.Bass("TRN2", target_bir_lowering=False)

    def din(name, shape, dt=F32):
        return nc.dram_tensor(name, list(shape), dt, kind="ExternalInput").ap()

    x_in = din("x", [NI * 128, D])
    meta_in = din("meta", [N_META, D])
    acbs = list(proj_cbs) if proj_cbs is not None else list(range(16))
    a_w_in = din("a_w_in", [max(n_a, 1), D, 512 * len(acbs)])
    a_w_out = din("a_w_out", [max(n_a, 1), D, D])
    w_kv = din("w_kv", [D, 512])
    b_w_in = din("b_w_in", [max(n_b, 1), D, 2 * D if n_b else 512])
    b_w_out = din("b_w_out", [max(n_b, 1), D, D if n_b else 512])
    gn_in = din("gn", [128, 5 * KC])
    fnorm_in = din("fnorm", [D])
    lamv_in = din("lamv", [2 * 4 * 128])
    subln_in = din("subln", [2 * 256])
    sinks_in = din("sinks", [2 * 32])
    ropeA_in = din("ropeA", [128, NSL * 2 * 64])
    ropeB_in = din("ropeB", [128, NSL * 2 * 64])
    maskA_in = din("maskA", [128, 4 * 512], BF16)
    maskB_in = din("maskB", [128, 4 * 512], BF16)
    out = nc.dram_tensor("out", [NI * 128, D], F32, kind="ExternalOutput").ap()
    if debug_x:
        dbg = nc.dram_tensor("dbg", [NSL * 128, D], F32, kind="ExternalOutput").ap()

    xres = nc.dram_tensor("xres", [NSL * 128, D], F32).ap()
    qT_loc = nc.dram_tensor("qT_loc", [8 * 128, 2 * TOK], BF16).ap()
    kT_loc = nc.dram_tensor("kT_loc", [8 * NSL * 128, 256], BF16)
    v_loc = nc.dram_tensor("v_loc", [8 * NSL * 128, 256], BF16)
    kT_all = nc.dram_tensor("kT_all", [2 * 8 * NSL * 128, 256], BF16)
    v_all = nc.dram_tensor("v_all", [2 * 8 * NSL * 128, 256], BF16)
    gate_s = nc.dram_tensor("gate_s", [NSL * 128, D], BF16).ap()
    kb_loc = nc.dram_tensor("kb_loc", [4 * NSL * 128, 128], BF16)
    vb_loc = nc.dram_tensor("vb_loc", [NSL * 128, 4 * 64], BF16)
    kb_all = nc.dram_tensor("kb_all", [2 * 4 * NSL * 128, 128], BF16)
    vb_all = nc.dram_tensor("vb_all", [2 * NSL * 128, 4 * 64], BF16)
    qb_loc = nc.dram_tensor("qb_loc", [16 * 128, TOK], BF16).ap()

    P = Prog(nc)
    NOCC = int(os.environ.get("KNOCC", "0"))
    CCCH = int(os.environ.get("KCCCH", "1"))

    def gather(loc, allt, reads, d_outs, only=None):
        la = loc.ap().bitcast(F32)
        aa = allt.ap().bitcast(F32)
        nch = len(d_outs)
        rows = la.shape[0] // nch
        for ch in (range(nch) if only is None else [only]):
            src = la[ch * rows:(ch + 1) * rows, :]
            dst = aa[2 * ch * rows:2 * (ch + 1) * rows, :]
            rd = reads[ch]
            if NOCC:
                P.dma("sp", dst[0:rows, :], src, reads=rd, writes=[d_outs[ch]])
                P.dma("sp", dst[rows:2 * rows, :], src, reads=rd, writes=[d_outs[ch]])
            else:
                P.custom("pool", lambda e, src=src, dst=dst: e.collective_compute(
                    "AllGather", ALU.bypass, replica_groups=PAIRS, ins=[src.opt()], outs=[dst.opt()]), 1,
                         reads=rd, writes=[d_outs[ch]])
    ident = P.sbuf([128, 128], BF16, "ident")
    gn = P.sbuf([128, 5 * KC], F32, "gn")
    ropeA = P.sbuf([128, NSL, 2, 64], F32, "ropeA")
    ropeB = P.sbuf([128, NSL, 2, 64], F32, "ropeB")
    maskA = P.sbuf([128, 4, 512], BF16, "maskA")
    maskB = P.sbuf([128, 4, 512], BF16, "maskB")
    lamv = P.sbuf([128, 2, 4, 128], F32, "lamv")
    subln = P.sbuf([128, 2, 256], F32, "subln")
    sinks = P.sbuf([128, 2, 32], F32, "sinks")
    nlam = P.sbuf([128, 2], F32, "nlam")
    epsT = P.sbuf([128, 1], F32, "epsT")
    hT = P.sbuf([128, KC, TOK], BF16, "hT")
    d_const = Dep()
    d_x = deps(NSL)
    d_hT = deps(NSL)

    with P.scope():
        idf = P.sbuf([128, 128], F32)
        lt = P.sbuf([128, 2, 2, 128], F32)
        ls = P.sbuf([128, 2, 2], F32)
        d_i = Dep()
        P.op("pool", lambda e: e.memset(idf[:], 1.0), writes=[d_i])
        P.op("pool", lambda e: e.affine_select(out=idf[:], in_=idf[:], pattern=[[-1, 128]], compare_op=ALU.is_equal,
                                               fill=0.0, base=0, channel_multiplier=1), reads=[d_i], writes=[d_i])
        P.op("dve", lambda e: e.tensor_copy(out=ident[:], in_=idf[:]), reads=[d_i], writes=[d_const])
        P.dma("sp", gn[:], gn_in, writes=[d_const])
        P.dma("sp", ropeA[:].rearrange("p a b c -> p (a b c)"), ropeA_in, writes=[d_const])
        P.dma("sp", ropeB[:].rearrange("p a b c -> p (a b c)"), ropeB_in, writes=[d_const])
        P.dma("sp", maskA[:].rearrange("p a b -> p (a b)"), maskA_in, writes=[d_const])
        P.dma("sp", maskB[:].rearrange("p a b -> p (a b)"), maskB_in, writes=[d_const])
        P.dma("sp", lamv[:].rearrange("p a b c -> p (a b c)"), lamv_in.partition_broadcast(128), writes=[d_const])
        P.dma("sp", subln[:].rearrange("p a b -> p (a b)"), subln_in.partition_broadcast(128), writes=[d_const])
        P.dma("sp", sinks[:].rearrange("p a b -> p (a b)"), sinks_in.partition_broadcast(128), writes=[d_const])
        zt = P.sbuf([128, 256], BF16)
        d_z = Dep()
        P.op("pool", lambda e: e.memset(zt[:], 0.0), writes=[d_z])
        kz = kT_loc.ap().rearrange("(h s d) c -> h s d c", h=8, s=NSL)
        vz = v_loc.ap().rearrange("(h s t) c -> h s t c", h=8, s=NSL)
        for hp in range(8):
            P.dma("sp", kz[hp, 0], zt[:], reads=[d_z])
            P.dma("sp", vz[hp, 0], zt[:], reads=[d_z])
        kbz = kb_loc.ap().rearrange("(c s d) t -> c s d t", c=4, s=NSL)
        for c in range(4):
            P.dma("sp", kbz[c, 0], zt[:, 0:128], reads=[d_z])
        P.dma("sp", vb_loc.ap()[0:128, :], zt[:], reads=[d_z])
        P.dma("sp", xres[0:N_META, :], meta_in, writes=[d_x[0]])
        for s in range(1, NSL):
            P.dma("sp", xres[s * 128:(s + 1) * 128, :], x_in[(s - 1) * 128:s * 128, :], writes=[d_x[s]])
        for l in range(2):
            for j in range(2):
                P.op("dve", lambda e, l=l, j=j: e.tensor_tensor(out=lt[:, l, j, :], in0=lamv[:, l, 2 * j, :],
                                                                 in1=lamv[:, l, 2 * j + 1, :], op=ALU.mult),
                     reads=[d_const], writes=[d_i])
                P.op("dve", lambda e, l=l, j=j: e.reduce_sum(out=ls[:, l, j:j + 1], in_=lt[:, l, j, :], axis=AX.X),
                     reads=[d_i], writes=[d_i])
        P.op("act", lambda e: e.activation(out=ls[:].rearrange("p a b -> p (a b)"), in_=ls[:].rearrange("p a b -> p (a b)"),
                                           func=AF.Exp), reads=[d_i], writes=[d_i])
        for l in range(2):
            lam_init = 0.8 - 0.6 * float(np.exp(-0.3 * l))
            P.op("dve", lambda e, l=l: e.tensor_tensor(out=nlam[:, l:l + 1], in0=ls[:, l, 1:2], in1=ls[:, l, 0:1],
                                                       op=ALU.subtract), reads=[d_i], writes=[d_const])
            P.op("dve", lambda e, l=l, li=lam_init: e.tensor_scalar(out=nlam[:, l:l + 1], in0=nlam[:, l:l + 1],
                                                                    scalar1=-li, scalar2=None, op0=ALU.add),
                 reads=[d_const], writes=[d_const])
        for l in range(2):
            lam_init = 0.8 - 0.6 * float(np.exp(-0.3 * l))
            P.op("dve", lambda e, l=l, li=lam_init: e.tensor_scalar(out=subln[:, l, :], in0=subln[:, l, :], scalar1=1.0 - li,
                                                                    scalar2=None, op0=ALU.mult),
                 reads=[d_const], writes=[d_const])
        P.op("dve", lambda e: e.memset(epsT[:], EPS), writes=[d_const])
        P.op("act", lambda e: e.activation(out=sinks[:].rearrange("p a b -> p (a b)"),
                                           in_=sinks[:].rearrange("p a b -> p (a b)"), func=AF.Exp),
             reads=[d_const], writes=[d_const])

    def phase_norm():
        with P.scope():
            xt = [P.sbuf([128, D], F32) for _ in range(2)]
            xn = [P.sbuf([128, D], BF16) for _ in range(2)]
            junk = P.sbuf([128, D], BF16)
            ss = [P.sbuf([128, 1], F32) for _ in range(2)]
            pt = [P.psum([128, 4, 128], BF16) for _ in range(3)]
            d_xt, d_xn, d_ss = deps(2), deps(2), deps(2)
            d_junk = Dep()
            d_pt = deps(3)
            n = 0
            for s in range(NSL):
                st, rows = slot_cols(s)
                b = s % 2
                P.dma("sp", xt[b][:rows], xres[s * 128:s * 128 + rows, :], reads=[d_x[s]], writes=[d_xt[b]])
                P.op("dve", lambda e, b=b: e.memset(ss[b][:], 0.0), writes=[d_ss[b]])
                P.op("act", lambda e, b=b, rows=rows: e.activation(out=junk[:rows], in_=xt[b][:rows], func=AF.Square,
                                                                  accum_out=ss[b][:rows]),
                     reads=[d_xt[b]], writes=[d_junk, d_ss[b]])
                P.op("act", lambda e, b=b, rows=rows: e.activation(out=ss[b][:rows], in_=ss[b][:rows], func=AF.Ln,
                                                                  bias=epsT[:rows], scale=1.0 / D),
                     reads=[d_ss[b], d_const], writes=[d_ss[b]])
                P.op("act", lambda e, b=b, rows=rows: e.activation(out=ss[b][:rows], in_=ss[b][:rows], func=AF.Exp,
                                                                  scale=-0.5),
                     reads=[d_ss[b]], writes=[d_ss[b]])
                P.op("dve", lambda e, b=b, rows=rows: e.tensor_scalar(out=xn[b][:rows], in0=xt[b][:rows],
                                                                     scalar1=ss[b][:rows, 0:1], scalar2=None, op0=ALU.mult),
                     reads=[d_xt[b], d_ss[b]], writes=[d_xn[b]])
                for k4 in range(4):
                    j = n % 3
                    n += 1
                    for k in range(4):
                        kc = k4 * 4 + k
                        P.op("pe", lambda e, j=j, k=k, kc=kc, b=b, rows=rows: e.transpose(
                            out=pt[j][:, k, :rows], in_=xn[b][:rows, kc * 128:(kc + 1) * 128], identity=ident[:rows, :rows]),
                             reads=[d_xn[b], d_const], writes=[d_pt[j]])
                    q = "act" if k4 % 2 else "dve"
                    if q == "act":
                        P.op(q, lambda e, j=j, k4=k4, st=st, rows=rows: e.activation(
                            out=hT[:, k4 * 4:k4 * 4 + 4, st:st + rows], in_=pt[j][:, :, :rows], func=AF.Copy),
                             reads=[d_pt[j]], writes=[d_hT[s]])
                    else:
                        P.op(q, lambda e, j=j, k4=k4, st=st, rows=rows: e.tensor_copy(
                            out=hT[:, k4 * 4:k4 * 4 + 4, st:st + rows], in_=pt[j][:, :, :rows]),
                             reads=[d_pt[j]], writes=[d_hT[s]])

    def project(w_ap, ncols, gcol, consume, flush, cbmap=None):
        ncb = ncols // 512
        with P.scope():
            wf = [P.sbuf([128, 8, 512], F32) for _ in range(2)]
            wb = [P.sbuf([128, KC, 512], BF16) for _ in range(2)]
            acc = [P.psum([128, 512], F32) for _ in range(4)]
            d_wf = deps(2)
            d_wb = [deps(KC) for _ in range(2)]
            d_acc = deps(4)
            cast_jobs = []

            def issue_load(cb):
                for half in range(2):
                    f = (2 * cb + half) % 2
                    P.dma("sp", wf[f][:], w_ap[half * 1024:(half + 1) * 1024, cb * 512:(cb + 1) * 512]
                          .rearrange("(kc p) n -> p kc n", p=128), writes=[d_wf[f]])
                    for k in range(8):
                        cast_jobs.append((cb, half, k))

            def do_casts(nmax):
                for _ in range(min(nmax, len(cast_jobs))):
                    cb, half, k = cast_jobs.pop(0)
                    f = (2 * cb + half) % 2
                    kc = half * 8 + k
                    q = "act" if k % 2 else "dve"
                    wbuf = wb[cb % 2]
                    if q == "act":
                        if gcol is not None:
                            P.op(q, lambda e, wbuf=wbuf, f=f, k=k, kc=kc: e.activation(
                                out=wbuf[:, kc, :], in_=wf[f][:, k, :], func=AF.Identity,
                                scale=gn[:, gcol * KC + kc:gcol * KC + kc + 1]),
                                 reads=[d_wf[f], d_const], writes=[d_wb[cb % 2][kc]])
                        else:
                            P.op(q, lambda e, wbuf=wbuf, f=f, k=k, kc=kc: e.activation(
                                out=wbuf[:, kc, :], in_=wf[f][:, k, :], func=AF.Copy),
                                 reads=[d_wf[f]], writes=[d_wb[cb % 2][kc]])
                    elif gcol is not None:
                        P.op(q, lambda e, wbuf=wbuf, f=f, k=k, kc=kc: e.tensor_scalar(
                            out=wbuf[:, kc, :], in0=wf[f][:, k, :], scalar1=gn[:, gcol * KC + kc:gcol * KC + kc + 1],
                            scalar2=None, op0=ALU.mult), reads=[d_wf[f], d_const], writes=[d_wb[cb % 2][kc]])
                    else:
                        P.op(q, lambda e, wbuf=wbuf, f=f, k=k, kc=kc: e.tensor_copy(out=wbuf[:, kc, :], in_=wf[f][:, k, :]),
                             reads=[d_wf[f]], writes=[d_wb[cb % 2][kc]])

            issue_load(0)
            do_casts(16)
            n = 0
            for cb in range(ncb):
                if cb + 1 < ncb:
                    issue_load(cb + 1)
                for s in range(NSL):
                    st, rows = slot_cols(s)
                    a = n % 4
                    n += 1
                    for kc in range(KC):
                        P.op("pe", lambda e, a=a, kc=kc, st=st, rows=rows, cb=cb: e.matmul(
                            acc[a][:rows, :], lhsT=hT[:, kc, st:st + rows], rhs=wb[cb % 2][:, kc, :],
                            start=(kc == 0), stop=(kc == KC - 1)),
                             reads=[d_hT[s], d_wb[cb % 2][kc]], writes=[d_acc[a]])
                    consume(cbmap[cb] if cbmap else cb, s, acc[a], rows, d_acc[a])
                    if s >= 2:
                        do_casts(2)
                do_casts(16)
            flush()

    def out_proj(w_ap, d_og):
        for s in range(NSL):
            d_hT[s] = d_og[s]
        with P.scope():
            xr = [P.sbuf([128, 512], F32) for _ in range(3)]
            xo = [P.sbuf([128, 512], F32) for _ in range(3)]
            d_xr, d_xo = deps(3), deps(3)
            d_xs = [[Dep() for _ in range(4)] for _ in range(NSL)]
            cnt = {"n": 0}

            def consume(cb, s, a, rows, d_a):
                j = cnt["n"] % 3
                cnt["n"] += 1
                P.dma("sp", xr[j][:rows], xres[s * 128:s * 128 + rows, cb * 512:(cb + 1) * 512], reads=[d_x[s]],
                      writes=[d_xr[j]])
                P.op("dve", lambda e: e.tensor_tensor(out=xo[j][:rows], in0=a[:rows, :], in1=xr[j][:rows], op=ALU.add),
                     reads=[d_a, d_xr[j]], writes=[d_xo[j]])
                P.dma("sp", xres[s * 128:s * 128 + rows, cb * 512:(cb + 1) * 512], xo[j][:rows], reads=[d_xo[j]],
                      writes=[d_xs[s][cb]])

            project(w_ap, D, None, consume, lambda: None)

    def a_layer(l):
        phase_norm()
        if stop_after == "norm":
            return
        d_q = [[Dep() for _ in range(NSL)] for _ in range(8)]
        d_k = [[Dep() for _ in range(NSL)] for _ in range(8)]
        d_v = [[Dep() for _ in range(NSL)] for _ in range(8)]
        d_g = [[Dep() for _ in range(4)] for _ in range(NSL)]
        qT4 = qT_loc.rearrange("(h d) (e t) -> h d e t", d=128, e=2)
        kT4 = kT_loc.ap().rearrange("(h s d) (e t) -> h s d e t", h=8, s=NSL, e=2)
        v4 = v_loc.ap().rearrange("(h s t) c -> h s t c", h=8, s=NSL)

        with P.scope():
            tmp = [P.sbuf([128, 4, 128], BF16) for _ in range(3)]
            T = [P.sbuf([128, 4, 4, 16], F32) for _ in range(2)]
            stg = [P.sbuf([128, 4, 128], BF16) for _ in range(3)]
            vt = [P.sbuf([128, 512], BF16) for _ in range(3)]
            tp = [P.psum([128, 4, 128], BF16) for _ in range(2)]
            d_tmp = [deps(3) for _ in range(3)]
            d_T = deps(2)
            d_stg = deps(3)
            d_vt = deps(3)
            d_tp = deps(2)
            cnt = {"r": 0, "t": 0, "v": 0}
            pending = []

            def emit_transposes(item):
                cb, s, j, rows = item
                st, _ = slot_cols(s)
                k = cnt["t"] % 2
                g = cnt["t"] % 3
                cnt["t"] += 1
                for h in range(4):
                    P.op("pe", lambda e, k=k, h=h, j=j, rows=rows: e.transpose(
                        out=tp[k][:, h, :rows], in_=tmp[j][:rows, h, :], identity=ident[:rows, :rows]),
                         reads=d_tmp[j] + [d_const], writes=[d_tp[k]])
                P.op("act", lambda e, k=k, g=g, rows=rows: e.activation(out=stg[g][:, :, :rows], in_=tp[k][:, :, :rows],
                                                                       func=AF.Copy),
                     reads=[d_tp[k]], writes=[d_stg[g]])
                isq = cb < 4
                c4 = cb % 4
                if DBG == 3:
                    return
                for hh in range(2):
                    hp = 2 * c4 + hh
                    if isq:
                        P.dma("sp", qT4[hp, :, :, st:st + rows], stg[g][:, 2 * hh:2 * hh + 2, :rows],
                              reads=[d_stg[g]], writes=[d_q[hp][s]])
                    else:
                        P.dma("sp", kT4[hp, s, :, :, 0:rows], stg[g][:, 2 * hh:2 * hh + 2, :rows],
                              reads=[d_stg[g]], writes=[d_k[hp][s]])

            def consume(cb, s, a, rows, d_a):
                kind = cb // 4
                if DBG == 1:
                    return
                if kind < 2:
                    j = cnt["r"] % 3
                    tt = cnt["r"] % 2
                    cnt["r"] += 1
                    a3 = a[:rows, :].rearrange("p (h d) -> p h d", h=4)
                    cs = ropeA[:rows, s, 0, :].rearrange("p (h i) -> p h i", h=4)
                    sn = ropeA[:rows, s, 1, :].rearrange("p (h i) -> p h i", h=4)
                    Tt = T[tt]
                    t3 = tmp[j]
                    P.op("dve", lambda e: e.tensor_tensor(out=Tt[:rows, 0], in0=a3[:, :, 0:16], in1=cs, op=ALU.mult),
                         reads=[d_a, d_const], writes=[d_T[tt]])
                    P.op("dve", lambda e: e.tensor_tensor(out=Tt[:rows, 1], in0=a3[:, :, 16:32], in1=sn, op=ALU.mult),
                         reads=[d_a], writes=[d_T[tt]])
                    P.op("dve", lambda e: e.tensor_tensor(out=Tt[:rows, 2], in0=a3[:, :, 16:32], in1=cs, op=ALU.mult),
                         reads=[d_a], writes=[d_T[tt]])
                    P.op("dve", lambda e: e.tensor_tensor(out=Tt[:rows, 3], in0=a3[:, :, 0:16], in1=sn, op=ALU.mult),
                         reads=[d_a], writes=[d_T[tt]])
                    if DBG == 4:
                        return
                    P.op("dve", lambda e: e.tensor_tensor(out=t3[:rows, :, 0:16], in0=Tt[:rows, 0], in1=Tt[:rows, 1],
                                                          op=ALU.subtract), reads=[d_T[tt]], writes=[d_tmp[j][0]])
                    P.op("dve", lambda e: e.tensor_tensor(out=t3[:rows, :, 16:32], in0=Tt[:rows, 2], in1=Tt[:rows, 3],
                                                          op=ALU.add), reads=[d_T[tt]], writes=[d_tmp[j][1]])
                    if DBG == 5:
                        return
                    P.op("dve", lambda e: e.tensor_copy(out=t3[:rows, :, 32:128], in_=a3[:, :, 32:128]),
                         reads=[d_a], writes=[d_tmp[j][2]])
                    if DBG == 2:
                        return
                    pending.append((cb, s, j, rows))
                    if len(pending) > 1:
                        emit_transposes(pending.pop(0))
                elif kind == 2:
                    j = cnt["v"] % 3
                    cnt["v"] += 1
                    c4 = cb % 4
                    P.op("act", lambda e: e.activation(out=vt[j][:rows], in_=a[:rows, :], func=AF.Copy),
                         reads=[d_a], writes=[d_vt[j]])
                    for hh in range(2):
                        hp = 2 * c4 + hh
                        P.dma("sp", v4[hp, s, 0:rows, :], vt[j][:rows, hh * 256:(hh + 1) * 256],
                              reads=[d_vt[j]], writes=[d_v[hp][s]])
                else:
                    j = cnt["v"] % 3
                    cnt["v"] += 1
                    c4 = cb % 4
                    P.op("act", lambda e: e.activation(out=vt[j][:rows], in_=a[:rows, :], func=AF.Silu),
                         reads=[d_a], writes=[d_vt[j]])
                    P.dma("sp", gate_s[s * 128:s * 128 + rows, c4 * 512:(c4 + 1) * 512], vt[j][:rows],
                          reads=[d_vt[j]], writes=[d_g[s][c4]])

            def flush():
                while pending:
                    emit_transposes(pending.pop(0))

            project(a_w_in[l], 512 * len(acbs), l, consume, flush, cbmap=acbs)

        if stop_after == "proj":
            return
        d_kall, d_vall = deps(8), deps(8)
        for hp in range(8):
            gather(kT_loc, kT_all, [d_k[h] for h in range(8)], d_kall, only=hp)
            gather(v_loc, v_all, [d_v[h] for h in range(8)], d_vall, only=hp)
        if stop_after == "gather":
            return

        kA = kT_all.ap().rearrange("(h r s d) c -> r h d s c", r=2, h=8, s=NSL)
        vA = v_all.ap().rearrange("(h r s t) c -> r h t s c", r=2, h=8, s=NSL)
        qT3 = qT_loc.rearrange("(h d) c -> h d c", d=128)
        SCALE = float(A_HD) ** -0.5
        lam_init = 0.8 - 0.6 * float(np.exp(-0.3 * l))
        d_og = deps(NSL)
        with P.scope():
            kbuf = [P.sbuf([128, 2 * NSL, 256], BF16) for _ in range(2)]
            vbuf = [P.sbuf([128, 2 * NSL, 264], BF16) for _ in range(2)]
            qbuf = [P.sbuf([128, 2 * TOK], BF16) for _ in range(2)]
            PT = [P.sbuf([128, 512], BF16) for _ in range(3)]
            gt = [P.sbuf([128, 256], BF16) for _ in range(2)]
            oa = [P.sbuf([128, 256], F32) for _ in range(2)]
            ob = [P.sbuf([128, 256], F32) for _ in range(2)]
            og = [P.sbuf([128, 256], BF16) for _ in range(2)]
            j32 = P.sbuf([128, 256], F32)
            sm = [P.sbuf([128, 8], F32) for _ in range(2)]
            S = [P.psum([128, 512], F32) for _ in range(2)]
            O = [[P.psum([128, 512], F32) for _ in range(2)] for _ in range(2)]
            tps = P.psum([128, 2, 128], BF16)
            d_kb, d_vb, d_qb = deps(2), deps(2), deps(2)
            d_PT, d_S = deps(3), deps(2)
            d_O = [deps(2) for _ in range(2)]
            d_gt, d_oa, d_ob, d_ogs, d_sm = deps(2), deps(2), deps(2), deps(2), deps(2)
            d_junk, d_tps = Dep(), Dep()
            for b in range(2):
                P.op("dve", lambda e, b=b: e.memset(vbuf[b][:, :, 256:257], 1.0), writes=[d_vb[b]])

            def load_hp(hp):
                b = hp % 2
                for r in range(2):
                    P.dma("sp", kbuf[b][:, r * NSL:(r + 1) * NSL, :], kA[r, hp], reads=[d_kall[hp]], writes=[d_kb[b]])
                    P.dma("sp", vbuf[b][:, r * NSL:(r + 1) * NSL, 0:256], vA[r, hp], reads=[d_vall[hp]], writes=[d_vb[b]])
                P.dma("sp", qbuf[b][:], qT3[hp], reads=d_q[hp], writes=[d_qb[b]])

            load_hp(0)
            ns = 0
            ngr = 0
            deferred = []

            def flush_deferred():
                while deferred:
                    deferred.pop(0)()
            for hp in range(8):
                b = hp % 2
                if hp + 1 < 8:
                    load_hp(hp + 1)
                qb3 = qbuf[b][:].rearrange("p (e t) -> p e t", e=2)
                groups = [("meta", None)] + [("real", m) for m in range(NG)]
                for kind, m in groups:
                    if kind == "meta":
                        qc, nq = 0, N_META
                        qtiles = [(0, N_META, 0)]
                        kslots = [(0, N_META, None)]
                    else:
                        qc, nq = N_META + 256 * m, 256
                        qtiles = [(0, 128, 1 + 2 * m), (128, 128, 2 + 2 * m)]
                        kslots = [(0, N_META, None)]
                        for g in range(4 * m + 4):
                            r, i = global_to_ridx(g)
                            kslots.append((r * NSL + 1 + i, 128, (g - 4 * m) if g >= 4 * m else None))
                    nks = len(kslots)
                    slot_bufs = []

                    def emit_qk(si):
                        nonlocal ns
                        kidx, nk, mi = kslots[si]
                        sb = ns % 2
                        pb = ns % 3
                        ns += 1
                        slot_bufs.append(pb)
                        for e_ in range(2):
                            P.op("pe", lambda e, sb=sb, e_=e_, kidx=kidx, nk=nk, qc=qc, nq=nq, b=b, qb3=qb3: e.matmul(
                                S[sb][:nk, e_ * 256:e_ * 256 + nq], lhsT=kbuf[b][:, kidx, e_ * 128:e_ * 128 + nk],
                                rhs=qb3[:, e_, qc:qc + nq], start=True, stop=True),
                                 reads=[d_kb[b], d_qb[b]], writes=[d_S[sb]])
                        if nq == 256:
                            P.op("act", lambda e, sb=sb, pb=pb, nk=nk: e.activation(out=PT[pb][:nk, :], in_=S[sb][:nk, :],
                                                                                 func=AF.Exp, scale=SCALE),
                                 reads=[d_S[sb]], writes=[d_PT[pb]])
                        else:
                            P.op("act", lambda e, sb=sb, pb=pb, nk=nk, nq=nq: e.activation(
                                out=PT[pb][:nk, :].rearrange("p (e q) -> p e q", e=2)[:, :, 0:nq],
                                in_=S[sb][:nk, :].rearrange("p (e q) -> p e q", e=2)[:, :, 0:nq], func=AF.Exp, scale=SCALE),
                                 reads=[d_S[sb]], writes=[d_PT[pb]])
                        if mi is not None:
                            P.op("dve", lambda e, pb=pb, mi=mi: e.tensor_tensor(out=PT[pb][:, :], in0=PT[pb][:, :],
                                                                                in1=maskA[:, mi, :], op=ALU.mult),
                                 reads=[d_PT[pb], d_const], writes=[d_PT[pb]])

                    def emit_pv(si):
                        kidx, nk, mi = kslots[si]
                        pb = slot_bufs[si]
                        for e_ in range(2):
                            for ti, (qo, qr, _) in enumerate(qtiles):
                                P.op("pe", lambda e, e_=e_, ti=ti, qo=qo, qr=qr, pb=pb, nk=nk, kidx=kidx, b=b, si=si, nks=nks:
                                     e.matmul(O[e_][ti][:qr, 0:257], lhsT=PT[pb][:nk, e_ * 256 + qo:e_ * 256 + qo + qr],
                                              rhs=vbuf[b][:nk, kidx, 0:257], start=(si == 0), stop=(si == nks - 1)),
                                     reads=[d_PT[pb], d_vb[b]], writes=[d_O[e_][ti]])

                    for si in range(nks):
                        emit_qk(si)
                        if si >= 1:
                            emit_pv(si - 1)
                        if si == 1:
                            flush_deferred()
                    emit_pv(nks - 1)
                    if nks == 1:
                        flush_deferred()
                    for ti, (qo, qr, slot) in enumerate(qtiles):
                        st, rows = slot_cols(slot)
                        w = ngr % 2
                        ngr += 1
                        smw = sm[w]
                        P.dma("sp", gt[w][:rows], gate_s[slot * 128:slot * 128 + rows, hp * 256:(hp + 1) * 256],
                              reads=d_g[slot], writes=[d_gt[w]])
                        P.op("dve", lambda e, smw=smw, ti=ti, rows=rows: e.reciprocal(out=smw[:rows, 0:1],
                                                                                    in_=O[0][ti][:rows, 256:257]),
                             reads=[d_O[0][ti]], writes=[d_sm[w]])
                        P.op("dve", lambda e, smw=smw, ti=ti, rows=rows: e.reciprocal(out=smw[:rows, 1:2],
                                                                                    in_=O[1][ti][:rows, 256:257]),
                             reads=[d_O[1][ti]], writes=[d_sm[w]])
                        P.op("dve", lambda e, smw=smw, rows=rows: e.tensor_tensor(out=smw[:rows, 2:3], in0=smw[:rows, 1:2],
                                                                                 in1=nlam[:rows, l:l + 1], op=ALU.mult),
                             reads=[d_sm[w], d_const], writes=[d_sm[w]])
                        P.op("dve", lambda e, smw=smw, ti=ti, rows=rows, w=w: e.tensor_scalar(
                            out=oa[w][:rows], in0=O[0][ti][:rows, 0:256], scalar1=smw[:rows, 0:1], scalar2=None, op0=ALU.mult),
                             reads=[d_O[0][ti], d_sm[w]], writes=[d_oa[w]])
                        P.op("dve", lambda e, smw=smw, ti=ti, rows=rows, w=w: e.scalar_tensor_tensor(
                            out=ob[w][:rows], in0=O[1][ti][:rows, 0:256], scalar=smw[:rows, 2:3], in1=oa[w][:rows],
                            op0=ALU.mult, op1=ALU.add),
                             reads=[d_O[1][ti], d_sm[w], d_oa[w]], writes=[d_ob[w]])
                        P.op("dve", lambda e, rows=rows, w=w: e.tensor_tensor(out=j32[:rows], in0=ob[w][:rows], in1=ob[w][:rows],
                                                                            op=ALU.mult),
                             reads=[d_ob[w]], writes=[d_junk])
                        P.op("dve", lambda e, smw=smw, rows=rows: e.reduce_sum(out=smw[:rows, 3:4], in_=j32[:rows], axis=AX.X),
                             reads=[d_junk, d_sm[w]], writes=[d_sm[w]])
                        P.op("act", lambda e, smw=smw, rows=rows: e.activation(
                            out=smw[:rows, 4:5], in_=smw[:rows, 3:4], func=AF.Ln, bias=epsT[:rows], scale=1.0 / 256.0),
                             reads=[d_sm[w], d_const], writes=[d_sm[w]])
                        P.op("act", lambda e, smw=smw, rows=rows: e.activation(
                            out=smw[:rows, 5:6], in_=smw[:rows, 4:5], func=AF.Exp, scale=-0.5),
                             reads=[d_sm[w]], writes=[d_sm[w]])
                        P.op("dve", lambda e, smw=smw, rows=rows, w=w: e.scalar_tensor_tensor(
                            out=oa[w][:rows], in0=ob[w][:rows], scalar=smw[:rows, 5:6], in1=subln[:rows, l, :],
                            op0=ALU.mult, op1=ALU.mult),
                             reads=[d_ob[w], d_sm[w], d_const], writes=[d_oa[w]])
                        P.op("dve", lambda e, rows=rows, w=w: e.tensor_tensor(out=og[w][:rows], in0=oa[w][:rows],
                                                                            in1=gt[w][:rows], op=ALU.mult),
                             reads=[d_oa[w], d_gt[w]], writes=[d_ogs[w]])
                        def _tr(w=w, rows=rows, hp=hp, st=st, slot=slot):
                            for c in range(2):
                                P.op("pe", lambda e, c=c: e.transpose(out=tps[:, c, :rows],
                                                                      in_=og[w][:rows, c * 128:(c + 1) * 128],
                                                                      identity=ident[:rows, :rows]),
                                     reads=[d_ogs[w], d_const], writes=[d_tps])
                            P.op("dve", lambda e: e.tensor_copy(
                                out=hT[:, 2 * hp:2 * hp + 2, st:st + rows], in_=tps[:, :, :rows]),
                                 reads=[d_tps], writes=[d_og[slot]])
                        deferred.append(_tr)
            flush_deferred()

        if stop_after == "attn":
            return
        out_proj(a_w_out[l], d_og)

    kb4 = kb_loc.ap().rearrange("(c s d) t -> c s d t", c=4, s=NSL)
    kbA = kb_all.ap().rearrange("(c r s d) t -> r c s d t", r=2, c=4, s=NSL)
    vbA = vb_all.ap().rearrange("(r s t) c -> r s t c", r=2, s=NSL)
    qb3 = qb_loc.rearrange("(a d) t -> a d t", d=128)
    d_kball, d_vball = deps(4), deps(1)
    SCALE_B = float(B_HD) ** -0.5

    def rope_b(a3, s, rows, nh, T_t, d_T_t, out3, d_out, d_a):
        cs = ropeB[:rows, s, 0, 0:nh * 8].rearrange("p (h i) -> p h i", h=nh)
        sn = ropeB[:rows, s, 1, 0:nh * 8].rearrange("p (h i) -> p h i", h=nh)
        P.op("dve", lambda e: e.tensor_tensor(out=T_t[:rows, 0, 0:nh], in0=a3[:, :, 0:8], in1=cs, op=ALU.mult),
             reads=[d_a, d_const], writes=[d_T_t])
        P.op("dve", lambda e: e.tensor_tensor(out=T_t[:rows, 1, 0:nh], in0=a3[:, :, 8:16], in1=sn, op=ALU.mult),
             reads=[d_a], writes=[d_T_t])
        P.op("dve", lambda e: e.tensor_tensor(out=T_t[:rows, 2, 0:nh], in0=a3[:, :, 8:16], in1=cs, op=ALU.mult),
             reads=[d_a], writes=[d_T_t])
        P.op("dve", lambda e: e.tensor_tensor(out=T_t[:rows, 3, 0:nh], in0=a3[:, :, 0:8], in1=sn, op=ALU.mult),
             reads=[d_a], writes=[d_T_t])
        P.op("dve", lambda e: e.tensor_tensor(out=out3[:, :, 0:8], in0=T_t[:rows, 0, 0:nh], in1=T_t[:rows, 1, 0:nh],
                                              op=ALU.subtract), reads=[d_T_t], writes=[d_out[0]])
        P.op("dve", lambda e: e.tensor_tensor(out=out3[:, :, 8:16], in0=T_t[:rows, 2, 0:nh], in1=T_t[:rows, 3, 0:nh],
                                              op=ALU.add), reads=[d_T_t], writes=[d_out[1]])
        P.op("dve", lambda e: e.tensor_copy(out=out3[:, :, 16:64], in_=a3[:, :, 16:64]), reads=[d_a], writes=[d_out[2]])

    def shared_kv():
        d_kb = deps(NSL)
        d_vb = deps(NSL)
        with P.scope():
            tmpk = [P.sbuf([128, 4, 2, 64], BF16) for _ in range(2)]
            T = [P.sbuf([128, 4, 8, 8], F32) for _ in range(2)]
            stg = [P.sbuf([128, 4, 128], BF16) for _ in range(2)]
            vt = [P.sbuf([128, 256], BF16) for _ in range(2)]
            tp = [P.psum([128, 4, 128], BF16) for _ in range(2)]
            d_tmp = [deps(4) for _ in range(2)]
            d_T, d_stg, d_vt, d_tp = deps(2), deps(2), deps(2), deps(2)
            cnt = {"n": 0}

            def consume(cb, s, a, rows, d_a):
                j = cnt["n"] % 2
                cnt["n"] += 1
                st, _ = slot_cols(s)
                a3 = a[:rows, 0:256].rearrange("p (h d) -> p h d", h=4)
                rope_b(a3, s, rows, 4, T[j], d_T[j], tmpk[j][:rows, :, 0, :], d_tmp[j], d_a)
                P.op("dve", lambda e: e.tensor_copy(out=tmpk[j][:rows, :, 1, :], in_=tmpk[j][:rows, :, 0, :]),
                     reads=d_tmp[j][0:3], writes=[d_tmp[j][3]])
                P.op("dve", lambda e: e.tensor_copy(out=vt[j][:rows], in_=a[:rows, 256:512]),
                     reads=[d_a], writes=[d_vt[j]])
                P.dma("sp", vb_loc.ap()[s * 128:s * 128 + rows, :], vt[j][:rows], reads=[d_vt[j]], writes=[d_vb[s]])
                for c in range(4):
                    P.op("pe", lambda e, c=c: e.transpose(out=tp[j][:, c, :rows],
                                                          in_=tmpk[j][:rows, c].rearrange("p a b -> p (a b)"),
                                                          identity=ident[:rows, :rows]),
                         reads=d_tmp[j] + [d_const], writes=[d_tp[j]])
                P.op("act", lambda e: e.activation(out=stg[j][:, :, :rows], in_=tp[j][:, :, :rows], func=AF.Copy),
                     reads=[d_tp[j]], writes=[d_stg[j]])
                P.dma("sp", kb4[:, s, :, 0:rows].rearrange("c d t -> d c t"), stg[j][:, :, :rows], reads=[d_stg[j]],
                      writes=[d_kb[s]])

            project(w_kv, 512, 2, consume, lambda: None)
        if stop_after == "bkv":
            return d_kb, d_vb
        gather(kb_loc, kb_all, [d_kb] * 4, d_kball)
        gather(vb_loc, vb_all, [d_vb], d_vball)
        return d_kb, d_vb

    def b_layer(j, d_kb, d_vb):
        if stop_after in ("bkv", "bgather"):
            return
        d_q = [[Dep() for _ in range(NSL)] for _ in range(4)]
        d_g = [[Dep() for _ in range(4)] for _ in range(NSL)]
        with P.scope():
            tmp = [P.sbuf([128, 8, 64], BF16) for _ in range(3)]
            T = [P.sbuf([128, 4, 8, 8], F32) for _ in range(2)]
            stg = [P.sbuf([128, 4, 128], BF16) for _ in range(3)]
            vt = [P.sbuf([128, 512], BF16) for _ in range(3)]
            tp = [P.psum([128, 4, 128], BF16) for _ in range(2)]
            d_tmp = [deps(3) for _ in range(3)]
            d_T, d_stg, d_vt, d_tp = deps(2), deps(3), deps(3), deps(2)
            cnt = {"r": 0, "t": 0, "v": 0}
            pending = []

            def emit_transposes(item):
                cb, s, jj, rows = item
                st, _ = slot_cols(s)
                k = cnt["t"] % 2
                g = cnt["t"] % 3
                cnt["t"] += 1
                for h in range(4):
                    P.op("pe", lambda e, h=h: e.transpose(out=tp[k][:, h, :rows],
                                                          in_=tmp[jj][:rows, 2 * h:2 * h + 2, :].rearrange("p a b -> p (a b)"),
                                                          identity=ident[:rows, :rows]),
                         reads=d_tmp[jj] + [d_const], writes=[d_tp[k]])
                P.op("act", lambda e: e.activation(out=stg[g][:, :, :rows], in_=tp[k][:, :, :rows], func=AF.Copy),
                     reads=[d_tp[k]], writes=[d_stg[g]])
                P.dma("sp", qb3[4 * cb:4 * cb + 4, :, st:st + rows].rearrange("a d t -> d a t"), stg[g][:, :, :rows],
                      reads=[d_stg[g]], writes=[d_q[cb][s]])

            def consume(cb, s, a, rows, d_a):
                if cb < 4:
                    jj = cnt["r"] % 3
                    tt = cnt["r"] % 2
                    cnt["r"] += 1
                    a3 = a[:rows, :].rearrange("p (h d) -> p h d", h=8)
                    rope_b(a3, s, rows, 8, T[tt], d_T[tt], tmp[jj][:rows], d_tmp[jj], d_a)
                    pending.append((cb, s, jj, rows))
                    if len(pending) > 1:
                        emit_transposes(pending.pop(0))
                else:
                    jj = cnt["v"] % 3
                    cnt["v"] += 1
                    c4 = cb - 4
                    P.op("act", lambda e: e.activation(out=vt[jj][:rows], in_=a[:rows, :], func=AF.Silu),
                         reads=[d_a], writes=[d_vt[jj]])
                    P.dma("sp", gate_s[s * 128:s * 128 + rows, c4 * 512:(c4 + 1) * 512], vt[jj][:rows],
                          reads=[d_vt[jj]], writes=[d_g[s][c4]])

            def flush():
                while pending:
                    emit_transposes(pending.pop(0))

            project(b_w_in[j], 2 * D, 3 + j, consume, flush)
        if stop_after == "bproj":
            return

        d_og = deps(NSL)
        with P.scope():
            ko = P.sbuf([128, 4, NSL, 128], BF16)
            vo = P.sbuf([128, NSL, 4, 66], BF16)
            kc_ = P.sbuf([128, 4, 2, NG, 128], BF16)
            vc_ = P.sbuf([128, 2, NG, 4, 66], BF16)
            qbuf = [P.sbuf([128, 4, TOK], BF16) for _ in range(2)]
            PT = [P.sbuf([128, 512], BF16) for _ in range(4)]
            gt = [P.sbuf([128, 512], BF16) for _ in range(2)]
            ob = [P.sbuf([128, 512], F32) for _ in range(2)]
            og = [P.sbuf([128, 512], BF16) for _ in range(2)]
            sm = [P.sbuf([128, 2, 8], F32) for _ in range(2)]
            S = [P.psum([128, 512], F32) for _ in range(4)]
            O = [P.psum([128, 4, 128], F32) for _ in range(2)]
            tps = P.psum([128, 4, 128], BF16)
            d_ko, d_vo, d_kc, d_vc = Dep(), Dep(), Dep(), Dep()
            d_qb = deps(2)
            d_PT, d_S, d_O = deps(4), deps(4), deps(2)
            d_gt, d_ob, d_ogs, d_sm = deps(2), deps(2), deps(2), deps(2)
            d_tps = Dep()
            P.op("pool", lambda e: e.memset(vo[:, :, :, 64:65], 1.0), writes=[d_vo])
            P.op("pool", lambda e: e.memset(vc_[:, :, :, :, 64:65], 1.0), writes=[d_vc])
            for c in range(4):
                P.dma("sp", ko[:, c], kb4[c].rearrange("s d t -> d s t"), reads=d_kb, writes=[d_ko])
            for s_ in range(NSL):
                P.dma("sp", vo[:, s_, :, 0:64], vb_loc.ap()[s_ * 128:(s_ + 1) * 128, :].rearrange("t (c d) -> t c d", c=4),
                      reads=[d_vb[s_]], writes=[d_vo])
            for m in range(NG):
                for r in range(2):
                    idx = 2 * m - 1 if r == 1 else 2 * m + 1
                    if idx < 0:
                        continue
                    for c in range(4):
                        P.dma("sp", kc_[:, c, r, m, :], kbA[r, c, 1 + idx], reads=[d_kball[c]], writes=[d_kc])
                    P.dma("sp", vc_[:, r, m, :, 0:64], vbA[r, 1 + idx].rearrange("t (c d) -> t c d", c=4),
                          reads=d_vball, writes=[d_vc])

            def load_q(c):
                P.dma("sp", qbuf[c % 2][:], qb3[4 * c:4 * c + 4].rearrange("a d t -> d a t"), reads=d_q[c],
                      writes=[d_qb[c % 2]])

            load_q(0)
            ns = 0
            ne = 0
            deferred_b = []

            def flush_deferred_b():
                while deferred_b:
                    deferred_b.pop(0)()
            for c in range(4):
                qb_ = qbuf[c % 2]
                if c + 1 < 4:
                    load_q(c + 1)
                for s in range(NSL):
                    st, rows = slot_cols(s)
                    ks = [(ko[:, c, 0, 0:N_META], vo[0:N_META, 0, c, 0:65], N_META, None)]
                    if s >= 1:
                        i = s - 1
                        if i % 2 == 1:
                            ks.append((ko[:, c, s - 1, :], vo[:, s - 1, c, 0:65], 128, 2))
                        else:
                            m = i // 2
                            if m >= 1:
                                ks.append((kc_[:, c, 1, m, :], vc_[:, 1, m, c, 0:65], 128, 0))
                            ks.append((kc_[:, c, 0, m, :], vc_[:, 0, m, c, 0:65], 128, 1))
                        ks.append((ko[:, c, s, :], vo[:, s, c, 0:65], 128, 3))
                    nks = len(ks)
                    steps = [(si, half) for si in range(nks) for half in range(2)]
                    step_buf = []

                    def emit_qk_b(n):
                        nonlocal ns
                        si, half = steps[n]
                        kap, vap, nk, mi = ks[si]
                        sb = ns % 4
                        ns += 1
                        step_buf.append(sb)
                        lo, hi = half * 64, half * 64 + 64
                        P.op("pe", lambda e, sb=sb, kap=kap, nk=nk, lo=lo, hi=hi, qb_=qb_, st=st, rows=rows: e.matmul(
                            S[sb][:nk, 0:4 * rows], lhsT=kap[lo:hi, 0:nk],
                            rhs=qb_[lo:hi, :, st:st + rows], start=True, stop=True),
                             reads=[d_ko, d_kc, d_qb[c % 2]], writes=[d_S[sb]])
                        P.op("act", lambda e, sb=sb, nk=nk, rows=rows: e.activation(
                            out=PT[sb][:nk, 0:4 * rows], in_=S[sb][:nk, 0:4 * rows], func=AF.Exp, scale=SCALE_B),
                             reads=[d_S[sb]], writes=[d_PT[sb]])
                        if mi is not None:
                            mq = "pool" if half else "dve"
                            P.op(mq, lambda e, sb=sb, mi=mi: e.tensor_tensor(out=PT[sb][:, :], in0=PT[sb][:, :],
                                                                             in1=maskB[:, mi, :], op=ALU.mult),
                                 reads=[d_PT[sb], d_const], writes=[d_PT[sb]])

                    def emit_pv_b(n):
                        si, half = steps[n]
                        kap, vap, nk, mi = ks[si]
                        sb = step_buf[n]
                        for a in range(4):
                            P.op("pe", lambda e, a=a, sb=sb, nk=nk, rows=rows, vap=vap, half=half, si=si, nks=nks: e.matmul(
                                O[half][:rows, a, 0:65], lhsT=PT[sb][:nk, a * rows:(a + 1) * rows], rhs=vap[0:nk, :],
                                start=(si == 0 and a == 0), stop=(si == nks - 1 and a == 3)),
                                 reads=[d_PT[sb], d_vo, d_vc], writes=[d_O[half]])

                    for n in range(len(steps)):
                        emit_qk_b(n)
                        if n >= 1:
                            emit_pv_b(n - 1)
                        if n == 1:
                            flush_deferred_b()
                    emit_pv_b(len(steps) - 1)
                    w = ne % 2
                    ne += 1
                    smw = sm[w]
                    P.dma("sp", gt[w][:rows], gate_s[s * 128:s * 128 + rows, c * 512:(c + 1) * 512], reads=d_g[s],
                          writes=[d_gt[w]])
                    for half in range(2):
                        P.op("dve", lambda e, half=half, smw=smw, rows=rows: e.tensor_tensor(
                            out=smw[:rows, half, 0:4], in0=O[half][:rows, :, 64],
                            in1=sinks[:rows, j, half * 16 + 4 * c:half * 16 + 4 * c + 4], op=ALU.add),
                             reads=[d_O[half], d_const], writes=[d_sm[w]])
                        P.op("dve", lambda e, half=half, smw=smw, rows=rows: e.reciprocal(out=smw[:rows, half, 4:8],
                                                                                       in_=smw[:rows, half, 0:4]),
                             reads=[d_sm[w]], writes=[d_sm[w]])
                        for a in range(4):
                            P.op("dve", lambda e, half=half, a=a, smw=smw, rows=rows, w=w: e.tensor_scalar(
                                out=ob[w][:rows, (2 * a + half) * 64:(2 * a + half) * 64 + 64], in0=O[half][:rows, a, 0:64],
                                scalar1=smw[:rows, half, 4 + a:5 + a], scalar2=None, op0=ALU.mult),
                                 reads=[d_O[half], d_sm[w]], writes=[d_ob[w]])
                    P.op("pool", lambda e, rows=rows, w=w: e.tensor_tensor(out=og[w][:rows], in0=ob[w][:rows], in1=gt[w][:rows],
                                                                         op=ALU.mult),
                         reads=[d_ob[w], d_gt[w]], writes=[d_ogs[w]])
                    def _trb(w=w, rows=rows, c=c, st=st, s=s):
                        for k in range(4):
                            P.op("pe", lambda e, k=k: e.transpose(out=tps[:, k, :rows],
                                                                  in_=og[w][:rows, k * 128:(k + 1) * 128],
                                                                  identity=ident[:rows, :rows]),
                                 reads=[d_ogs[w], d_const], writes=[d_tps])
                        P.op("act", lambda e: e.activation(
                            out=hT[:, 4 * c:4 * c + 4, st:st + rows], in_=tps[:, :, :rows], func=AF.Copy),
                             reads=[d_tps], writes=[d_og[s]])
                    deferred_b.append(_trb)
            flush_deferred_b()
        if stop_after == "battn":
            return
        out_proj(b_w_out[j], d_og)

    def final_phase():
        with P.scope():
            fn = P.sbuf([128, D], F32)
            xt = [P.sbuf([128, D], F32) for _ in range(2)]
            xo = [P.sbuf([128, D], F32) for _ in range(2)]
            junk = P.sbuf([128, D], BF16)
            ss = [P.sbuf([128, 1], F32) for _ in range(2)]
            d_fn, d_junk = Dep(), Dep()
            d_xt, d_xo, d_ss = deps(2), deps(2), deps(2)
            P.dma("sp", fn[:], fnorm_in.partition_broadcast(128), writes=[d_fn])
            recs = []
            for s in range(1, NSL):
                b = s % 2
                P.dma("sp", xt[b][:], xres[s * 128:(s + 1) * 128, :], reads=[d_x[s]], writes=[d_xt[b]])
                P.op("dve", lambda e, b=b: e.memset(ss[b][:], 0.0), writes=[d_ss[b]])
                P.op("act", lambda e, b=b: e.activation(out=junk[:], in_=xt[b][:], func=AF.Square, accum_out=ss[b][:]),
                     reads=[d_xt[b]], writes=[d_junk, d_ss[b]])
                P.op("act", lambda e, b=b: e.activation(out=ss[b][:], in_=ss[b][:], func=AF.Ln, bias=epsT[:], scale=1.0 / D),
                     reads=[d_ss[b], d_const], writes=[d_ss[b]])
                P.op("act", lambda e, b=b: e.activation(out=ss[b][:], in_=ss[b][:], func=AF.Exp, scale=-0.5),
                     reads=[d_ss[b]], writes=[d_ss[b]])
                P.op("dve", lambda e, b=b: e.scalar_tensor_tensor(out=xo[b][:], in0=xt[b][:], scalar=ss[b][:, 0:1], in1=fn[:],
                                                                  op0=ALU.mult, op1=ALU.mult),
                     reads=[d_xt[b], d_ss[b], d_fn], writes=[d_xo[b]])
                recs.append(P.dma("sp", out[(s - 1) * 128:s * 128, :], xo[b][:], reads=[d_xo[b]]))
            for r in recs:
                P.wait("sp", r)

    if stop_after != "init":
        for l in range(n_a):
            a_layer(l)
        if n_b > 0:
            phase_norm()
            d_kb, d_vb = shared_kv()
            for j in range(n_b):
                if j > 0:
                    phase_norm()
                b_layer(j, d_kb, d_vb)

    with P.scope():
        if debug_x:
            for s in range(NSL):
                st, rows = slot_cols(s)
                r = P.dma("sp", dbg[s * 128:s * 128 + rows, :], xres[s * 128:s * 128 + rows, :], reads=[d_x[s]])
                P.wait("sp", r)
        pass
    if n_a == 2 and n_b == 2 and stop_after is None:
        final_phase()
    else:
        rs_ = [P.dma("sp", out[(s - 1) * 128:s * 128, :], xres[s * 128:(s + 1) * 128, :], reads=[d_x[s]])
               for s in range(1, NSL)]
        for r in rs_:
            P.wait("sp", r)
    P.finish()
    return nc


def make_in_maps(inputs, NI, n_a=2, n_b=2, proj_cbs=None):
    NSL = NI + 1
    f32 = np.float32
    x = np.asarray(inputs["x"], f32)
    B = x.shape[0]
    gn = np.zeros((128, 5 * KC), f32)
    gains = [inputs["a_norm"][0], inputs["a_norm"][1], inputs["kv_norm"], inputs["b_norm"][0], inputs["b_norm"][1]]
    for w, g in enumerate(gains):
        gn[:, w * KC:(w + 1) * KC] = np.asarray(g, f32).reshape(KC, 128).T
    lamv = np.stack([np.stack([np.asarray(inputs[k], f32)[l] for k in
                               ("a_lambda_q1", "a_lambda_k1", "a_lambda_q2", "a_lambda_k2")]) for l in range(2)]).reshape(-1)
    shared = {
        "meta": np.ascontiguousarray(inputs["meta_tokens"], f32),
        "a_w_in": np.ascontiguousarray(np.asarray(inputs["a_w_in"], f32)[:max(n_a, 1)] if proj_cbs is None else
                                       np.concatenate([np.asarray(inputs["a_w_in"], f32)[:max(n_a, 1), :, cb * 512:(cb + 1) * 512]
                                                       for cb in proj_cbs], axis=2)),
        "a_w_out": np.ascontiguousarray(np.asarray(inputs["a_w_out"], f32)[:max(n_a, 1)]),
        "w_kv": np.ascontiguousarray(inputs["w_kv"], f32),
        "b_w_in": np.ascontiguousarray(np.asarray(inputs["b_w_in"], f32)[:max(n_b, 1), :, :(2 * D if n_b else 512)]),
        "b_w_out": np.ascontiguousarray(np.asarray(inputs["b_w_out"], f32)[:max(n_b, 1), :, :(D if n_b else 512)]),
        "gn": gn,
        "fnorm": np.ascontiguousarray(inputs["final_norm"], f32),
        "lamv": np.ascontiguousarray(lamv, f32),
        "subln": np.ascontiguousarray(inputs["a_subln"], f32).reshape(-1),
        "sinks": np.ascontiguousarray(np.asarray(inputs["b_sinks"], f32).reshape(2, 4, 4, 2).transpose(0, 3, 1, 2)).reshape(-1),
    }
    tabs = [host_tables(NI, p) for p in range(2)]
    in_maps = []
    for c in range(8):
        b, p = c // 2, c % 2
        xt = x[b].reshape(2 * NI, 128, D)
        own = [own_global_tile(i, p) for i in range(NI)]
        m = dict(shared)
        m["x"] = np.ascontiguousarray(xt[own].reshape(NI * 128, D))
        ra, rb, ma, mb = tabs[p]
        m["ropeA"] = ra.reshape(128, -1)
        m["ropeB"] = rb.reshape(128, -1)
        m["maskA"] = ma.reshape(128, -1)
        m["maskB"] = mb.reshape(128, -1)
        in_maps.append(m)
    return in_maps


def assemble(results, NI, key="out"):
    B = 4
    outp = np.zeros((B, 2 * NI, 128, D), np.float32)
    for c in range(8):
        b, p = c // 2, c % 2
        y = np.asarray(results[c][key]).reshape(NI, 128, D)
        for i in range(NI):
            outp[b, own_global_tile(i, p)] = y[i]
    return outp.reshape(B, 2 * NI * 128, D)


_CACHE = {}


def kernel(**inputs):
    NI = 16
    if "nc" not in _CACHE:
        _CACHE["nc"] = build_program(NI)
    nc = _CACHE["nc"]
    in_maps = make_in_maps(inputs, NI)
    res = run_bass_kernel_spmd(nc, in_maps, core_ids=list(range(8)))
    return assemble(res.results, NI)
```
